# Optimizing a Trainium2 kernel written in Bass

```python
import math
import jax
import jax.numpy as jnp
from jax import lax
import numpy as np


D_MODEL = 1024
BATCH = 8
SEQ = 4096
DEPTH = 1

GRID_W = 64
CTX_LEN = 256
RET_HEADS = 4
RET_DK = D_MODEL // RET_HEADS
RET_DV = 2 * RET_DK
RET_CHUNK = 128
DIFF_DH = 64
DIFF_HEADS = D_MODEL // (2 * DIFF_DH)
DIFF_BLOCK = 128
D_FF = ((8 * D_MODEL // 3 + 127) // 128) * 128
CONV_W = 3
ROPE_BASE = 10000.0
LN_EPS = 1e-5
RET_Q = RET_HEADS * RET_DK
RET_V = RET_HEADS * RET_DV
DIFF_QK = DIFF_HEADS * 2 * DIFF_DH
DIFF_V = DIFF_HEADS * 2 * DIFF_DH
IN_SIZES = (RET_Q, RET_Q, RET_V, RET_V, DIFF_QK, DIFF_QK, DIFF_V, D_MODEL, D_MODEL)
N_IN = 2 * RET_Q + 2 * RET_V + 2 * DIFF_QK + DIFF_V + 2 * D_MODEL

kernel_name = 'hybrid_retention_diffattn_dit_layer'


def layer_norm(x, g, b):
    xf = x.astype(jnp.float32)
    mu = xf.mean(-1, keepdims=True)
    var = jnp.mean(jnp.square(xf - mu), -1, keepdims=True)
    return ((xf - mu) * lax.rsqrt(var + LN_EPS) * g + b).astype(x.dtype)


def axial_rope(row, col, head_dim):
    half = head_dim // 2
    inv = ROPE_BASE ** (-(jnp.arange(0, half, 2, dtype=jnp.float32) / half))
    ang = jnp.concatenate([row[:, None] * inv, col[:, None] * inv], axis=-1)
    return jnp.cos(ang), jnp.sin(ang)


def apply_rope(x, cos, sin):
    d = x.shape[-1] // 2
    x1, x2 = x[..., :d], x[..., d:]
    return jnp.concatenate([x1 * cos - x2 * sin, x2 * cos + x1 * sin], axis=-1)


def split_in(p):
    offs = np.cumsum(IN_SIZES)[:-1].tolist()
    return jnp.split(p, offs, axis=-1)


def to_heads(t, n):
    B, T, _ = t.shape
    return t.reshape(B, T, n, -1).transpose(0, 2, 1, 3)


def merge_heads(t):
    B, H, T, d = t.shape
    return t.transpose(0, 2, 1, 3).reshape(B, T, H * d)


def diff_qk_heads(t):
    B, T, _ = t.shape
    return t.reshape(B, T, DIFF_HEADS, 2, DIFF_DH).transpose(0, 2, 3, 1, 4)


def project_heads(p, rope_ret=None, rope_dif=None):
    qr, kr, vr, gr, qd, kd, vd, gate_r, gate_d = split_in(p)
    qr, kr, vr = to_heads(qr, RET_HEADS), to_heads(kr, RET_HEADS), to_heads(vr, RET_HEADS)
    qd, kd = diff_qk_heads(qd), diff_qk_heads(kd)
    vd = to_heads(vd, DIFF_HEADS)
    if rope_ret is not None:
        qr, kr = apply_rope(qr, *rope_ret), apply_rope(kr, *rope_ret)
        qd, kd = apply_rope(qd, *rope_dif), apply_rope(kd, *rope_dif)
    return (qr, kr * RET_DK ** -0.5, vr, gr, qd * DIFF_DH ** -0.5, kd, vd, gate_r, gate_d)


def retention_chunked(q, k, v, log_g, s0):
    B, H, T, _ = q.shape
    dv = v.shape[-1]
    n = T // RET_CHUNK
    pos = jnp.arange(RET_CHUNK, dtype=jnp.float32)
    rel = pos[:, None] - pos[None, :]
    d_in = jnp.where(rel >= 0, jnp.exp(jnp.maximum(rel, 0.0) * log_g[:, None, None]), 0.0)
    d_q = jnp.exp((pos + 1.0) * log_g[:, None])[..., None]
    d_k = jnp.exp((RET_CHUNK - 1.0 - pos) * log_g[:, None])[..., None]
    d_s = jnp.exp(RET_CHUNK * log_g)[:, None, None]

    def chunks(t):
        return t.reshape(B, H, n, RET_CHUNK, t.shape[-1]).transpose(2, 0, 1, 3, 4)

    def step(s, qkv):
        qc, kc, vc = qkv
        inner = jnp.einsum('bhqd,bhkd->bhqk', qc, kc) * d_in
        y = jnp.einsum('bhqk,bhkv->bhqv', inner, vc) + jnp.einsum('bhqd,bhdv->bhqv', qc, s) * d_q
        s = s * d_s + jnp.einsum('bhkd,bhkv->bhdv', kc * d_k, vc)
        return s, y

    s_fin, ys = lax.scan(step, s0, (chunks(q), chunks(k), chunks(v)))
    return ys.transpose(1, 2, 0, 3, 4).reshape(B, H, T, dv), s_fin


def context_state(k, v, log_g, reverse):
    P = k.shape[2]
    pos = jnp.arange(P, dtype=jnp.float32)
    dist = pos if reverse else (P - 1.0) - pos
    w = jnp.exp(dist[None, :] * log_g[:, None])
    return jnp.einsum('bhtd,bhtv->bhdv', k * w[None, :, :, None], v)


def flip_t(t):
    return t[:, :, ::-1]


def diff_attention(q, k, v, lam):
    B, H, _, T, d = q.shape
    nb = T // DIFF_BLOCK
    qb = q.reshape(B, H, 2, nb, DIFF_BLOCK, d).transpose(3, 0, 1, 2, 4, 5)

    def one(qblk):
        s = jnp.einsum('bhcqd,bhckd->bhcqk', qblk, k)
        p = jax.nn.softmax(s, axis=-1)
        a = p[:, :, 0] - lam * p[:, :, 1]
        return jnp.einsum('bhqk,bhkv->bhqv', a, v)

    o = lax.map(one, qb)
    return o.transpose(1, 2, 0, 3, 4).reshape(B, H, T, v.shape[-1])


def retention_readout(y, g, w):
    mu = y.mean(-1, keepdims=True)
    var = jnp.mean(jnp.square(y - mu), -1, keepdims=True)
    y = merge_heads((y - mu) * lax.rsqrt(var + LN_EPS))
    return (jax.nn.silu(g) * y) @ w


def diff_readout(o, g_sub, lam_init, w):
    o = o * lax.rsqrt(jnp.mean(jnp.square(o), -1, keepdims=True) + LN_EPS) * g_sub * (1.0 - lam_init)
    return merge_heads(o) @ w


def merge_branches(y_ret, y_dif, gate_r, gate_d, b_gate, w_o):
    gr = jax.nn.sigmoid(gate_r + b_gate[:D_MODEL])
    gd = jax.nn.sigmoid(gate_d + b_gate[D_MODEL:])
    return (gr * y_ret + gd * y_dif) @ w_o


def dwconv_centred(a, w, b):
    pad = CONV_W // 2
    T = a.shape[1]
    ap = jnp.pad(a, ((0, 0), (pad, pad), (0, 0)))
    out = b
    for j in range(CONV_W):
        out = out + ap[:, j:j + T] * w[j]
    return out


def conv_ffn(h, w_up, conv_w, conv_b, w_down):
    u, gate = jnp.split(h @ w_up, 2, axis=-1)
    u = dwconv_centred(u, conv_w, conv_b)
    return (jax.nn.gelu(u, approximate=False) * gate) @ w_down


def setup_inputs(seed: int = 0) -> dict:
    key = jax.random.key(seed)
    ks = jax.random.split(key, 24)
    L, D = DEPTH, D_MODEL
    beta = (8.0 * DEPTH) ** -0.25

    def nrm(k, shape, scale):
        return scale * jax.random.normal(k, shape, jnp.float32)

    gamma0 = 1.0 - 2.0 ** (-5.0 - np.arange(RET_HEADS, dtype=np.float32))
    logit0 = jnp.asarray(np.log(gamma0 / (1.0 - gamma0)), jnp.float32)
    return {
        'x': nrm(ks[0], (BATCH, SEQ, D), 1.0),
        'c': nrm(ks[1], (BATCH, D), 1.0),
        'ctx': nrm(ks[2], (BATCH, CTX_LEN, D), 1.0),
        'c_ctx': nrm(ks[3], (D,), 1.0),
        'ln_in_g': 1.0 + nrm(ks[4], (D,), 0.02),
        'ln_in_b': nrm(ks[5], (D,), 0.02),
        'w_mod': nrm(ks[6], (L, D, 6 * D), 0.5 * D ** -0.5),
        'b_mod': nrm(ks[7], (L, 6 * D), 0.02),
        'w_in': nrm(ks[8], (L, D, N_IN), D ** -0.5),
        'b_gate': nrm(ks[9], (L, 2 * D), 0.02),
        'ret_decay_logit': logit0 + nrm(ks[10], (L, 2, RET_HEADS), 0.1),
        'diff_lambda': nrm(ks[11], (L, 4, DIFF_DH), 0.1),
        'diff_subln_g': 1.0 + nrm(ks[12], (L, 2 * DIFF_DH), 0.02),
        'w_ret_out': nrm(ks[13], (L, RET_V, D), RET_V ** -0.5),
        'w_diff_out': nrm(ks[14], (L, DIFF_V, D), DIFF_V ** -0.5),
        'w_o': nrm(ks[15], (L, D, D), beta * D ** -0.5),
        'ln1_g': 1.0 + nrm(ks[16], (L, D), 0.02),
        'ln1_b': nrm(ks[17], (L, D), 0.02),
        'w_up': nrm(ks[18], (L, D, 2 * D_FF), D ** -0.5),
        'conv_w': nrm(ks[19], (L, CONV_W, D_FF), CONV_W ** -0.5),
        'conv_b': nrm(ks[20], (L, D_FF), 0.02),
        'w_down': nrm(ks[21], (L, D_FF, D), beta * D_FF ** -0.5),
        'ln2_g': 1.0 + nrm(ks[22], (L, D), 0.02),
        'ln2_b': nrm(ks[23], (L, D), 0.02),
    }


def reference(x, c, ctx, c_ctx, ln_in_g, ln_in_b, w_mod, b_mod, w_in, b_gate, ret_decay_logit,
              diff_lambda, diff_subln_g, w_ret_out, w_diff_out, w_o, ln1_g, ln1_b,
              w_up, conv_w, conv_b, w_down, ln2_g, ln2_b):
    f32 = jnp.float32
    dtype = x.dtype
    S = x.shape[1]
    ROWS = S // GRID_W
    row = jnp.repeat(jnp.arange(ROWS, dtype=f32), GRID_W)
    col = jnp.tile(jnp.arange(GRID_W, dtype=f32), ROWS)
    rope_ret = axial_rope(row, col, RET_DK)
    rope_dif = axial_rope(row, col, DIFF_DH)
    alpha = (2.0 * DEPTH) ** 0.25

    x = layer_norm(x, ln_in_g, ln_in_b)
    xc = layer_norm(ctx, ln_in_g, ln_in_b)
    cond = jax.nn.silu(c)
    cond_ctx = jax.nn.silu(c_ctx)

    for i in range(DEPTH):
        ctx_out = i < DEPTH - 1
        lam_init = 0.8 - 0.6 * math.exp(-0.3 * i)
        mod = cond @ w_mod[i] + b_mod[i]
        mod_c = cond_ctx @ w_mod[i] + b_mod[i]
        sh1, sc1, g1, sh2, sc2, g2 = jnp.split(mod[:, None, :], 6, axis=-1)
        sh1c, sc1c, g1c, sh2c, sc2c, g2c = jnp.split(mod_c, 6)

        lq1, lk1, lq2, lk2 = diff_lambda[i].astype(f32)
        lam = jnp.exp(jnp.sum(lq1 * lk1)) - jnp.exp(jnp.sum(lq2 * lk2)) + lam_init
        log_g = jax.nn.log_sigmoid(ret_decay_logit[i].astype(f32))

        p = ((x * (1 + sc1) + sh1) @ w_in[i]).astype(f32)
        pc = ((xc * (1 + sc1c) + sh1c) @ w_in[i]).astype(f32)
        qr, kr, vr, gr, qd, kd, vd, gate_r, gate_d = project_heads(p, rope_ret, rope_dif)
        qr_c, kr_c, vr_c, gr_c, qd_c, kd_c, vd_c, gate_r_c, gate_d_c = project_heads(pc)

        if ctx_out:
            zeros = jnp.zeros(kr_c.shape[:2] + (RET_DK, RET_DV), f32)
            yc_f, s_f = retention_chunked(qr_c, kr_c, vr_c, log_g[0], zeros)
            yc_b, s_b = retention_chunked(flip_t(qr_c), flip_t(kr_c), flip_t(vr_c), log_g[1], zeros)
            yc_ret = yc_f + flip_t(yc_b)
        else:
            s_f = context_state(kr_c, vr_c, log_g[0], reverse=False)
            s_b = context_state(kr_c, vr_c, log_g[1], reverse=True)
        y_f, _ = retention_chunked(qr, kr, vr, log_g[0], s_f)
        y_b, _ = retention_chunked(flip_t(qr), flip_t(kr), flip_t(vr), log_g[1], s_b)
        y_ret = retention_readout(y_f + flip_t(y_b), gr, w_ret_out[i])

        k_all = jnp.concatenate([kd, kd_c], axis=3)
        v_all = jnp.concatenate([vd, vd_c], axis=2)
        y_dif = diff_readout(diff_attention(qd, k_all, v_all, lam), diff_subln_g[i], lam_init, w_diff_out[i])

        y_mix = merge_branches(y_ret, y_dif, gate_r, gate_d, b_gate[i], w_o[i]).astype(dtype)
        x = layer_norm(alpha * x + g1 * y_mix, ln1_g[i], ln1_b[i])

        if ctx_out:
            yc_r = retention_readout(yc_ret, gr_c, w_ret_out[i])
            yc_d = diff_readout(diff_attention(qd_c, kd_c, vd_c, lam), diff_subln_g[i], lam_init, w_diff_out[i])
            yc_mix = merge_branches(yc_r, yc_d, gate_r_c, gate_d_c, b_gate[i], w_o[i]).astype(dtype)
            xc = layer_norm(alpha * xc + g1c * yc_mix, ln1_g[i], ln1_b[i])

        y_ff = conv_ffn(x * (1 + sc2) + sh2, w_up[i], conv_w[i], conv_b[i], w_down[i])
        x = layer_norm(alpha * x + g2 * y_ff, ln2_g[i], ln2_b[i])
        if ctx_out:
            yc_ff = conv_ffn(xc * (1 + sc2c) + sh2c, w_up[i], conv_w[i], conv_b[i], w_down[i])
            xc = layer_norm(alpha * xc + g2c * yc_ff, ln2_g[i], ln2_b[i])

    return x
```

```python
import numpy as np
import ml_dtypes
from contextlib import ExitStack
import concourse.bass as bass
import concourse.mybir as mybir
from concourse.bass_utils import run_bass_kernel_spmd

F32 = mybir.dt.float32
BF16 = mybir.dt.bfloat16
AF = mybir.ActivationFunctionType
ALU = mybir.AluOpType

D = 1024
S = 4096
P = 256
NT = S + P
NIN = 11264
DFF = 2816
EPS = 1e-5
ALPHA = 2.0 ** 0.25
LAM_INIT = 0.2
OFF_QR, OFF_KR, OFF_VR, OFF_GR, OFF_QD, OFF_KD, OFF_VD, OFF_GTR, OFF_GTD = (
    0, 1024, 2048, 4096, 6144, 7168, 8192, 9216, 10240)

SAME_ENG_RAW = True


class Buf:
    __slots__ = ("name", "lw", "rs", "excl")

    def __init__(self, name, excl=False):
        self.name = name
        self.lw = None
        self.rs = []
        self.excl = excl


class Op:
    __slots__ = ("eng", "fn", "reads", "writes", "dma", "key", "pos", "signal", "sigval",
                 "waits", "kref", "extra")

    def __init__(self, eng, fn, reads, writes, dma):
        self.eng = eng
        self.fn = fn
        self.reads = reads
        self.writes = writes
        self.dma = dma
        self.key = None
        self.pos = 0
        self.signal = False
        self.sigval = 0
        self.waits = []
        self.kref = None
        self.extra = None


class Prog:
    ENG = ("pe", "act", "dve", "pool", "sp")

    def __init__(self, nc, es):
        self.nc = nc
        self.eo = dict(pe=nc.tensor, act=nc.scalar, dve=nc.vector, pool=nc.gpsimd, sp=nc.sync)
        self.esem = {e: es.enter_context(nc.semaphore("s_" + e)) for e in self.ENG}
        self.nslots = dict(sp=12, act=6, pool=8)
        self.dsem = {}
        self.slot_cnt = {}
        self.slot_last = {}
        for q, n in self.nslots.items():
            for i in range(n):
                k = ("d", q, i)
                self.dsem[k] = es.enter_context(nc.semaphore("d_%s%d" % (q, i)))
                self.slot_cnt[k] = 0
                self.slot_last[k] = None
        self.slot_rr = {q: 0 for q in self.nslots}
        self.pending = []
        self.known = {e: {} for e in self.ENG}
        self.epos = {e: 0 for e in self.ENG}
        self.esig = {e: 0 for e in self.ENG}
        self.last_real = {e: None for e in self.ENG}
        self.nops = 0

    def add(self, eng, fn, reads=(), writes=(), dma=False):
        op = Op(eng, fn, tuple(reads), tuple(writes), dma)
        self.pending.append(op)
        return op

    def barrier(self):
        marks = []
        for e in self.ENG:
            op = self.add(e, None)
            op.extra = "barrier"
            marks.append(op)
        return marks

    def flush(self, final=False):
        if not final:
            self.barrier()
        ops = self.pending
        self.pending = []
        nc = self.nc
        for X in ops:
            q = X.eng
            deps = {}
            if X.extra == "barrier":
                for e in self.ENG:
                    if self.last_real[e] is not None:
                        deps[self.last_real[e]] = False
                for k, d in self.slot_last.items():
                    if d is not None:
                        deps[d] = False
            else:
                for b in X.reads:
                    if b.lw is not None:
                        deps[b.lw] = True
                    if b.excl:
                        for r in b.rs:
                            deps.setdefault(r, False)
                for b in X.writes:
                    if b.lw is not None:
                        deps.setdefault(b.lw, False)
                    for r in b.rs:
                        deps.setdefault(r, False)
            if X.dma:
                i = self.slot_rr[q]
                self.slot_rr[q] = (i + 1) % self.nslots[q]
                key = ("d", q, i)
                prev = self.slot_last[key]
                if prev is not None:
                    deps.setdefault(prev, False)
                self.slot_cnt[key] += 1
                X.key = key
                X.pos = self.slot_cnt[key]
                self.slot_last[key] = X
            else:
                self.epos[q] += 1
                X.key = q
                X.pos = self.epos[q]
                if X.fn is not None:
                    self.last_real[q] = X
            known = self.known[q]
            need = {}
            for Dp, raw in deps.items():
                if Dp is X:
                    continue
                if (not Dp.dma) and Dp.key == q and X.extra != "barrier":
                    if q == "pe" or not SAME_ENG_RAW:
                        continue
                if known.get(Dp.key, 0) >= Dp.pos:
                    continue
                o = need.get(Dp.key)
                if o is None or o.pos < Dp.pos:
                    need[Dp.key] = Dp
            if need:
                newk = dict(known)
                for k, Dp in need.items():
                    Dp.signal = True
                    X.waits.append(Dp)
                    kr = Dp.kref
                    if kr:
                        for kk, vv in kr.items():
                            if newk.get(kk, 0) < vv:
                                newk[kk] = vv
                    if newk.get(k, 0) < Dp.pos:
                        newk[k] = Dp.pos
                self.known[q] = newk
            X.kref = self.known[q]
            for b in X.reads:
                if b.excl:
                    b.lw = X
                    b.rs = []
                else:
                    b.rs.append(X)
            for b in X.writes:
                b.lw = X
                b.rs = []
        per = {e: [] for e in self.ENG}
        for X in ops:
            if (not X.dma) and X.signal:
                if X.fn is None:
                    raise RuntimeError("signal on empty op")
                self.esig[X.eng] += 1
                X.sigval = self.esig[X.eng]
            per[X.eng].append(X)
        self.nops += len(ops)
        esem, dsem = self.esem, self.dsem

        def emit(eng, lst):
            for X in lst:
                for Dp in X.waits:
                    if Dp.dma:
                        eng.wait_ge(dsem[Dp.key], 16 * Dp.pos)
                    else:
                        eng.wait_ge(esem[Dp.key], Dp.sigval)
                if X.fn is not None:
                    ins = X.fn(eng)
                    if X.dma:
                        ins.then_inc(dsem[X.key], 16)
                    elif X.signal:
                        ins.then_inc(esem[X.eng], 1)

        with nc.Block() as block:
            if per["pe"]:
                @block.tensor
                def _(e):
                    emit(e, per["pe"])
            if per["act"]:
                @block.scalar
                def _(e):
                    emit(e, per["act"])
            if per["dve"]:
                @block.vector
                def _(e):
                    emit(e, per["dve"])
            if per["pool"]:
                @block.gpsimd
                def _(e):
                    emit(e, per["pool"])
            if per["sp"]:
                @block.sync
                def _(e):
                    emit(e, per["sp"])

    def mm(self, out, lhsT, rhs, start, stop, r, w, skip=False):
        if skip:
            return self.add("pe", lambda e: e.matmul(out, lhsT, rhs, start=start, stop=stop, skip_group_check=True), r, w)
        return self.add("pe", lambda e: e.matmul(out, lhsT, rhs, start=start, stop=stop), r, w)

    def tr(self, out, in_, ident, r, w):
        return self.add("pe", lambda e: e.transpose(out, in_, ident), r, w)

    def act(self, out, in_, func, r, w, bias=None, scale=None, accum_out=None, eng="act"):
        kw = {}
        if bias is not None:
            kw["bias"] = bias
        if scale is not None:
            kw["scale"] = scale
        if accum_out is not None:
            kw["accum_out"] = accum_out
        return self.add(eng, lambda e: e.activation(out, in_, func, **kw), r, w)

    def ts(self, out, in0, s1, s2, op0, op1, r, w, eng="dve", accum_out=None):
        if op1 is None:
            return self.add(eng, lambda e: e.tensor_scalar(out, in0, s1, None, op0), r, w)
        if accum_out is not None:
            return self.add(eng, lambda e: e.tensor_scalar(out, in0, s1, s2, op0, op1, accum_out), r, w)
        return self.add(eng, lambda e: e.tensor_scalar(out, in0, s1, s2, op0, op1), r, w)

    def tt(self, out, in0, in1, op, r, w, eng="dve"):
        return self.add(eng, lambda e: e.tensor_tensor(out, in0, in1, op), r, w)

    def stt(self, out, in0, scalar, in1, op0, op1, r, w, eng="dve", accum_out=None):
        if accum_out is not None:
            return self.add(eng, lambda e: e.scalar_tensor_tensor(out, in0, scalar, in1, op0, op1, accum_out), r, w)
        return self.add(eng, lambda e: e.scalar_tensor_tensor(out, in0, scalar, in1, op0, op1), r, w)

    def cp(self, out, in_, r, w, eng="dve"):
        if eng == "act":
            return self.add("act", lambda e: e.copy(out, in_), r, w)
        return self.add(eng, lambda e: e.tensor_copy(out, in_), r, w)

    def memset(self, out, val, w, eng="dve"):
        return self.add(eng, lambda e: e.memset(out, val), (), w)

    def dma(self, out, in_, r, w, q="sp"):
        return self.add(q, lambda e: e.dma_start(out=out, in_=in_), r, w, dma=True)


class K:
    pass


def _consts():
    c = {}
    c["identb"] = np.eye(128, dtype=np.float32).astype(ml_dtypes.bfloat16)
    c["identf"] = np.eye(128, dtype=np.float32)
    pr = np.zeros((128, 128), np.float32)
    for m in range(128):
        b, i = m // 64, m % 64
        if i < 32:
            pr[b * 64 + i + 32, m] = -1.0
        else:
            pr[b * 64 + i - 32, m] = 1.0
    c["permR"] = pr.astype(ml_dtypes.bfloat16)
    k = np.arange(128, dtype=np.float32)[:, None]
    q = np.arange(128, dtype=np.float32)[None, :]
    dec = np.zeros((128, 4, 128), np.float32)
    dec[:, 0] = np.maximum(q - k, 0)
    dec[:, 1] = np.maximum(k - q, 0)
    dec[:, 2] = (q >= k)
    dec[:, 3] = (k >= q)
    c["dectab"] = dec
    j = np.arange(128, dtype=np.float32)
    prow = np.zeros((128, 2, 128), np.float32)
    prow[:, 0, :] = j + 1.0
    prow[:, 1, :] = 128.0 - j
    c["posrow"] = prow
    p = np.arange(128, dtype=np.float32)
    pc = np.stack([127.0 - p, p, 255.0 - p, 127.0 - p, p, 128.0 + p], axis=1)
    c["poscol"] = np.ascontiguousarray(pc.astype(np.float32))
    t = np.arange(S)
    row = (t // 64).astype(np.float64)
    col = (t % 64).astype(np.float64)

    def tabs(head_dim):
        half = head_dim // 2
        inv = 10000.0 ** (-(np.arange(0, half, 2, dtype=np.float64) / half))
        inv = inv.astype(np.float32).astype(np.float64)
        ang = np.concatenate([row[:, None] * inv, col[:, None] * inv], axis=-1)
        ang = ang.astype(np.float32).astype(np.float64)
        return np.cos(ang).astype(np.float32).T, np.sin(ang).astype(np.float32).T

    cr, sr = tabs(256)
    c["ropeR"] = np.ascontiguousarray(np.stack([cr, sr, cr / 16.0, sr / 16.0], 0).astype(np.float32))
    cd, sd = tabs(64)
    cd4 = np.tile(cd, (4, 1))
    sd4 = np.tile(sd, (4, 1))
    c["ropeD"] = np.ascontiguousarray(np.stack([cd4 / 8.0, sd4 / 8.0, cd4, sd4], 0).astype(np.float32))
    return c


CONST_DT = dict(identb=BF16, identf=F32, permR=BF16, dectab=F32, posrow=F32, poscol=F32, ropeR=F32, ropeD=F32)

IN_SHAPES = dict(
    x=[S, D], ctx=[P, D], ccol=[128, 8, 2], lnin_col=[128, 8, 2], ln1_col=[128, 8, 2],
    lnin_row=[2, D], ln1_row=[2, D], ln2_row=[2, D],
    w_mod=[D, 6 * D], bmod_col=[128, 48], bmod_row=[1, 6 * D], w_in=[D, NIN], bgate_col=[128, 16],
    decay_logit=[1, 8], diff_lambda=[1, 256], subln_col=[128, 1],
    w_ret_out=[2048, D], w_diff_out=[D, D], w_o=[D, D], w_up=[D, 2 * DFF],
    convw_col=[128, 22, 3], convb_col=[128, 22], w_down=[DFF, D],
)


def host_inputs(inp, b, consts):
    f = lambda a: np.ascontiguousarray(np.asarray(a, dtype=np.float32))
    col8 = lambda v: f(v).reshape(8, 128).T
    m = {}
    m["x"] = f(inp["x"][b])
    m["ctx"] = f(inp["ctx"][b])
    m["ccol"] = f(np.stack([col8(inp["c"][b]), col8(inp["c_ctx"])], -1))
    m["lnin_col"] = f(np.stack([col8(inp["ln_in_g"]), col8(inp["ln_in_b"])], -1))
    m["ln1_col"] = f(np.stack([col8(inp["ln1_g"][0]), col8(inp["ln1_b"][0])], -1))
    m["lnin_row"] = f(np.stack([inp["ln_in_g"], inp["ln_in_b"]], 0))
    m["ln1_row"] = f(np.stack([inp["ln1_g"][0], inp["ln1_b"][0]], 0))
    m["ln2_row"] = f(np.stack([inp["ln2_g"][0], inp["ln2_b"][0]], 0))
    m["w_mod"] = f(inp["w_mod"][0])
    m["bmod_col"] = f(f(inp["b_mod"][0]).reshape(48, 128).T)
    m["bmod_row"] = f(inp["b_mod"][0]).reshape(1, -1)
    m["w_in"] = f(inp["w_in"][0])
    m["bgate_col"] = f(f(inp["b_gate"][0]).reshape(16, 128).T)
    m["decay_logit"] = f(inp["ret_decay_logit"][0]).reshape(1, 8)
    m["diff_lambda"] = f(inp["diff_lambda"][0]).reshape(1, 256)
    m["subln_col"] = f(inp["diff_subln_g"][0]).reshape(128, 1)
    m["w_ret_out"] = f(inp["w_ret_out"][0])
    m["w_diff_out"] = f(inp["w_diff_out"][0])
    m["w_o"] = f(inp["w_o"][0])
    m["w_up"] = f(inp["w_up"][0])
    m["convw_col"] = f(f(inp["conv_w"][0]).reshape(3, 22, 128).transpose(2, 1, 0))
    m["convb_col"] = f(f(inp["conv_b"][0]).reshape(22, 128).T)
    m["w_down"] = f(inp["w_down"][0])
    m.update(consts)
    return m


def build_program(debug=()):
    nc = bass.Bass("TRN2", target_bir_lowering=False)
    k = K()
    k.nc = nc
    k.debug = set(debug)
    k.din = {}
    for n, shp in IN_SHAPES.items():
        k.din[n] = nc.dram_tensor(n, shp, F32, kind="ExternalInput").ap()
    cst = _consts()
    for n, arr in cst.items():
        k.din[n] = nc.dram_tensor(n, list(arr.shape), CONST_DT[n], kind="ExternalInput").ap()
    k.out = nc.dram_tensor("out", [S, D], F32, kind="ExternalOutput").ap()
    k.dbg = {}

    def dbg_t(name, shape, dt=F32):
        if name in k.debug:
            k.dbg[name] = nc.dram_tensor("dbg_" + name, shape, dt, kind="ExternalOutput").ap()
            return k.dbg[name]
        return None
    k.dbg_t = dbg_t
    k.retpreT = nc.dram_tensor("retpreT", [2048, S], BF16, kind="Internal").ap()
    k.difpreT = nc.dram_tensor("difpreT", [1024, S], BF16, kind="Internal").ap()
    k.ybscr = nc.dram_tensor("ybscr", [4, 32, 128, 512], F32, kind="Internal").ap()
    k.x1scr = nc.dram_tensor("x1scr", [S, D], F32, kind="Internal").ap()
    k.growscr = nc.dram_tensor("growscr", [128, 2048], F32, kind="Internal").ap()
    k.wbf = {}
    k.Bwbf = {}
    for n_, shp in (("w_ret_out", [2048, D]), ("w_diff_out", [D, D]), ("w_o", [D, D]), ("w_up", [D, 2 * DFF]),
                    ("w_down", [DFF, D]), ("w_gates", [D, 2048])):
        k.wbf[n_] = nc.dram_tensor("wbf_" + n_, shp, BF16, kind="Internal").ap()
        k.Bwbf[n_] = Buf("wbf_" + n_)

    with ExitStack() as es:
        pr = Prog(nc, es)
        k.pr = pr
        sb = lambda n, s, d: es.enter_context(nc.sbuf_tensor("sb_" + n, s, d))
        k.xmT = sb("xmT", [128, 8, 1 + NT], BF16)
        k.Bx = [Buf("xm%d" % g) for g in range(9)]
        k.Bxpad = Buf("xmpad")
        k.NW = 6
        k.wsl = [sb("w%d" % i, [128, 8, 512], BF16) for i in range(k.NW)]
        k.Bw = [Buf("w%d" % i) for i in range(k.NW)]
        k.wctr = 0
        k.psall = es.enter_context(nc.psum_tensor("psall", [128, 8, 512], F32))
        k.ps = [k.psall[:, i, :] for i in range(8)]
        k.Bps = [Buf("ps%d" % i, excl=True) for i in range(8)]
        k.identb = sb("identb", [128, 128], BF16)
        k.identf = sb("identf", [128, 128], F32)
        k.permR = sb("permR", [128, 128], BF16)
        k.poscol = sb("poscol", [128, 6], F32)
        k.Bc = Buf("consts")
        k.small = sb("small", [128, 64], F32)
        k.Bsm = Buf("small")
        k.modT = sb("modT", [128, 48, 2], F32)
        k.Bmod = Buf("modT")
        k.AB1 = sb("AB1", [128, 8, 4], F32)
        k.AB2 = sb("AB2", [128, 8, 2], F32)
        k.Bgrow = Buf("grow")
        k.stats = sb("stats", [128, 34, 4], F32)
        k.Bst = [Buf("st%d" % i) for i in range(34)]
        k.dec = sb("dec", [128, 4, 3, 128], F32)
        k.deccol = sb("deccol", [128, 4, 8], F32)
        k.Bdec = Buf("dec")
        k.bgate = sb("bgate", [128, 16], F32)
        k.convw = sb("convw", [128, 22, 3], F32)
        k.convb = sb("convb", [128, 22], F32)
        k.subln = sb("subln", [128, 1], F32)

        phase0(k)
        phaseA(k)
        if "stopA" not in k.debug:
            if "skipB2" not in k.debug:
                phaseB2(k)
            if "stopB2" not in k.debug:
                phaseB1(k)
                if "stopB1" not in k.debug:
                    phaseC1(k)
                    if "stopC1" not in k.debug:
                        phaseC2(k)
        pr.flush(final=False)
    return nc, cst


def wtile_load(k, pieces, q="pool", rd=()):
    pr = k.pr
    s = k.wctr % k.NW
    k.wctr += 1
    for off, src in pieces:
        nk = src.shape[0] // 128
        n = src.shape[1]
        qq = q if q == "pool" else ("sp" if k.wctr % 2 == 0 else "act")
        pr.dma(k.wsl[s][:, 0:nk, off:off + n], src.rearrange("(kc p) c -> p kc c", p=128), list(rd), [k.Bw[s]], q=qq)
    return k.wsl[s], k.Bw[s]


def rstd_ops(k, out_col, var_col, tmp_col, r, w, eng="act"):
    pr = k.pr
    pr.act(tmp_col, var_col, AF.Ln, list(r) + [k.Bsm], w, bias=k.small[:, 0:1])
    pr.act(out_col, tmp_col, AF.Exp, w, w, scale=-0.5)


def phase0(k):
    pr, nc, din = k.pr, k.nc, k.din
    sm = k.small
    with ExitStack() as pes:
        sb = lambda n, s, d: pes.enter_context(nc.sbuf_tensor("sp_" + n, s, d))
        k.grow = sb("grow", [128, 2, 1024], F32)
        k.dectab = sb("dectab", [128, 4, 128], F32)
        k.posrow = sb("posrow", [128, 2, 128], F32)
        pr.dma(k.identb[:], din["identb"], [], [k.Bc])
        pr.dma(k.identf[:], din["identf"], [], [k.Bc])
        pr.dma(k.permR[:], din["permR"], [], [k.Bc])
        pr.dma(k.dectab[:], din["dectab"], [], [k.Bc])
        pr.dma(k.posrow[:], din["posrow"], [], [k.Bc])
        pr.dma(k.poscol[:], din["poscol"], [], [k.Bc])
        pr.dma(k.bgate[:], din["bgate_col"], [], [k.Bc])
        pr.dma(k.convw[:], din["convw_col"], [], [k.Bc])
        pr.dma(k.convb[:], din["convb_col"], [], [k.Bc])
        pr.dma(k.subln[:], din["subln_col"], [], [k.Bc])
        pr.memset(sm[:], 0.0, [k.Bsm])
        pr.memset(sm[:, 0:1], EPS, [k.Bsm])
        pr.memset(k.xmT[:, :, 0:1], 0.0, [k.Bxpad])
        ccol = sb("ccol", [128, 8, 2], F32)
        cond = sb("cond", [128, 8, 2], F32)
        lnin = sb("lnin", [128, 8, 2], F32)
        ln1c = sb("ln1c", [128, 8, 2], F32)
        bmodc = sb("bmodc", [128, 48], F32)
        brow = sb("brow", [128, 2, 1024], F32)
        condrep = sb("condrep", [128, 8, 128], F32)
        ones = sb("ones", [128, 128], F32)
        lamb = sb("lamb", [128, 256], F32)
        tmp = sb("tmp0", [128, 8, 2], F32)
        B0 = Buf("p0")
        Bcr = Buf("condrep")
        pr.dma(ccol[:], din["ccol"], [], [B0])
        pr.dma(lnin[:], din["lnin_col"], [], [B0])
        pr.dma(ln1c[:], din["ln1_col"], [], [B0])
        pr.dma(bmodc[:], din["bmod_col"], [], [B0])
        pr.dma(brow[:, 0, :], din["bmod_row"][:, 2048:3072].partition_broadcast(128), [], [B0])
        pr.dma(brow[:, 1, :], din["bmod_row"][:, 5120:6144].partition_broadcast(128), [], [B0])
        pr.dma(sm[:, 8:16], din["decay_logit"].partition_broadcast(128), [], [k.Bsm])
        pr.dma(lamb[:], din["diff_lambda"].partition_broadcast(128), [], [B0])
        pr.memset(ones[:], 1.0, [B0])
        pr.act(cond[:], ccol[:], AF.Silu, [B0], [B0])
        for kc in range(8):
            pr.ts(condrep[:, kc, :], ones[:], cond[:, kc, 0:1], None, ALU.mult, None, [B0], [Bcr])
        pr.tt(lamb[:, 0:64], lamb[:, 0:64], lamb[:, 64:128], ALU.mult, [B0], [B0])
        pr.tt(lamb[:, 128:192], lamb[:, 128:192], lamb[:, 192:256], ALU.mult, [B0], [B0])
        pr.add("dve", lambda e: e.reduce_sum(sm[:, 24:25], lamb[:, 0:64], mybir.AxisListType.X), [B0], [k.Bsm])
        pr.add("dve", lambda e: e.reduce_sum(sm[:, 25:26], lamb[:, 128:192], mybir.AxisListType.X), [B0], [k.Bsm])
        pr.act(sm[:, 24:26], sm[:, 24:26], AF.Exp, [k.Bsm], [k.Bsm])
        pr.tt(sm[:, 1:2], sm[:, 24:25], sm[:, 25:26], ALU.subtract, [k.Bsm], [k.Bsm])
        pr.ts(sm[:, 1:2], sm[:, 1:2], LAM_INIT, None, ALU.add, None, [k.Bsm], [k.Bsm])
        pr.ts(sm[:, 2:3], sm[:, 1:2], -1.0, None, ALU.mult, None, [k.Bsm], [k.Bsm])
        pr.ts(sm[:, 3:4], k.subln[:, 0:1], 1.0 - LAM_INIT, None, ALU.mult, None, [k.Bc], [k.Bsm])
        pr.act(sm[:, 16:24], sm[:, 8:16], AF.Exp, [k.Bsm], [k.Bsm], scale=-1.0)
        pr.ts(sm[:, 16:24], sm[:, 16:24], 1.0, None, ALU.add, None, [k.Bsm], [k.Bsm])
        pr.act(sm[:, 16:24], sm[:, 16:24], AF.Ln, [k.Bsm], [k.Bsm])
        pr.ts(sm[:, 16:24], sm[:, 16:24], -1.0, None, ALU.mult, None, [k.Bsm], [k.Bsm])
        dt = k.dectab
        t1 = sb("t1", [128, 128], F32)
        t2 = sb("t2", [128, 128], F32)
        Bt = Buf("t12")
        for h in range(4):
            lgf = sm[:, 16 + h:17 + h]
            lgb = sm[:, 20 + h:21 + h]
            pr.act(t1[:], dt[:, 0, :], AF.Exp, [k.Bc, k.Bsm], [Bt], scale=lgf)
            pr.tt(t1[:], t1[:], dt[:, 2, :], ALU.mult, [Bt, k.Bc], [Bt])
            pr.act(t2[:], dt[:, 1, :], AF.Exp, [k.Bc, k.Bsm], [Bt], scale=lgb)
            pr.tt(t2[:], t2[:], dt[:, 3, :], ALU.mult, [Bt, k.Bc], [Bt])
            pr.tt(k.dec[:, h, 0, :], t1[:], t2[:], ALU.add, [Bt], [k.Bdec])
            pr.act(k.dec[:, h, 1, :], k.posrow[:, 0, :], AF.Exp, [k.Bc, k.Bsm], [k.Bdec], scale=lgf)
            pr.act(k.dec[:, h, 2, :], k.posrow[:, 1, :], AF.Exp, [k.Bc, k.Bsm], [k.Bdec], scale=lgb)
            dc = k.deccol
            pc = k.poscol
            pr.act(dc[:, h, 0:1], pc[:, 0:1], AF.Exp, [k.Bc, k.Bsm], [k.Bdec], scale=lgf)
            pr.act(dc[:, h, 1:2], pc[:, 1:2], AF.Exp, [k.Bc, k.Bsm], [k.Bdec], scale=lgb)
            pr.act(dc[:, h, 2:3], lgf, AF.Exp, [k.Bsm], [k.Bdec], scale=128.0)
            pr.act(dc[:, h, 3:4], lgb, AF.Exp, [k.Bsm], [k.Bdec], scale=128.0)
            pr.act(dc[:, h, 4:6], pc[:, 2:4], AF.Exp, [k.Bc, k.Bsm], [k.Bdec], scale=lgf)
            pr.act(dc[:, h, 6:8], pc[:, 4:6], AF.Exp, [k.Bc, k.Bsm], [k.Bdec], scale=lgb)
        wm = [sb("wm%d" % i, [128, 8, 512], F32) for i in range(3)]
        Bwm = [Buf("wm%d" % i) for i in range(3)]
        psc = k.ps[0]
        for j in range(12):
            s = j % 3
            pr.dma(wm[s][:], din["w_mod"][:, j * 512:(j + 1) * 512].rearrange("(kc p) c -> p kc c", p=128),
                   [], [Bwm[s]], q=("sp" if j % 2 == 0 else "act"))
            for o in range(4):
                oc = j * 4 + o
                for kc in range(8):
                    pr.mm(psc[:, oc * 2:oc * 2 + 2], wm[s][:, kc, o * 128:(o + 1) * 128], cond[:, kc, :],
                          kc == 0, kc == 7, [Bwm[s], B0], [k.Bps[0]])
            if j in (4, 5, 10, 11):
                gi, half = (0 if j < 6 else 1), j % 2
                pb = k.ps[1 + (j % 2)]
                for kc in range(8):
                    pr.mm(pb[:], condrep[:, kc, :], wm[s][:, kc, :], kc == 0, kc == 7, [Bwm[s], Bcr], [k.Bps[1 + (j % 2)]])
                pr.tt(k.grow[:, gi, half * 512:(half + 1) * 512], pb[:], brow[:, gi, half * 512:(half + 1) * 512],
                      ALU.add, [k.Bps[1 + (j % 2)], B0], [k.Bgrow])
        pview = psc[:, 0:96].rearrange("p (o s) -> p o s", s=2)
        for s2 in range(2):
            pr.tt(k.modT[:, :, s2], pview[:, :, s2], bmodc[:], ALU.add, [k.Bps[0], B0], [k.Bmod])
        for s2 in range(2):
            pr.ts(tmp[:, :, s2], k.modT[:, 8:16, s2], 1.0, None, ALU.add, None, [k.Bmod], [B0])
            pr.tt(k.AB1[:, :, s2], tmp[:, :, s2], lnin[:, :, 0], ALU.mult, [B0], [k.Bmod])
            pr.tt(k.AB1[:, :, 2 + s2], tmp[:, :, s2], lnin[:, :, 1], ALU.mult, [B0], [k.Bmod])
            pr.tt(k.AB1[:, :, 2 + s2], k.AB1[:, :, 2 + s2], k.modT[:, 0:8, s2], ALU.add, [k.Bmod], [k.Bmod])
        pr.ts(tmp[:, :, 0], k.modT[:, 32:40, 0], 1.0, None, ALU.add, None, [k.Bmod], [B0])
        pr.tt(k.AB2[:, :, 0], tmp[:, :, 0], ln1c[:, :, 0], ALU.mult, [B0], [k.Bmod])
        pr.tt(k.AB2[:, :, 1], tmp[:, :, 0], ln1c[:, :, 1], ALU.mult, [B0], [k.Bmod])
        pr.tt(k.AB2[:, :, 1], k.AB2[:, :, 1], k.modT[:, 24:32, 0], ALU.add, [k.Bmod], [k.Bmod])
        d = k.dbg_t("modT", [128, 96])
        if d is not None:
            pr.dma(d, k.modT[:].rearrange("p o s -> p (o s)"), [k.Bmod], [])
        Bgs = Buf("growscr")
        k.Bgs = Bgs
        pr.dma(k.growscr, k.grow[:].rearrange("p o s -> p (o s)"), [k.Bgrow], [Bgs])
        d = k.dbg_t("grow", [128, 2048])
        if d is not None:
            pr.dma(d, k.grow[:].rearrange("p o s -> p (o s)"), [k.Bgrow], [])
        d = k.dbg_t("small", [128, 64])
        if d is not None:
            pr.dma(d, k.small[:], [k.Bsm], [])
        d = k.dbg_t("dec", [128, 4 * 3 * 128])
        if d is not None:
            pr.dma(d, k.dec[:].rearrange("p a b c -> p (a b c)"), [k.Bdec], [])
        d = k.dbg_t("deccol", [128, 32])
        if d is not None:
            pr.dma(d, k.deccol[:].rearrange("p a b -> p (a b)"), [k.Bdec], [])
        pr.flush()


def precast_weights(k):
    pr, din = k.pr, k.din
    for n_ in ("w_ret_out", "w_diff_out", "w_o", "w_up", "w_down", "w_gates"):
        src = din["w_in"][:, OFF_GTR:OFF_GTR + 2048] if n_ == "w_gates" else din[n_]
        rows = src.shape[0]
        for r0 in range(0, rows, 256):
            r1 = min(rows, r0 + 256)
            pr.dma(k.wbf[n_][r0:r1, :], src[r0:r1, :], [], [k.Bwbf[n_]], q="pool")


def phaseA(k):
    pr, nc, din = k.pr, k.nc, k.din
    precast_weights(k)
    with ExitStack() as pes:
        sb = lambda n, s, d: pes.enter_context(nc.sbuf_tensor("sp_" + n, s, d))
        xin = [sb("xin%d" % i, [128, 1024], F32) for i in range(3)]
        Bxin = [Buf("xin%d" % i) for i in range(3)]
        xnb = [sb("xnb%d" % i, [128, 1024], BF16) for i in range(2)]
        Bxnb = [Buf("xnb%d" % i) for i in range(2)]
        st6 = [sb("st6%d" % i, [128, 2, 6], F32) for i in range(2)]
        Bst6 = [Buf("st6%d" % i) for i in range(2)]
        for tt in range(34):
            lat = tt < 32
            src = din["x"][tt * 128:(tt + 1) * 128, :] if lat else din["ctx"][(tt - 32) * 128:(tt - 31) * 128, :]
            s3, s2 = tt % 3, tt % 2
            pr.dma(xin[s3][:], src, [], [Bxin[s3]], q=("sp" if tt % 2 == 0 else "act"))
            for i in range(2):
                pr.add("dve", lambda e, i=i, s3=s3, s2=s2: e.bn_stats(st6[s2][:, i, :], xin[s3][:, i * 512:(i + 1) * 512]),
                       [Bxin[s3]], [Bst6[s2]])
            stt_ = k.stats[:, tt, :]
            pr.add("dve", lambda e, s2=s2, stt_=stt_: e.bn_aggr(stt_[:, 0:2], st6[s2][:].rearrange("p a b -> p (a b)")),
                   [Bst6[s2]], [k.Bst[tt]])
            rstd_ops(k, stt_[:, 3:4], stt_[:, 1:2], stt_[:, 2:3], [k.Bst[tt]], [k.Bst[tt]])
            pr.ts(xnb[s2][:], xin[s3][:], stt_[:, 0:1], stt_[:, 3:4], ALU.subtract, ALU.mult,
                  [Bxin[s3], k.Bst[tt]], [Bxnb[s2]])
            pb = k.ps[s2].bitcast(BF16)
            for kc in range(8):
                pr.tr(pb[:, kc * 128:(kc + 1) * 128], xnb[s2][:, kc * 128:(kc + 1) * 128], k.identb[:],
                      [Bxnb[s2], k.Bc], [k.Bps[s2]])
            g = tt // 4 if lat else 8
            si = 0 if lat else 1
            for kc in range(8):
                pr.act(k.xmT[:, kc, 1 + tt * 128:1 + (tt + 1) * 128], pb[:, kc * 128:(kc + 1) * 128], AF.Identity,
                       [k.Bps[s2], k.Bmod], [k.Bx[g]], scale=k.AB1[:, kc, si:si + 1], bias=k.AB1[:, kc, 2 + si:3 + si],
                       eng=("act" if kc % 2 == 0 else "act"))
        d = k.dbg_t("xmT", [128, 8 * (1 + NT)], BF16)
        if d is not None:
            pr.dma(d, k.xmT[:].rearrange("p a b -> p (a b)"), k.Bx + [k.Bxpad], [])
        d = k.dbg_t("stats", [128, 34 * 4])
        if d is not None:
            pr.dma(d, k.stats[:].rearrange("p a b -> p (a b)"), k.Bst, [])
        pr.flush()


def phaseB2(k):
    pr, nc, din = k.pr, k.nc, k.din
    sm = k.small
    with ExitStack() as pes:
        sb = lambda n, s, d: pes.enter_context(nc.sbuf_tensor("sp_" + n, s, d))
        QT = [sb("dQT%d" % i, [128, S], BF16) for i in range(2)]
        KT = [sb("dKT%d" % i, [128, NT], BF16) for i in range(2)]
        Vg = [sb("dV%d" % i, [128, 34, 128], BF16) for i in range(2)]
        BQ = [[Buf("dq%d_%d" % (i, g)) for g in range(8)] for i in range(2)]
        BK = [[Buf("dk%d_%d" % (i, g)) for g in range(9)] for i in range(2)]
        BV = [[Buf("dv%d_%d" % (i, g)) for g in range(9)] for i in range(2)]
        tab = sb("dtab", [128, 4, 512], F32)
        Btab = Buf("dtab")
        qbf = [sb("dqbf%d" % i, [128, 512], BF16) for i in range(2)]
        Bqbf = [Buf("dqbf%d" % i) for i in range(2)]
        t1 = [sb("dt1%d" % i, [128, 512], F32) for i in range(2)]
        t2 = [sb("dt2%d" % i, [128, 512], F32) for i in range(2)]
        Bt1 = [Buf("dt1%d" % i) for i in range(2)]
        Bt2 = [Buf("dt2%d" % i) for i in range(2)]
        Pt = [sb("dP%d" % i, [128, 512], BF16) for i in range(3)]
        BP = [Buf("dP%d" % i) for i in range(3)]
        o32 = [sb("do%d" % i, [128, 256], F32) for i in range(2)]
        Bo = [Buf("do%d" % i) for i in range(2)]
        Osb = sb("dOsb", [128, 512], F32)
        BOsb = Buf("dOsb")
        rr = sb("drr", [128, 512], F32)
        Brr = Buf("drr")
        rs = sb("drs", [128, 256], F32)
        Brs = Buf("drs")
        onesf = sb("donesf", [128, 128], F32)
        Bones = Buf("dones")
        pr.memset(onesf[:], 1.0, [Bones])
        onesb = sb("donesb", [128, 128], BF16)
        pr.memset(onesb[:], 1.0, [Bones])
        stg = [sb("dstg%d" % i, [128, 512], BF16) for i in range(2)]
        Bstg = [Buf("dstg%d" % i) for i in range(2)]
        w_in = din["w_in"]
        NH_ = 1 if "nh1" in k.debug else 8
        NQC_ = 2 if "nqc2" in k.debug else 16
        lamneg = sm[:, 2:3]
        ctr = [0]

        def wload(h):
            return wtile_load(k, [(0, w_in[:, OFF_QD + h * 128:OFF_QD + (h + 1) * 128]),
                                  (128, w_in[:, OFF_KD + h * 128:OFF_KD + (h + 1) * 128]),
                                  (256, w_in[:, OFF_VD + h * 128:OFF_VD + (h + 1) * 128])])

        def proj_units(h, s_, W, BW):
            units = []
            pb, Bpb = k.ps[5], k.Bps[5]
            pr2, Bpr2 = k.ps[7], k.Bps[7]
            for g in range(8):
                for which in range(2):
                    i = (g * 2 + which) % 2

                    def st0(g=g, which=which):
                        if which == 0:
                            for t_ in range(4):
                                pr.dma(tab[:, t_, :], din["ropeD"][t_, :, g * 512:(g + 1) * 512], [], [Btab], q="sp")
                        for kc in range(8):
                            pr.mm(pb, W[:, kc, which * 128:(which + 1) * 128], k.xmT[:, kc, 1 + g * 512:1 + (g + 1) * 512],
                                  kc == 0, kc == 7, [BW, k.Bx[g]], [Bpb])

                    def st1(i=i):
                        pr.cp(qbf[i][:], pb, [Bpb], [Bqbf[i]], eng="act")

                    def st2(i=i):
                        pr.mm(pr2, k.permR[:], qbf[i][:], True, True, [k.Bc, Bqbf[i]], [Bpr2])

                    def st3(i=i, which=which):
                        pr.tt(t1[i][:], pb, tab[:, 2 * which, :], ALU.mult, [Bpb, Btab], [Bt1[i]])
                        pr.tt(t2[i][:], pr2, tab[:, 2 * which + 1, :], ALU.mult, [Bpr2, Btab], [Bt2[i]])

                    def st4(i=i, g=g, which=which):
                        dst = (QT[s_] if which == 0 else KT[s_])[:, g * 512:(g + 1) * 512]
                        pr.tt(dst, t1[i][:], t2[i][:], ALU.add, [Bt1[i], Bt2[i]], [(BQ[s_] if which == 0 else BK[s_])[g]], eng="pool")
                    units.append([st0, st1, st2, st3, st4])

            def c0():
                for kc in range(8):
                    pr.mm(pb[:, 0:256], W[:, kc, 128:256], k.xmT[:, kc, 1 + S:1 + NT], kc == 0, kc == 7, [BW, k.Bx[8]], [Bpb])

            def c1():
                pr.cp(KT[s_][:, S:NT], pb[:, 0:256], [Bpb], [BK[s_][8]], eng="act")
            units.append([c0, c1])
            for g in range(9):
                def v0(g=g):
                    nch = 4 if g < 8 else 2
                    for j in range(nch):
                        tc = g * 4 + j
                        for kc in range(8):
                            pr.mm(pb[:, j * 128:(j + 1) * 128], k.xmT[:, kc, 1 + tc * 128:1 + (tc + 1) * 128], W[:, kc, 256:384],
                                  kc == 0, kc == 7, [BW, k.Bx[g]], [Bpb])

                def v1(g=g):
                    nch = 4 if g < 8 else 2
                    pr.cp(Vg[s_][:, g * 4:g * 4 + nch, :], pb[:, 0:nch * 128].rearrange("p (a b) -> p a b", b=128),
                          [Bpb], [BV[s_][g]], eng="dve")
                units.append([v0, v1])
            return units

        Wn = wload(0)
        for u in proj_units(0, 0, Wn[0], Wn[1]):
            for st in u:
                st()
        for h in range(NH_):
            s_ = h % 2
            pend = []
            if h + 1 < NH_:
                Wn = wload(h + 1)
                pend = proj_units(h + 1, (h + 1) % 2, Wn[0], Wn[1])
            if "noattn" in k.debug:
                for u in pend:
                    for st in u:
                        st()
                continue
            sched = {}
            its = [(qc, kc) for qc in range(NQC_) for kc in range(34)]
            every = max(15, (len(its) - 40) // max(1, len(pend))) if pend else 0

            def emitS(i):
                qc, kc = its[i]
                sset = 2 * (i % 2)
                gk = min(kc // 4, 8)
                for c in range(2):
                    pr.mm(k.ps[sset + c][:, 0:256], KT[s_][c * 64:(c + 1) * 64, kc * 128:(kc + 1) * 128],
                          QT[s_][c * 64:(c + 1) * 64, qc * 256:(qc + 1) * 256], True, True,
                          [BK[s_][gk], BQ[s_][qc // 2]], [k.Bps[sset + c]])

            def epi0(qc):
                pr.cp(Osb[:], k.ps[4], [k.Bps[4]], [BOsb], eng="act")
                pr.add("dve", lambda e_: e_.reciprocal(rr[:], k.ps[6]), [k.Bps[6]], [Brr])

            def epiA(qc):
                pr.ts(rr[:, 256:512], rr[:, 256:512], lamneg, None, ALU.mult, None, [Brr, k.Bsm], [Brr])
                pr.tt(o32[0][:], Osb[:, 0:256], rr[:, 0:256], ALU.mult, [BOsb, Brr], [Bo[0]])
                pr.tt(o32[1][:], Osb[:, 256:512], rr[:, 256:512], ALU.mult, [BOsb, Brr], [Bo[1]])
                pr.tt(o32[0][:], o32[0][:], o32[1][:], ALU.add, [Bo[0], Bo[1]], [Bo[0]], eng="pool")
                pr.tt(o32[1][:], o32[0][:], o32[0][:], ALU.mult, [Bo[0]], [Bo[1]], eng="pool")

            def epiB(qc, h=h):
                pr.mm(k.ps[0][:, 256:512], onesf[:], o32[1][:], True, True, [Bones, Bo[1]], [k.Bps[0]])
                pr.act(rs[:], k.ps[0][:, 256:512], AF.Ln, [k.Bps[0], k.Bsm], [Brs], scale=1.0 / 128.0, bias=sm[:, 0:1])
                pr.act(rs[:], rs[:], AF.Exp, [Brs], [Brs], scale=-0.5)
                si = (qc // 2) % 2
                pr.stt(stg[si][:, (qc % 2) * 256:(qc % 2 + 1) * 256], o32[0][:], sm[:, 3:4], rs[:], ALU.mult, ALU.mult,
                       [Bo[0], Brs, k.Bsm], [Bstg[si]])
                if qc % 2 == 1:
                    pr.dma(k.difpreT[h * 128:(h + 1) * 128, (qc - 1) * 256:(qc + 1) * 256], stg[si][:], [Bstg[si]], [], q="sp")

            emitS(0)
            dsum = [None]
            for i, (qc, kc) in enumerate(its):
                sset = 2 * (i % 2)
                gk = min(kc // 4, 8)
                if i + 1 < len(its):
                    emitS(i + 1)
                if dsum[0] is not None:
                    dsum[0]()
                    dsum[0] = None
                p_ = i % 3
                pr.act(Pt[p_][:].rearrange("p (c q) -> p c q", c=2), k.psall[:, sset:sset + 2, 0:256], AF.Exp,
                       [k.Bps[sset], k.Bps[sset + 1]], [BP[p_]])

                def sums_(p_=p_, kc=kc, gk=gk):
                    pr.mm(k.ps[4], Vg[s_][:, kc, :], Pt[p_][:], kc == 0, kc == 33, [BP[p_], BV[s_][gk]], [k.Bps[4]])
                    pr.mm(k.ps[6], onesb[:], Pt[p_][:], kc == 0, kc == 33, [BP[p_], Bones], [k.Bps[6]])
                if kc == 33:
                    sums_()
                else:
                    dsum[0] = sums_
                if kc == 33:
                    epi0(qc)
                if qc > 0 and kc == 1:
                    epiA(qc - 1)
                if qc > 0 and kc == 6:
                    epiB(qc - 1)
                if pend and i >= 12 and (i - 12) % every == 0:
                    for j_, st in enumerate(pend.pop(0)):
                        sched.setdefault(i + 3 * j_, []).append(st)
                for st in sched.pop(i, []):
                    st()
            epiA(NQC_ - 1)
            epiB(NQC_ - 1)
            for i_ in sorted(sched):
                for st in sched[i_]:
                    st()
            for u in pend:
                for st in u:
                    st()
        d = k.dbg_t("difpreT", [1024, S], BF16)
        if d is not None:
            pr.flush()
            for i_ in range(8):
                pr.dma(d[i_ * 128:(i_ + 1) * 128, :], k.difpreT[i_ * 128:(i_ + 1) * 128, :], [], [])
        pr.flush()


def phaseB1(k):
    pr, nc, din = k.pr, k.nc, k.din
    sm = k.small
    with ExitStack() as pes:
        sb = lambda n, s, d: pes.enter_context(nc.sbuf_tensor("sp_" + n, s, d))
        NS = 2
        QT = [sb("rQT%d" % i, [128, 2, 512], BF16) for i in range(NS)]
        QsT = [sb("rQsT%d" % i, [128, 2, 512], BF16) for i in range(NS)]
        KT = [sb("rKT%d" % i, [128, 2, 512], BF16) for i in range(NS)]
        Ktok = [sb("rKtok%d" % i, [128, 4, 256], BF16) for i in range(NS)]
        V = [sb("rV%d" % i, [128, 4, 512], BF16) for i in range(NS)]
        G = [sb("rG%d" % i, [128, 4, 512], BF16) for i in range(NS)]
        tabR = [sb("rtab0", [128, 4, 512], F32)] * NS
        BS = [{n: Buf("r%s%d" % (n, i)) for n in "QT QsT KT Ktok V G".split()} for i in range(NS)]
        _bt = Buf("rtab")
        for _b in BS:
            _b["tab"] = _bt
        ra = [sb("rr%d" % i, [128, 512], F32) for i in range(4)]
        dqrep = sb("rdq", [128, 2, 512], F32)
        S32 = sb("rS32", [128, 2, 512], F32)
        Sbf = sb("rSbf", [128, 2, 512], BF16)
        ybt = [sb("rybt%d" % i, [128, 512], F32) for i in range(2)]
        ytot = sb("rytot", [128, 512], F32)
        og = sb("rog", [128, 512], BF16)
        innb = sb("rinnb", [128, 128], BF16)
        stg = [sb("rstg0", [128, 4, 512], BF16)] * 2
        KcT = sb("rKcT", [128, 2, 256], BF16)
        Kcw = sb("rKcw", [128, 2, 2, 256], BF16)
        Vc = sb("rVc", [128, 2, 512], BF16)
        gst = sb("rgst", [128, 16], F32)
        B = {n: Buf("r" + n) for n in "dq S32 Sbf ytot og innb KcT Kcw Vc gst".split()}
        BSb = [Buf("rSbf0"), Buf("rSbf1")]
        BS32 = [Buf("rS32_0"), Buf("rS32_1")]
        Bra = [Buf("rra%d" % i) for i in range(4)]
        Bybt = [Buf("rybt%d" % i) for i in range(2)]
        Bstg = [Buf("rstg0")] * 2
        Byb = [Buf("rybscr%d" % i) for i in range(32)]
        w_in = din["w_in"]
        ps, Bps = k.ps, k.Bps
        NH_ = 1 if "nh1" in k.debug else 4
        ybctr = [0]

        def proj_mm(W, BW, g, coff, b0):
            for dc in range(2):
                for kc in range(8):
                    pr.mm(ps[b0 + dc], W[:, kc, coff + dc * 128:coff + (dc + 1) * 128], k.xmT[:, kc, 1 + g * 512:1 + (g + 1) * 512],
                          kc == 0, kc == 7, [BW, k.Bx[g]], [Bps[b0 + dc]])

        def rope_ops(s_, b0, tq, dst, Bdst, scale_rep=None, dst2=None, Bdst2=None):
            tb = tabR[s_]
            Bt = BS[s_]["tab"]
            cs, sn = tb[:, tq, :], tb[:, tq + 1, :]
            pr.tt(ra[0][:], ps[b0], cs, ALU.mult, [Bps[b0], Bt], [Bra[0]])
            pr.tt(ra[3][:], ps[b0], sn, ALU.mult, [Bps[b0], Bt], [Bra[3]])
            pr.tt(ra[1][:], ps[b0 + 1], sn, ALU.mult, [Bps[b0 + 1], Bt], [Bra[1]])
            pr.tt(ra[2][:], ps[b0 + 1], cs, ALU.mult, [Bps[b0 + 1], Bt], [Bra[2]])
            if dst2 is None:
                pr.tt(dst[:, 0, :], ra[0][:], ra[1][:], ALU.subtract, [Bra[0], Bra[1]], [Bdst], eng="pool")
                pr.tt(dst[:, 1, :], ra[2][:], ra[3][:], ALU.add, [Bra[2], Bra[3]], [Bdst], eng="pool")
            else:
                pr.tt(ra[0][:], ra[0][:], ra[1][:], ALU.subtract, [Bra[0], Bra[1]], [Bra[0]], eng="pool")
                pr.tt(ra[2][:], ra[2][:], ra[3][:], ALU.add, [Bra[2], Bra[3]], [Bra[2]], eng="pool")
                if dst is not None:
                    pr.cp(dst[:, 0, :], ra[0][:], [Bra[0]], [Bdst], eng="act")
                    pr.cp(dst[:, 1, :], ra[2][:], [Bra[2]], [Bdst], eng="act")
                pr.tt(dst2[:, 0, :], ra[0][:], scale_rep, ALU.mult, [Bra[0], B["dq"]], [Bdst2])
                pr.tt(dst2[:, 1, :], ra[2][:], scale_rep, ALU.mult, [Bra[2], B["dq"]], [Bdst2])

        def ktok_make(s_, dkcol):
            ptb = ps[4].bitcast(BF16)
            for ci in range(4):
                for dc in range(2):
                    pr.tr(ptb[:, (ci * 2 + dc) * 128:(ci * 2 + dc + 1) * 128], KT[s_][:, dc, ci * 128:(ci + 1) * 128], k.identb[:],
                          [BS[s_]["KT"], k.Bc], [Bps[4]])
            pr.act(Ktok[s_][:].rearrange("p a b -> p (a b)"), ptb, AF.Copy, [Bps[4], k.Bdec], [BS[s_]["Ktok"]], scale=dkcol)

        def v_make(W, BW, g, dst, Bdst, cis, silu=False):
            for ci in cis:
                pb = ci % 2
                tc = g * 4 + ci
                for kc in range(8):
                    pr.mm(ps[pb], k.xmT[:, kc, 1 + tc * 128:1 + (tc + 1) * 128], W[:, kc, :], kc == 0, kc == 7,
                          [BW, k.Bx[g]], [Bps[pb]])
                if silu:
                    pr.act(dst[:, ci, :], ps[pb], AF.Silu, [Bps[pb]], [Bdst])
                else:
                    pr.cp(dst[:, ci, :], ps[pb], [Bps[pb]], [Bdst], eng="act")

        def state_update(s_, ci, dscol, alt=7, mid=None):
            banks = (7, alt)

            def mm_(dc):
                pr.mm(ps[banks[dc]], Ktok[s_][:, ci, dc * 128:(dc + 1) * 128], V[s_][:, ci, :], True, True,
                      [BS[s_]["Ktok"], BS[s_]["V"]], [Bps[banks[dc]]])

            def upd_(dc):
                pr.stt(S32[:, dc, :], S32[:, dc, :], dscol, ps[banks[dc]], ALU.mult, ALU.add,
                       [BS32[dc], Bps[banks[dc]], k.Bdec], [BS32[dc]])
                pr.cp(Sbf[:, dc, :], S32[:, dc, :], [BS32[dc]], [BSb[dc]], eng="act")
            if alt != 7:
                mm_(0)
                mm_(1)
                upd_(0)
                upd_(1)
            else:
                mm_(0)
                if mid is not None:
                    mid()
                upd_(0)
                mm_(1)
                upd_(1)

        def init_state(dirn):
            for dc in range(2):
                for j in range(2):
                    pr.mm(ps[7], Kcw[:, dirn, j, dc * 128:(dc + 1) * 128], Vc[:, j, :], j == 0, j == 1, [B["Kcw"], B["Vc"]], [Bps[7]])
                pr.cp(S32[:, dc, :], ps[7], [Bps[7]], [BS32[dc]], eng="dve")
                pr.cp(Sbf[:, dc, :], S32[:, dc, :], [BS32[dc]], [BSb[dc]], eng="act")

        for h in range(NH_):
            WQK, BWQK = wtile_load(k, [(0, w_in[:, OFF_QR + h * 256:OFF_QR + (h + 1) * 256]),
                                        (256, w_in[:, OFF_KR + h * 256:OFF_KR + (h + 1) * 256])])
            WV, BWV = wtile_load(k, [(0, w_in[:, OFF_VR + h * 512:OFF_VR + (h + 1) * 512])])
            WG, BWG = wtile_load(k, [(0, w_in[:, OFF_GR + h * 512:OFF_GR + (h + 1) * 512])])
            dc_ = k.deccol
            for d_ in range(2):
                for r_ in range(4):
                    pr.cp(dqrep[:, d_, r_ * 128:(r_ + 1) * 128], k.dec[:, h, 1 + d_, :], [k.Bdec], [B["dq"]], eng="dve")
            for dc in range(2):
                for kc in range(8):
                    pr.mm(ps[dc][:, 0:256], WQK[:, kc, 256 + dc * 128:256 + (dc + 1) * 128], k.xmT[:, kc, 1 + S:1 + NT],
                          kc == 0, kc == 7, [BWQK, k.Bx[8]], [Bps[dc]])
                pr.act(KcT[:, dc, :], ps[dc][:, 0:256], AF.Copy, [Bps[dc]], [B["KcT"]], scale=1.0 / 16.0)
            ptb = ps[4].bitcast(BF16)
            for j in range(2):
                for dc in range(2):
                    pr.tr(ptb[:, (j * 2 + dc) * 128:(j * 2 + dc + 1) * 128], KcT[:, dc, j * 128:(j + 1) * 128], k.identb[:],
                          [B["KcT"], k.Bc], [Bps[4]])
            for dirn in range(2):
                for j in range(2):
                    pr.ts(Kcw[:, dirn, j, :], ptb[:, j * 256:(j + 1) * 256], dc_[:, h, 4 + 2 * dirn + j:5 + 2 * dirn + j], None,
                          ALU.mult, None, [Bps[4], k.Bdec], [B["Kcw"]])
            for j in range(2):
                for kc in range(8):
                    pr.mm(ps[2], k.xmT[:, kc, 1 + S + j * 128:1 + S + (j + 1) * 128], WV[:, kc, :], kc == 0, kc == 7,
                          [BWV, k.Bx[8]], [Bps[2]])
                pr.cp(Vc[:, j, :], ps[2], [Bps[2]], [B["Vc"]], eng="act")

            def prep_part(sweep, g, part):
                s_ = g % 2
                bs = BS[s_]
                if part == 0:
                    for t_ in range(4):
                        pr.dma(tabR[s_][:, t_, :], din["ropeR"][t_, :, g * 512:(g + 1) * 512], [], [bs["tab"]], q="sp")
                    proj_mm(WQK, BWQK, g, 0, 0)
                    proj_mm(WQK, BWQK, g, 256, 2)
                    if sweep == 1:
                        rope_ops(s_, 0, 0, None, None, scale_rep=dqrep[:, 1, :], dst2=QsT[s_], Bdst2=bs["QsT"])
                    else:
                        rope_ops(s_, 0, 0, QT[s_], bs["QT"], scale_rep=dqrep[:, 0, :], dst2=QsT[s_], Bdst2=bs["QsT"])
                elif part == 1:
                    rope_ops(s_, 2, 2, KT[s_], bs["KT"])
                    v_make(WV, BWV, g, V[s_], bs["V"], (0, 1))
                elif part == 2:
                    v_make(WV, BWV, g, V[s_], bs["V"], (2, 3))
                else:
                    if sweep == 2:
                        v_make(WG, BWG, g, G[s_], bs["G"], (0, 1, 2, 3), silu=True)
                    ktok_make(s_, dc_[:, h, 1:2] if sweep == 1 else dc_[:, h, 0:1])

            init_state(1)
            order = list(range(7, -1, -1))
            for part in range(4):
                prep_part(1, order[0], part)
            for gi, g in enumerate(order):
                s_ = g % 2
                for step, ci in enumerate(range(3, -1, -1)):
                    for dc in range(2):
                        pr.mm(ps[5], QsT[s_][:, dc, ci * 128:(ci + 1) * 128], Sbf[:, dc, :], dc == 0, dc == 1,
                              [BS[s_]["QsT"], BSb[dc]], [Bps[5]])
                    yi = ybctr[0] % 2
                    ybctr[0] += 1
                    pr.cp(ybt[yi][:], ps[5], [Bps[5]], [Bybt[yi]], eng="act")
                    pr.dma(k.ybscr[h, g * 4 + ci], ybt[yi][:], [Bybt[yi]], [Byb[g * 4 + ci]], q="sp")
                    state_update(s_, ci, dc_[:, h, 3:4], alt=6)
                    if gi + 1 < 8:
                        prep_part(1, order[gi + 1], step)
            def yb_load(n2, base, h=h):
                c_ = n2
                yj = (base + n2) % 2
                pr.dma(ybt[yj][:], k.ybscr[h, c_], [Byb[c_]], [Bybt[yj]], q="sp")
            init_state(0)
            order = list(range(8))
            for part in range(4):
                prep_part(2, order[0], part)
            for gi, g in enumerate(order):
                s_ = g % 2
                sg = g % 2
                for step, ci in enumerate(range(4)):
                    n_ = gi * 4 + step
                    if n_ == 0:
                        yb_base = ybctr[0]
                        yb_load(0, yb_base)
                    if n_ + 1 < 32:
                        yb_load(n_ + 1, yb_base)
                    yi = (yb_base + n_) % 2
                    ybctr[0] += 1
                    for dc in range(2):
                        pr.mm(ps[5][:, 0:128], KT[s_][:, dc, ci * 128:(ci + 1) * 128], QT[s_][:, dc, ci * 128:(ci + 1) * 128],
                              dc == 0, dc == 1, [BS[s_]["KT"], BS[s_]["QT"]], [Bps[5]])
                    pr.tt(innb[:], ps[5][:, 0:128], k.dec[:, h, 0, :], ALU.mult, [Bps[5], k.Bdec], [B["innb"]])
                    for dc in range(2):
                        pr.mm(ps[6], QsT[s_][:, dc, ci * 128:(ci + 1) * 128], Sbf[:, dc, :], dc == 0, False,
                              [BS[s_]["QsT"], BSb[dc]], [Bps[6]])
                    def av_(s_=s_, ci=ci):
                        pr.mm(ps[6], innb[:], V[s_][:, ci, :], False, True, [B["innb"], BS[s_]["V"]], [Bps[6]])
                    state_update(s_, ci, dc_[:, h, 2:3], mid=av_)
                    pr.tt(ytot[:], ps[6], ybt[yi][:], ALU.add, [Bps[6], Bybt[yi]], [B["ytot"]])
                    pr.add("dve", lambda e: e.bn_stats(gst[:, 0:6], ytot[:]), [B["ytot"]], [B["gst"]])
                    pr.add("dve", lambda e: e.bn_aggr(gst[:, 6:8], gst[:, 0:6]), [B["gst"]], [B["gst"]])
                    rstd_ops(k, gst[:, 9:10], gst[:, 7:8], gst[:, 8:9], [B["gst"]], [B["gst"]])
                    pr.ts(ytot[:], ytot[:], gst[:, 6:7], gst[:, 9:10], ALU.subtract, ALU.mult, [B["ytot"], B["gst"]], [B["ytot"]])
                    pr.tt(og[:], ytot[:], G[s_][:, ci, :], ALU.mult, [B["ytot"], BS[s_]["G"]], [B["og"]], eng="pool")
                    if gi + 1 < 8:
                        prep_part(2, order[gi + 1], step)
                    ptb = ps[4].bitcast(BF16)
                    for vc in range(4):
                        pr.tr(ptb[:, vc * 128:(vc + 1) * 128], og[:, vc * 128:(vc + 1) * 128], k.identb[:], [B["og"], k.Bc], [Bps[4]])
                    pr.cp(stg[sg][:, :, ci * 128:(ci + 1) * 128], ptb[:, 0:512].rearrange("p (a b) -> p a b", b=128),
                          [Bps[4]], [Bstg[sg]], eng="act")
                pr.dma(k.retpreT[h * 512:(h + 1) * 512, g * 512:(g + 1) * 512].rearrange("(vc p) n -> p vc n", p=128),
                       stg[sg][:], [Bstg[sg]], [], q="sp")
        d = k.dbg_t("retpreT", [2048, S], BF16)
        if d is not None:
            pr.flush()
            for i_ in range(16):
                pr.dma(d[i_ * 128:(i_ + 1) * 128, :], k.retpreT[i_ * 128:(i_ + 1) * 128, :], [], [])
        pr.flush()


def run_steps(k, steps):
    allb = list(k.Bwbf.values())

    def load(specs):
        return [wtile_load(k, sp, q="hw", rd=allb) for sp in specs]
    nxt = load(steps[0][0])
    for i, (specs, fn) in enumerate(steps):
        cur = nxt
        if i + 1 < len(steps):
            nxt = load(steps[i + 1][0])
        fn(cur)


def ln_rows(k, pr, xin, Bxin, gst, Bg, st6, out_ops):
    rows = xin.shape[0]
    for i in range(2):
        pr.add("dve", lambda e, i=i: e.bn_stats(st6[0:rows, i, :], xin[:, i * 512:(i + 1) * 512]), [Bxin], [Bg])
    pr.add("dve", lambda e: e.bn_aggr(gst[0:rows, 0:2], st6[0:rows].rearrange("p a b -> p (a b)")), [Bg], [Bg])
    pr.act(gst[0:rows, 2:3], gst[0:rows, 1:2], AF.Ln, [Bg, k.Bsm], [Bg], bias=k.small[0:rows, 0:1])
    pr.act(gst[0:rows, 3:4], gst[0:rows, 2:3], AF.Exp, [Bg], [Bg], scale=-0.5)


def phaseC1(k):
    pr, nc, din = k.pr, k.nc, k.din
    ps, Bps = k.ps, k.Bps
    with ExitStack() as pes:
        sb = lambda n, s, d: pes.enter_context(nc.sbuf_tensor("sp_" + n, s, d))
        rows = sb("c1rows", [128, 4, 1024], F32)
        Brows = Buf("c1rows")
        rp = sb("c1rp", [128, 16, 512], BF16)
        dp = sb("c1dp", [128, 8, 512], BF16)
        Brp, Bdp = Buf("c1rp"), Buf("c1dp")
        m1 = sb("c1m1", [128, 4, 512], F32)
        Bm1 = Buf("c1m1")
        sg = [sb("c1sg%d" % i, [128, 512], F32) for i in range(2)]
        Bsg = [Buf("c1sg%d" % i) for i in range(2)]
        mg = sb("c1mg", [128, 8, 512], BF16)
        Bmg = Buf("c1mg")
        xt = [sb("c1xt0", [128, 1024], F32)] * 2
        Bxt = [Buf("c1xt0")] * 2
        tmp = sb("c1tmp", [128, 512], F32)
        Btmp = Buf("c1tmp")
        x1 = [sb("c1x10", [128, 1024], F32)] * 2
        Bx1 = [Buf("c1x10")] * 2
        x1nb = sb("c1x1nb", [128, 1024], BF16)
        Bx1nb = Buf("c1x1nb")
        gst = sb("c1gst", [128, 8], F32)
        st6 = sb("c1st6", [128, 2, 6], F32)
        Bg = Buf("c1g")
        g1row = sb("c1g1row", [128, 1024], F32)
        Bg1 = Buf("c1g1row")
        pr.dma(g1row[:], k.growscr[:, 0:1024], [k.Bgs], [Bg1])
        pr.dma(rows[:, 0, :], din["lnin_row"][0:1, :].partition_broadcast(128), [], [Brows])
        pr.dma(rows[:, 1, :], din["lnin_row"][1:2, :].partition_broadcast(128), [], [Brows])
        pr.dma(rows[:, 2, :], din["ln1_row"][0:1, :].partition_broadcast(128), [], [Brows])
        pr.dma(rows[:, 3, :], din["ln1_row"][1:2, :].partition_broadcast(128), [], [Brows])
        pr.ts(rows[:, 0:2, :], rows[:, 0:2, :], ALPHA, None, ALU.mult, None, [Brows], [Brows])
        w_in = din["w_in"]
        sctr = [0]
        steps = []
        for g in range(8):
            def loadpre(tiles, g):
                pr.dma(rp[:], k.retpreT[:, g * 512:(g + 1) * 512].rearrange("(kc p) n -> p kc n", p=128), [], [Brp], q="sp")
                pr.dma(dp[:], k.difpreT[:, g * 512:(g + 1) * 512].rearrange("(kc p) n -> p kc n", p=128), [], [Bdp], q="act")

            def gate_sig(Wg, BWg, oc4, bcol, g):
                i = oc4 % 2
                pb = 2 * i + 1
                for kc in range(8):
                    pr.mm(ps[pb], Wg[:, kc, oc4 * 128:(oc4 + 1) * 128], k.xmT[:, kc, 1 + g * 512:1 + (g + 1) * 512],
                          kc == 0, kc == 7, [BWg, k.Bx[g]], [Bps[pb]])
                pr.act(sg[i][:], ps[pb], AF.Sigmoid, [Bps[pb], k.Bc], [Bsg[i]], bias=bcol)
                return i

            for ch in range(2):
                def stepA(tiles, g=g, ch=ch):
                    (W0, B0), (W1, B1), (Wg, BWg) = tiles
                    if ch == 0:
                        loadpre(tiles, g)
                    for oc4 in range(4):
                        oc = ch * 4 + oc4
                        for kc in range(16):
                            W, BW_ = (W0, B0) if kc < 8 else (W1, B1)
                            pr.mm(ps[2 * (oc4 % 2)], W[:, kc % 8, oc4 * 128:(oc4 + 1) * 128], rp[:, kc, :], kc == 0, kc == 15,
                                  [BW_, Brp], [Bps[2 * (oc4 % 2)]])
                        i = gate_sig(Wg, BWg, oc4, k.bgate[:, oc:oc + 1], g)
                        pr.tt(m1[:, oc4, :], ps[2 * i], sg[i][:], ALU.mult, [Bps[2 * i], Bsg[i]], [Bm1])
                steps.append(([[(0, k.wbf["w_ret_out"][0:1024, ch * 512:(ch + 1) * 512])],
                               [(0, k.wbf["w_ret_out"][1024:2048, ch * 512:(ch + 1) * 512])],
                               [(0, k.wbf["w_gates"][:, ch * 512:(ch + 1) * 512])]], stepA))

                def stepB(tiles, g=g, ch=ch):
                    (Wd, BWd), (Wg, BWg) = tiles
                    for oc4 in range(4):
                        oc = ch * 4 + oc4
                        for kc in range(8):
                            pr.mm(ps[2 * (oc4 % 2)], Wd[:, kc, oc4 * 128:(oc4 + 1) * 128], dp[:, kc, :], kc == 0, kc == 7,
                                  [BWd, Bdp], [Bps[2 * (oc4 % 2)]])
                        i = gate_sig(Wg, BWg, oc4, k.bgate[:, 8 + oc:9 + oc], g)
                        pr.tt(sg[i][:], ps[2 * i], sg[i][:], ALU.mult, [Bps[2 * i], Bsg[i]], [Bsg[i]])
                        pr.tt(mg[:, oc, :], sg[i][:], m1[:, oc4, :], ALU.add, [Bsg[i], Bm1], [Bmg], eng="pool")
                steps.append(([[(0, k.wbf["w_diff_out"][:, ch * 512:(ch + 1) * 512])],
                               [(0, k.wbf["w_gates"][:, 1024 + ch * 512:1024 + (ch + 1) * 512])]], stepB))

            def stepO(tiles, g=g):
                for ts_ in range(4):
                    tt_ = g * 4 + ts_
                    i2 = tt_ % 2
                    pr.dma(xt[i2][:], din["x"][tt_ * 128:(tt_ + 1) * 128, :], [], [Bxt[i2]], q="sp")
                    for chh in range(2):
                        Wo, BWo = tiles[chh]
                        for kc in range(8):
                            pr.mm(ps[4 + chh], mg[:, kc, ts_ * 128:(ts_ + 1) * 128], Wo[:, kc, :], kc == 0, kc == 7,
                                  [Bmg, BWo], [Bps[4 + chh]])
                    st = k.stats[:, tt_, :]
                    X = xt[i2]
                    pr.ts(X[:], X[:], st[:, 0:1], st[:, 3:4], ALU.subtract, ALU.mult, [Bxt[i2], k.Bst[tt_]], [Bxt[i2]])
                    pr.tt(X[:], X[:], rows[:, 0, :], ALU.mult, [Bxt[i2], Brows], [Bxt[i2]])
                    pr.tt(X[:], X[:], rows[:, 1, :], ALU.add, [Bxt[i2], Brows], [Bxt[i2]], eng="pool")
                    for chh in range(2):
                        hs = slice(chh * 512, (chh + 1) * 512)
                        pr.tt(tmp[:], ps[4 + chh], g1row[:, hs], ALU.mult, [Bps[4 + chh], Bg1], [Btmp])
                        pr.tt(X[:, hs], X[:, hs], tmp[:], ALU.add, [Bxt[i2], Btmp], [Bxt[i2]])
                    ln_rows(k, pr, X[:], Bxt[i2], gst, Bg, st6, None)
                    pr.ts(x1nb[:], X[:], gst[:, 0:1], gst[:, 3:4], ALU.subtract, ALU.mult, [Bxt[i2], Bg], [Bx1nb])
                    pr.ts(x1[i2][:], X[:], gst[:, 0:1], gst[:, 3:4], ALU.subtract, ALU.mult, [Bxt[i2], Bg], [Bx1[i2]])
                    pr.tt(x1[i2][:], x1[i2][:], rows[:, 2, :], ALU.mult, [Bx1[i2], Brows], [Bx1[i2]], eng="pool")
                    pr.tt(x1[i2][:], x1[i2][:], rows[:, 3, :], ALU.add, [Bx1[i2], Brows], [Bx1[i2]], eng="pool")
                    pr.dma(k.x1scr[tt_ * 128:(tt_ + 1) * 128, :], x1[i2][:], [Bx1[i2]], [], q="act")
                    ptb = ps[6 + ts_ % 2].bitcast(BF16)
                    Bptb = Bps[6 + ts_ % 2]
                    for kc in range(8):
                        pr.tr(ptb[:, kc * 128:(kc + 1) * 128], x1nb[:, kc * 128:(kc + 1) * 128], k.identb[:], [Bx1nb, k.Bc], [Bptb])
                    for kc in range(8):
                        pr.act(k.xmT[:, kc, 1 + tt_ * 128:1 + (tt_ + 1) * 128], ptb[:, kc * 128:(kc + 1) * 128], AF.Identity,
                               [Bptb, k.Bmod], [k.Bx[g]], scale=k.AB2[:, kc, 0:1], bias=k.AB2[:, kc, 1:2])
            steps.append(([[(0, k.wbf["w_o"][:, 0:512])], [(0, k.wbf["w_o"][:, 512:1024])]], stepO))
        run_steps(k, steps)
        d = k.dbg_t("x1", [S, D])
        if d is not None:
            pr.flush()
            for i_ in range(8):
                pr.dma(d[i_ * 512:(i_ + 1) * 512, :], k.x1scr[i_ * 512:(i_ + 1) * 512, :], [], [])
        pr.flush()


def phaseC2(k):
    pr, nc, din = k.pr, k.nc, k.din
    ps, Bps = k.ps, k.Bps
    with ExitStack() as pes:
        sb = lambda n, s, d: pes.enter_context(nc.sbuf_tensor("sp_" + n, s, d))
        rows = sb("c2rows", [128, 2, 1024], F32)
        Brows = Buf("c2rows")
        xo = sb("c2xo", [128, 4, 1024], F32)
        Bxo = [Buf("c2xo%d" % i) for i in range(4)]
        actT = sb("c2act", [128, 22, 512], BF16)
        Bact = Buf("c2act")
        ut = [sb("c2ut%d" % i, [128, 512], F32) for i in range(4)]
        But = [Buf("c2ut%d" % i) for i in range(4)]
        tmp = sb("c2tmp", [128, 512], F32)
        Btmp = Buf("c2tmp")
        gst = sb("c2gst", [128, 8], F32)
        st6 = sb("c2st6", [128, 2, 6], F32)
        Bg = Buf("c2g")
        g2row = sb("c2g2row", [128, 1024], F32)
        Bg2 = Buf("c2g2row")
        pr.dma(g2row[:], k.growscr[:, 1024:2048], [k.Bgs], [Bg2])
        pr.dma(rows[:, 0, :], din["ln2_row"][0:1, :].partition_broadcast(128), [], [Brows])
        pr.dma(rows[:, 1, :], din["ln2_row"][1:2, :].partition_broadcast(128), [], [Brows])
        pr.memset(k.xmT[:, :, 1 + S:2 + S], 0.0, [k.Bx[8]])
        hT = k.xmT
        Bh = k.Bx + [k.Bxpad]
        groups = [(i * 510, 510) for i in range(8)] + [(4080, 16)]
        steps = []
        fctr = [0]
        for (t0, n) in groups:
            nsub = (n + 127) // 128
            for fg in range(11):
                def stepU(tiles, t0=t0, n=n, fg=fg, nsub=nsub):
                    Wu, BWu = tiles[0]
                    if fg == 0:
                        for ts_ in range(nsub):
                            r_ = min(128, n - ts_ * 128)
                            pr.dma(xo[0:r_, ts_, :], k.x1scr[t0 + ts_ * 128:t0 + ts_ * 128 + r_, :], [], [Bxo[ts_]], q="sp")
                            pr.ts(xo[0:r_, ts_, :], xo[0:r_, ts_, :], ALPHA, None, ALU.mult, None, [Bxo[ts_]], [Bxo[ts_]], eng="pool")
                    for f2 in range(2):
                        fc = fg * 2 + f2
                        i = fctr[0] % 4
                        fctr[0] += 1
                        pu, pg = ps[2 * i], ps[2 * i + 1]
                        Bpu, Bpg = Bps[2 * i], Bps[2 * i + 1]
                        for kc in range(8):
                            pr.mm(pu[:, 0:n + 2], Wu[:, kc, f2 * 128:(f2 + 1) * 128], hT[:, kc, t0:t0 + n + 2], kc == 0, kc == 7,
                                  [BWu] + Bh, [Bpu])
                        for kc in range(8):
                            pr.mm(pg[:, 0:n], Wu[:, kc, 256 + f2 * 128:256 + (f2 + 1) * 128], hT[:, kc, t0 + 1:t0 + 1 + n], kc == 0, kc == 7,
                                  [BWu] + Bh, [Bpg])
                        U = ut[i]
                        pr.act(U[:, 0:n], pu[:, 1:n + 1], AF.Identity, [Bpu, k.Bc], [But[i]], scale=k.convw[:, fc, 1:2], bias=k.convb[:, fc:fc + 1])
                        pr.stt(U[:, 0:n], pu[:, 0:n], k.convw[:, fc, 0:1], U[:, 0:n], ALU.mult, ALU.add, [Bpu, k.Bc, But[i]], [But[i]])
                        pr.stt(U[:, 0:n], pu[:, 2:n + 2], k.convw[:, fc, 2:3], U[:, 0:n], ALU.mult, ALU.add, [Bpu, k.Bc, But[i]], [But[i]])
                        pr.act(U[:, 0:n], U[:, 0:n], AF.Gelu, [But[i]], [But[i]])
                        pr.tt(actT[:, fc, 0:n], U[:, 0:n], pg[:, 0:n], ALU.mult, [But[i], Bpg], [Bact])
                steps.append(([[(0, k.wbf["w_up"][:, fg * 256:(fg + 1) * 256]), (256, k.wbf["w_up"][:, DFF + fg * 256:DFF + (fg + 1) * 256])]], stepU))
            for chh in range(2):
                def stepD(tiles, t0=t0, n=n, chh=chh, nsub=nsub):
                    hs = slice(chh * 512, (chh + 1) * 512)
                    for ts_ in range(nsub):
                        r_ = min(128, n - ts_ * 128)
                        pb = 4 + ts_
                        for fc in range(22):
                            Wd, BWd = tiles[fc // 8]
                            pr.mm(ps[pb][0:r_, :], actT[:, fc, ts_ * 128:ts_ * 128 + r_], Wd[:, fc % 8, :], fc == 0, fc == 21,
                                  [Bact, BWd], [Bps[pb]])
                        pr.tt(tmp[0:r_, :], ps[pb][0:r_, :], g2row[0:r_, hs], ALU.mult, [Bps[pb], Bg2], [Btmp])
                        pr.tt(xo[0:r_, ts_, hs], xo[0:r_, ts_, hs], tmp[0:r_, :], ALU.add, [Bxo[ts_], Btmp], [Bxo[ts_]])
                        if chh == 1:
                            X = xo[0:r_, ts_, :]
                            ln_rows(k, pr, X, Bxo[ts_], gst, Bg, st6, None)
                            pr.ts(X, X, gst[0:r_, 0:1], gst[0:r_, 3:4], ALU.subtract, ALU.mult, [Bxo[ts_], Bg], [Bxo[ts_]])
                            pr.tt(X, X, rows[0:r_, 0, :], ALU.mult, [Bxo[ts_], Brows], [Bxo[ts_]], eng="pool")
                            pr.tt(X, X, rows[0:r_, 1, :], ALU.add, [Bxo[ts_], Brows], [Bxo[ts_]], eng="pool")
                            pr.dma(k.out[t0 + ts_ * 128:t0 + ts_ * 128 + r_, :], X, [Bxo[ts_]], [], q="act")
                steps.append(([[(0, k.wbf["w_down"][0:1024, chh * 512:(chh + 1) * 512])],
                               [(0, k.wbf["w_down"][1024:2048, chh * 512:(chh + 1) * 512])],
                               [(0, k.wbf["w_down"][2048:2816, chh * 512:(chh + 1) * 512])]], stepD))
        run_steps(k, steps)
        pr.flush()


_CACHE = {}


def run(inputs, debug=(), cores=8):
    key = tuple(sorted(debug))
    if key not in _CACHE:
        _CACHE[key] = build_program(debug)
    nc, cst = _CACHE[key]
    in_maps = [host_inputs(inputs, b, cst) for b in range(cores)]
    res = run_bass_kernel_spmd(nc, in_maps, core_ids=list(range(cores)))
    return res.results


def kernel(**inputs):
    results = run(inputs)
    return np.stack([np.asarray(r["out"], dtype=np.float32) for r in results], axis=0)
```

```python
import numpy as np
import ml_dtypes
from contextlib import ExitStack
import concourse.bass as bass
import concourse.mybir as mybir
from concourse.bass_utils import run_bass_kernel_spmd

F32 = mybir.dt.float32
BF16 = mybir.dt.bfloat16
AF = mybir.ActivationFunctionType
ALU = mybir.AluOpType

D = 1024
S = 4096
P = 256
NT = S + P
NIN = 11264
DFF = 2816
EPS = 1e-5
ALPHA = 2.0 ** 0.25
LAM_INIT = 0.2
OFF_QR, OFF_KR, OFF_VR, OFF_GR, OFF_QD, OFF_KD, OFF_VD, OFF_GTR, OFF_GTD = (
    0, 1024, 2048, 4096, 6144, 7168, 8192, 9216, 10240)

SAME_ENG_RAW = True


class Buf:
    __slots__ = ("name", "lw", "rs", "excl")

    def __init__(self, name, excl=False):
        self.name = name
        self.lw = None
        self.rs = []
        self.excl = excl


class Op:
    __slots__ = ("eng", "fn", "reads", "writes", "dma", "key", "pos", "signal", "sigval",
                 "waits", "kref", "extra")

    def __init__(self, eng, fn, reads, writes, dma):
        self.eng = eng
        self.fn = fn
        self.reads = reads
        self.writes = writes
        self.dma = dma
        self.key = None
        self.pos = 0
        self.signal = False
        self.sigval = 0
        self.waits = []
        self.kref = None
        self.extra = None


class Prog:
    ENG = ("pe", "act", "dve", "pool", "sp")

    def __init__(self, nc, es):
        self.nc = nc
        self.eo = dict(pe=nc.tensor, act=nc.scalar, dve=nc.vector, pool=nc.gpsimd, sp=nc.sync)
        self.esem = {e: es.enter_context(nc.semaphore("s_" + e)) for e in self.ENG}
        self.nslots = dict(sp=12, act=6, pool=8)
        self.dsem = {}
        self.slot_cnt = {}
        self.slot_last = {}
        for q, n in self.nslots.items():
            for i in range(n):
                k = ("d", q, i)
                self.dsem[k] = es.enter_context(nc.semaphore("d_%s%d" % (q, i)))
                self.slot_cnt[k] = 0
                self.slot_last[k] = None
        self.slot_rr = {q: 0 for q in self.nslots}
        self.pending = []
        self.known = {e: {} for e in self.ENG}
        self.epos = {e: 0 for e in self.ENG}
        self.esig = {e: 0 for e in self.ENG}
        self.last_real = {e: None for e in self.ENG}
        self.nops = 0

    def add(self, eng, fn, reads=(), writes=(), dma=False):
        op = Op(eng, fn, tuple(reads), tuple(writes), dma)
        self.pending.append(op)
        return op

    def barrier(self):
        marks = []
        for e in self.ENG:
            op = self.add(e, None)
            op.extra = "barrier"
            marks.append(op)
        return marks

    def flush(self, final=False):
        if not final:
            self.barrier()
        ops = self.pending
        self.pending = []
        nc = self.nc
        for X in ops:
            q = X.eng
            deps = {}
            if X.extra == "barrier":
                for e in self.ENG:
                    if self.last_real[e] is not None:
                        deps[self.last_real[e]] = False
                for k, d in self.slot_last.items():
                    if d is not None:
                        deps[d] = False
            else:
                for b in X.reads:
                    if b.lw is not None:
                        deps[b.lw] = True
                    if b.excl:
                        for r in b.rs:
                            deps.setdefault(r, False)
                for b in X.writes:
                    if b.lw is not None:
                        deps.setdefault(b.lw, False)
                    for r in b.rs:
                        deps.setdefault(r, False)
            if X.dma:
                i = self.slot_rr[q]
                self.slot_rr[q] = (i + 1) % self.nslots[q]
                key = ("d", q, i)
                prev = self.slot_last[key]
                if prev is not None:
                    deps.setdefault(prev, False)
                self.slot_cnt[key] += 1
                X.key = key
                X.pos = self.slot_cnt[key]
                self.slot_last[key] = X
            else:
                self.epos[q] += 1
                X.key = q
                X.pos = self.epos[q]
                if X.fn is not None:
                    self.last_real[q] = X
            known = self.known[q]
            need = {}
            for Dp, raw in deps.items():
                if Dp is X:
                    continue
                if (not Dp.dma) and Dp.key == q and X.extra != "barrier":
                    if q == "pe" or not SAME_ENG_RAW:
                        continue
                if known.get(Dp.key, 0) >= Dp.pos:
                    continue
                o = need.get(Dp.key)
                if o is None or o.pos < Dp.pos:
                    need[Dp.key] = Dp
            if need:
                newk = dict(known)
                for k, Dp in need.items():
                    Dp.signal = True
                    X.waits.append(Dp)
                    kr = Dp.kref
                    if kr:
                        for kk, vv in kr.items():
                            if newk.get(kk, 0) < vv:
                                newk[kk] = vv
                    if newk.get(k, 0) < Dp.pos:
                        newk[k] = Dp.pos
                self.known[q] = newk
            X.kref = self.known[q]
            for b in X.reads:
                if b.excl:
                    b.lw = X
                    b.rs = []
                else:
                    b.rs.append(X)
            for b in X.writes:
                b.lw = X
                b.rs = []
        per = {e: [] for e in self.ENG}
        for X in ops:
            if (not X.dma) and X.signal:
                if X.fn is None:
                    raise RuntimeError("signal on empty op")
                self.esig[X.eng] += 1
                X.sigval = self.esig[X.eng]
            per[X.eng].append(X)
        self.nops += len(ops)
        esem, dsem = self.esem, self.dsem

        def emit(eng, lst):
            for X in lst:
                for Dp in X.waits:
                    if Dp.dma:
                        eng.wait_ge(dsem[Dp.key], 16 * Dp.pos)
                    else:
                        eng.wait_ge(esem[Dp.key], Dp.sigval)
                if X.fn is not None:
                    ins = X.fn(eng)
                    if X.dma:
                        ins.then_inc(dsem[X.key], 16)
                    elif X.signal:
                        ins.then_inc(esem[X.eng], 1)

        with nc.Block() as block:
            if per["pe"]:
                @block.tensor
                def _(e):
                    emit(e, per["pe"])
            if per["act"]:
                @block.scalar
                def _(e):
                    emit(e, per["act"])
            if per["dve"]:
                @block.vector
                def _(e):
                    emit(e, per["dve"])
            if per["pool"]:
                @block.gpsimd
                def _(e):
                    emit(e, per["pool"])
            if per["sp"]:
                @block.sync
                def _(e):
                    emit(e, per["sp"])

    def mm(self, out, lhsT, rhs, start, stop, r, w, skip=False):
        if skip:
            return self.add("pe", lambda e: e.matmul(out, lhsT, rhs, start=start, stop=stop, skip_group_check=True), r, w)
        return self.add("pe", lambda e: e.matmul(out, lhsT, rhs, start=start, stop=stop), r, w)

    def tr(self, out, in_, ident, r, w):
        return self.add("pe", lambda e: e.transpose(out, in_, ident), r, w)

    def act(self, out, in_, func, r, w, bias=None, scale=None, accum_out=None, eng="act"):
        kw = {}
        if bias is not None:
            kw["bias"] = bias
        if scale is not None:
            kw["scale"] = scale
        if accum_out is not None:
            kw["accum_out"] = accum_out
        return self.add(eng, lambda e: e.activation(out, in_, func, **kw), r, w)

    def ts(self, out, in0, s1, s2, op0, op1, r, w, eng="dve", accum_out=None):
        if op1 is None:
            return self.add(eng, lambda e: e.tensor_scalar(out, in0, s1, None, op0), r, w)
        if accum_out is not None:
            return self.add(eng, lambda e: e.tensor_scalar(out, in0, s1, s2, op0, op1, accum_out), r, w)
        return self.add(eng, lambda e: e.tensor_scalar(out, in0, s1, s2, op0, op1), r, w)

    def tt(self, out, in0, in1, op, r, w, eng="dve"):
        return self.add(eng, lambda e: e.tensor_tensor(out, in0, in1, op), r, w)

    def stt(self, out, in0, scalar, in1, op0, op1, r, w, eng="dve", accum_out=None):
        if accum_out is not None:
            return self.add(eng, lambda e: e.scalar_tensor_tensor(out, in0, scalar, in1, op0, op1, accum_out), r, w)
        return self.add(eng, lambda e: e.scalar_tensor_tensor(out, in0, scalar, in1, op0, op1), r, w)

    def cp(self, out, in_, r, w, eng="dve"):
        if eng == "act":
            return self.add("act", lambda e: e.copy(out, in_), r, w)
        return self.add(eng, lambda e: e.tensor_copy(out, in_), r, w)

    def memset(self, out, val, w, eng="dve"):
        return self.add(eng, lambda e: e.memset(out, val), (), w)

    def dma(self, out, in_, r, w, q="sp"):
        return self.add(q, lambda e: e.dma_start(out=out, in_=in_), r, w, dma=True)


class K:
    pass


def _consts():
    c = {}
    c["identb"] = np.eye(128, dtype=np.float32).astype(ml_dtypes.bfloat16)
    c["identf"] = np.eye(128, dtype=np.float32)
    pr = np.zeros((128, 128), np.float32)
    for m in range(128):
        b, i = m // 64, m % 64
        if i < 32:
            pr[b * 64 + i + 32, m] = -1.0
        else:
            pr[b * 64 + i - 32, m] = 1.0
    c["permR"] = pr.astype(ml_dtypes.bfloat16)
    k = np.arange(128, dtype=np.float32)[:, None]
    q = np.arange(128, dtype=np.float32)[None, :]
    dec = np.zeros((128, 4, 128), np.float32)
    dec[:, 0] = np.maximum(q - k, 0)
    dec[:, 1] = np.maximum(k - q, 0)
    dec[:, 2] = (q >= k)
    dec[:, 3] = (k >= q)
    c["dectab"] = dec
    j = np.arange(128, dtype=np.float32)
    prow = np.zeros((128, 2, 128), np.float32)
    prow[:, 0, :] = j + 1.0
    prow[:, 1, :] = 128.0 - j
    c["posrow"] = prow
    p = np.arange(128, dtype=np.float32)
    pc = np.stack([127.0 - p, p, 255.0 - p, 127.0 - p, p, 128.0 + p], axis=1)
    c["poscol"] = np.ascontiguousarray(pc.astype(np.float32))
    t = np.arange(S)
    row = (t // 64).astype(np.float64)
    col = (t % 64).astype(np.float64)

    def tabs(head_dim):
        half = head_dim // 2
        inv = 10000.0 ** (-(np.arange(0, half, 2, dtype=np.float64) / half))
        inv = inv.astype(np.float32).astype(np.float64)
        ang = np.concatenate([row[:, None] * inv, col[:, None] * inv], axis=-1)
        ang = ang.astype(np.float32).astype(np.float64)
        return np.cos(ang).astype(np.float32).T, np.sin(ang).astype(np.float32).T

    cr, sr = tabs(256)
    c["ropeR"] = np.ascontiguousarray(np.stack([cr, sr, cr / 16.0, sr / 16.0], 0).astype(np.float32))
    cd, sd = tabs(64)
    cd4 = np.tile(cd, (4, 1))
    sd4 = np.tile(sd, (4, 1))
    c["ropeD"] = np.ascontiguousarray(np.stack([cd4 / 8.0, sd4 / 8.0, cd4, sd4], 0).astype(np.float32))
    return c


CONST_DT = dict(identb=BF16, identf=F32, permR=BF16, dectab=F32, posrow=F32, poscol=F32, ropeR=F32, ropeD=F32)

IN_SHAPES = dict(
    x=[S, D], ctx=[P, D], ccol=[128, 8, 2], lnin_col=[128, 8, 2], ln1_col=[128, 8, 2],
    lnin_row=[2, D], ln1_row=[2, D], ln2_row=[2, D],
    w_mod=[D, 6 * D], bmod_col=[128, 48], bmod_row=[1, 6 * D], w_in=[D, NIN], bgate_col=[128, 16],
    decay_logit=[1, 8], diff_lambda=[1, 256], subln_col=[128, 1],
    w_ret_out=[2048, D], w_diff_out=[D, D], w_o=[D, D], w_up=[D, 2 * DFF],
    convw_col=[128, 22, 3], convb_col=[128, 22], w_down=[DFF, D],
)


def host_inputs(inp, b, consts):
    f = lambda a: np.ascontiguousarray(np.asarray(a, dtype=np.float32))
    col8 = lambda v: f(v).reshape(8, 128).T
    m = {}
    m["x"] = f(inp["x"][b])
    m["ctx"] = f(inp["ctx"][b])
    m["ccol"] = f(np.stack([col8(inp["c"][b]), col8(inp["c_ctx"])], -1))
    m["lnin_col"] = f(np.stack([col8(inp["ln_in_g"]), col8(inp["ln_in_b"])], -1))
    m["ln1_col"] = f(np.stack([col8(inp["ln1_g"][0]), col8(inp["ln1_b"][0])], -1))
    m["lnin_row"] = f(np.stack([inp["ln_in_g"], inp["ln_in_b"]], 0))
    m["ln1_row"] = f(np.stack([inp["ln1_g"][0], inp["ln1_b"][0]], 0))
    m["ln2_row"] = f(np.stack([inp["ln2_g"][0], inp["ln2_b"][0]], 0))
    m["w_mod"] = f(inp["w_mod"][0])
    m["bmod_col"] = f(f(inp["b_mod"][0]).reshape(48, 128).T)
    m["bmod_row"] = f(inp["b_mod"][0]).reshape(1, -1)
    m["w_in"] = f(inp["w_in"][0])
    m["bgate_col"] = f(f(inp["b_gate"][0]).reshape(16, 128).T)
    m["decay_logit"] = f(inp["ret_decay_logit"][0]).reshape(1, 8)
    m["diff_lambda"] = f(inp["diff_lambda"][0]).reshape(1, 256)
    m["subln_col"] = f(inp["diff_subln_g"][0]).reshape(128, 1)
    m["w_ret_out"] = f(inp["w_ret_out"][0])
    m["w_diff_out"] = f(inp["w_diff_out"][0])
    m["w_o"] = f(inp["w_o"][0])
    m["w_up"] = f(inp["w_up"][0])
    m["convw_col"] = f(f(inp["conv_w"][0]).reshape(3, 22, 128).transpose(2, 1, 0))
    m["convb_col"] = f(f(inp["conv_b"][0]).reshape(22, 128).T)
    m["w_down"] = f(inp["w_down"][0])
    m.update(consts)
    return m


def build_program(debug=()):
    nc = bass.Bass("TRN2", target_bir_lowering=False)
    k = K()
    k.nc = nc
    k.debug = set(debug)
    k.din = {}
    for n, shp in IN_SHAPES.items():
        k.din[n] = nc.dram_tensor(n, shp, F32, kind="ExternalInput").ap()
    cst = _consts()
    for n, arr in cst.items():
        k.din[n] = nc.dram_tensor(n, list(arr.shape), CONST_DT[n], kind="ExternalInput").ap()
    k.out = nc.dram_tensor("out", [S, D], F32, kind="ExternalOutput").ap()
    k.dbg = {}

    def dbg_t(name, shape, dt=F32):
        if name in k.debug:
            k.dbg[name] = nc.dram_tensor("dbg_" + name, shape, dt, kind="ExternalOutput").ap()
            return k.dbg[name]
        return None
    k.dbg_t = dbg_t
    k.retpreT = nc.dram_tensor("retpreT", [2048, S], BF16, kind="Internal").ap()
    k.difpreT = nc.dram_tensor("difpreT", [1024, S], BF16, kind="Internal").ap()
    k.ybscr = nc.dram_tensor("ybscr", [4, 32, 128, 512], F32, kind="Internal").ap()
    k.x1scr = nc.dram_tensor("x1scr", [S, D], F32, kind="Internal").ap()
    k.growscr = nc.dram_tensor("growscr", [128, 2048], F32, kind="Internal").ap()
    k.wbf = {}
    k.Bwbf = {}
    for n_, shp in (("w_ret_out", [2048, D]), ("w_diff_out", [D, D]), ("w_o", [D, D]), ("w_up", [D, 2 * DFF]),
                    ("w_down", [DFF, D]), ("w_gates", [D, 2048])):
        k.wbf[n_] = nc.dram_tensor("wbf_" + n_, shp, BF16, kind="Internal").ap()
        k.Bwbf[n_] = Buf("wbf_" + n_)

    with ExitStack() as es:
        pr = Prog(nc, es)
        k.pr = pr
        sb = lambda n, s, d: es.enter_context(nc.sbuf_tensor("sb_" + n, s, d))
        k.xmT = sb("xmT", [128, 8, 1 + NT], BF16)
        k.Bx = [Buf("xm%d" % g) for g in range(9)]
        k.Bxpad = Buf("xmpad")
        k.NW = 6
        k.wsl = [sb("w%d" % i, [128, 8, 512], BF16) for i in range(k.NW)]
        k.Bw = [Buf("w%d" % i) for i in range(k.NW)]
        k.wctr = 0
        k.psall = es.enter_context(nc.psum_tensor("psall", [128, 8, 512], F32))
        k.ps = [k.psall[:, i, :] for i in range(8)]
        k.Bps = [Buf("ps%d" % i, excl=True) for i in range(8)]
        k.identb = sb("identb", [128, 128], BF16)
        k.identf = sb("identf", [128, 128], F32)
        k.permR = sb("permR", [128, 128], BF16)
        k.poscol = sb("poscol", [128, 6], F32)
        k.Bc = Buf("consts")
        k.small = sb("small", [128, 64], F32)
        k.Bsm = Buf("small")
        k.modT = sb("modT", [128, 48, 2], F32)
        k.Bmod = Buf("modT")
        k.AB1 = sb("AB1", [128, 8, 4], F32)
        k.AB2 = sb("AB2", [128, 8, 2], F32)
        k.Bgrow = Buf("grow")
        k.stats = sb("stats", [128, 34, 4], F32)
        k.Bst = [Buf("st%d" % i) for i in range(34)]
        k.dec = sb("dec", [128, 4, 3, 128], F32)
        k.deccol = sb("deccol", [128, 4, 8], F32)
        k.Bdec = Buf("dec")
        k.bgate = sb("bgate", [128, 16], F32)
        k.convw = sb("convw", [128, 22, 3], F32)
        k.convb = sb("convb", [128, 22], F32)
        k.subln = sb("subln", [128, 1], F32)

        phase0(k)
        phaseA(k)
        if "stopA" not in k.debug:
            if "skipB2" not in k.debug:
                phaseB2(k)
            if "stopB2" not in k.debug:
                phaseB1(k)
                if "stopB1" not in k.debug:
                    phaseC1(k)
                    if "stopC1" not in k.debug:
                        phaseC2(k)
        pr.flush(final=False)
    return nc, cst


def wtile_load(k, pieces, q="pool", rd=()):
    pr = k.pr
    s = k.wctr % k.NW
    k.wctr += 1
    for off, src in pieces:
        nk = src.shape[0] // 128
        n = src.shape[1]
        qq = q if q == "pool" else ("sp" if k.wctr % 2 == 0 else "act")
        pr.dma(k.wsl[s][:, 0:nk, off:off + n], src.rearrange("(kc p) c -> p kc c", p=128), list(rd), [k.Bw[s]], q=qq)
    return k.wsl[s], k.Bw[s]


def rstd_ops(k, out_col, var_col, tmp_col, r, w, eng="act"):
    pr = k.pr
    pr.act(tmp_col, var_col, AF.Ln, list(r) + [k.Bsm], w, bias=k.small[:, 0:1])
    pr.act(out_col, tmp_col, AF.Exp, w, w, scale=-0.5)


def phase0(k):
    pr, nc, din = k.pr, k.nc, k.din
    sm = k.small
    with ExitStack() as pes:
        sb = lambda n, s, d: pes.enter_context(nc.sbuf_tensor("sp_" + n, s, d))
        k.grow = sb("grow", [128, 2, 1024], F32)
        k.dectab = sb("dectab", [128, 4, 128], F32)
        k.posrow = sb("posrow", [128, 2, 128], F32)
        pr.dma(k.identb[:], din["identb"], [], [k.Bc])
        pr.dma(k.identf[:], din["identf"], [], [k.Bc])
        pr.dma(k.permR[:], din["permR"], [], [k.Bc])
        pr.dma(k.dectab[:], din["dectab"], [], [k.Bc])
        pr.dma(k.posrow[:], din["posrow"], [], [k.Bc])
        pr.dma(k.poscol[:], din["poscol"], [], [k.Bc])
        pr.dma(k.bgate[:], din["bgate_col"], [], [k.Bc])
        pr.dma(k.convw[:], din["convw_col"], [], [k.Bc])
        pr.dma(k.convb[:], din["convb_col"], [], [k.Bc])
        pr.dma(k.subln[:], din["subln_col"], [], [k.Bc])
        pr.memset(sm[:], 0.0, [k.Bsm])
        pr.memset(sm[:, 0:1], EPS, [k.Bsm])
        pr.memset(k.xmT[:, :, 0:1], 0.0, [k.Bxpad])
        ccol = sb("ccol", [128, 8, 2], F32)
        cond = sb("cond", [128, 8, 2], F32)
        lnin = sb("lnin", [128, 8, 2], F32)
        ln1c = sb("ln1c", [128, 8, 2], F32)
        bmodc = sb("bmodc", [128, 48], F32)
        brow = sb("brow", [128, 2, 1024], F32)
        condrep = sb("condrep", [128, 8, 128], F32)
        ones = sb("ones", [128, 128], F32)
        lamb = sb("lamb", [128, 256], F32)
        tmp = sb("tmp0", [128, 8, 2], F32)
        B0 = Buf("p0")
        Bcr = Buf("condrep")
        pr.dma(ccol[:], din["ccol"], [], [B0])
        pr.dma(lnin[:], din["lnin_col"], [], [B0])
        pr.dma(ln1c[:], din["ln1_col"], [], [B0])
        pr.dma(bmodc[:], din["bmod_col"], [], [B0])
        pr.dma(brow[:, 0, :], din["bmod_row"][:, 2048:3072].partition_broadcast(128), [], [B0])
        pr.dma(brow[:, 1, :], din["bmod_row"][:, 5120:6144].partition_broadcast(128), [], [B0])
        pr.dma(sm[:, 8:16], din["decay_logit"].partition_broadcast(128), [], [k.Bsm])
        pr.dma(lamb[:], din["diff_lambda"].partition_broadcast(128), [], [B0])
        pr.memset(ones[:], 1.0, [B0])
        pr.act(cond[:], ccol[:], AF.Silu, [B0], [B0])
        for kc in range(8):
            pr.ts(condrep[:, kc, :], ones[:], cond[:, kc, 0:1], None, ALU.mult, None, [B0], [Bcr])
        pr.tt(lamb[:, 0:64], lamb[:, 0:64], lamb[:, 64:128], ALU.mult, [B0], [B0])
        pr.tt(lamb[:, 128:192], lamb[:, 128:192], lamb[:, 192:256], ALU.mult, [B0], [B0])
        pr.add("dve", lambda e: e.reduce_sum(sm[:, 24:25], lamb[:, 0:64], mybir.AxisListType.X), [B0], [k.Bsm])
        pr.add("dve", lambda e: e.reduce_sum(sm[:, 25:26], lamb[:, 128:192], mybir.AxisListType.X), [B0], [k.Bsm])
        pr.act(sm[:, 24:26], sm[:, 24:26], AF.Exp, [k.Bsm], [k.Bsm])
        pr.tt(sm[:, 1:2], sm[:, 24:25], sm[:, 25:26], ALU.subtract, [k.Bsm], [k.Bsm])
        pr.ts(sm[:, 1:2], sm[:, 1:2], LAM_INIT, None, ALU.add, None, [k.Bsm], [k.Bsm])
        pr.ts(sm[:, 2:3], sm[:, 1:2], -1.0, None, ALU.mult, None, [k.Bsm], [k.Bsm])
        pr.ts(sm[:, 3:4], k.subln[:, 0:1], 1.0 - LAM_INIT, None, ALU.mult, None, [k.Bc], [k.Bsm])
        pr.act(sm[:, 16:24], sm[:, 8:16], AF.Exp, [k.Bsm], [k.Bsm], scale=-1.0)
        pr.ts(sm[:, 16:24], sm[:, 16:24], 1.0, None, ALU.add, None, [k.Bsm], [k.Bsm])
        pr.act(sm[:, 16:24], sm[:, 16:24], AF.Ln, [k.Bsm], [k.Bsm])
        pr.ts(sm[:, 16:24], sm[:, 16:24], -1.0, None, ALU.mult, None, [k.Bsm], [k.Bsm])
        dt = k.dectab
        t1 = sb("t1", [128, 128], F32)
        t2 = sb("t2", [128, 128], F32)
        Bt = Buf("t12")
        for h in range(4):
            lgf = sm[:, 16 + h:17 + h]
            lgb = sm[:, 20 + h:21 + h]
            pr.act(t1[:], dt[:, 0, :], AF.Exp, [k.Bc, k.Bsm], [Bt], scale=lgf)
            pr.tt(t1[:], t1[:], dt[:, 2, :], ALU.mult, [Bt, k.Bc], [Bt])
            pr.act(t2[:], dt[:, 1, :], AF.Exp, [k.Bc, k.Bsm], [Bt], scale=lgb)
            pr.tt(t2[:], t2[:], dt[:, 3, :], ALU.mult, [Bt, k.Bc], [Bt])
            pr.tt(k.dec[:, h, 0, :], t1[:], t2[:], ALU.add, [Bt], [k.Bdec])
            pr.act(k.dec[:, h, 1, :], k.posrow[:, 0, :], AF.Exp, [k.Bc, k.Bsm], [k.Bdec], scale=lgf)
            pr.act(k.dec[:, h, 2, :], k.posrow[:, 1, :], AF.Exp, [k.Bc, k.Bsm], [k.Bdec], scale=lgb)
            dc = k.deccol
            pc = k.poscol
            pr.act(dc[:, h, 0:1], pc[:, 0:1], AF.Exp, [k.Bc, k.Bsm], [k.Bdec], scale=lgf)
            pr.act(dc[:, h, 1:2], pc[:, 1:2], AF.Exp, [k.Bc, k.Bsm], [k.Bdec], scale=lgb)
            pr.act(dc[:, h, 2:3], lgf, AF.Exp, [k.Bsm], [k.Bdec], scale=128.0)
            pr.act(dc[:, h, 3:4], lgb, AF.Exp, [k.Bsm], [k.Bdec], scale=128.0)
            pr.act(dc[:, h, 4:6], pc[:, 2:4], AF.Exp, [k.Bc, k.Bsm], [k.Bdec], scale=lgf)
            pr.act(dc[:, h, 6:8], pc[:, 4:6], AF.Exp, [k.Bc, k.Bsm], [k.Bdec], scale=lgb)
        wm = [sb("wm%d" % i, [128, 8, 512], F32) for i in range(3)]
        Bwm = [Buf("wm%d" % i) for i in range(3)]
        psc = k.ps[0]
        for j in range(12):
            s = j % 3
            pr.dma(wm[s][:], din["w_mod"][:, j * 512:(j + 1) * 512].rearrange("(kc p) c -> p kc c", p=128),
                   [], [Bwm[s]], q=("sp" if j % 2 == 0 else "act"))
            for o in range(4):
                oc = j * 4 + o
                for kc in range(8):
                    pr.mm(psc[:, oc * 2:oc * 2 + 2], wm[s][:, kc, o * 128:(o + 1) * 128], cond[:, kc, :],
                          kc == 0, kc == 7, [Bwm[s], B0], [k.Bps[0]])
            if j in (4, 5, 10, 11):
                gi, half = (0 if j < 6 else 1), j % 2
                pb = k.ps[1 + (j % 2)]
                for kc in range(8):
                    pr.mm(pb[:], condrep[:, kc, :], wm[s][:, kc, :], kc == 0, kc == 7, [Bwm[s], Bcr], [k.Bps[1 + (j % 2)]])
                pr.tt(k.grow[:, gi, half * 512:(half + 1) * 512], pb[:], brow[:, gi, half * 512:(half + 1) * 512],
                      ALU.add, [k.Bps[1 + (j % 2)], B0], [k.Bgrow])
        pview = psc[:, 0:96].rearrange("p (o s) -> p o s", s=2)
        for s2 in range(2):
            pr.tt(k.modT[:, :, s2], pview[:, :, s2], bmodc[:], ALU.add, [k.Bps[0], B0], [k.Bmod])
        for s2 in range(2):
            pr.ts(tmp[:, :, s2], k.modT[:, 8:16, s2], 1.0, None, ALU.add, None, [k.Bmod], [B0])
            pr.tt(k.AB1[:, :, s2], tmp[:, :, s2], lnin[:, :, 0], ALU.mult, [B0], [k.Bmod])
            pr.tt(k.AB1[:, :, 2 + s2], tmp[:, :, s2], lnin[:, :, 1], ALU.mult, [B0], [k.Bmod])
            pr.tt(k.AB1[:, :, 2 + s2], k.AB1[:, :, 2 + s2], k.modT[:, 0:8, s2], ALU.add, [k.Bmod], [k.Bmod])
        pr.ts(tmp[:, :, 0], k.modT[:, 32:40, 0], 1.0, None, ALU.add, None, [k.Bmod], [B0])
        pr.tt(k.AB2[:, :, 0], tmp[:, :, 0], ln1c[:, :, 0], ALU.mult, [B0], [k.Bmod])
        pr.tt(k.AB2[:, :, 1], tmp[:, :, 0], ln1c[:, :, 1], ALU.mult, [B0], [k.Bmod])
        pr.tt(k.AB2[:, :, 1], k.AB2[:, :, 1], k.modT[:, 24:32, 0], ALU.add, [k.Bmod], [k.Bmod])
        d = k.dbg_t("modT", [128, 96])
        if d is not None:
            pr.dma(d, k.modT[:].rearrange("p o s -> p (o s)"), [k.Bmod], [])
        Bgs = Buf("growscr")
        k.Bgs = Bgs
        pr.dma(k.growscr, k.grow[:].rearrange("p o s -> p (o s)"), [k.Bgrow], [Bgs])
        d = k.dbg_t("grow", [128, 2048])
        if d is not None:
            pr.dma(d, k.grow[:].rearrange("p o s -> p (o s)"), [k.Bgrow], [])
        d = k.dbg_t("small", [128, 64])
        if d is not None:
            pr.dma(d, k.small[:], [k.Bsm], [])
        d = k.dbg_t("dec", [128, 4 * 3 * 128])
        if d is not None:
            pr.dma(d, k.dec[:].rearrange("p a b c -> p (a b c)"), [k.Bdec], [])
        d = k.dbg_t("deccol", [128, 32])
        if d is not None:
            pr.dma(d, k.deccol[:].rearrange("p a b -> p (a b)"), [k.Bdec], [])
        pr.flush()


def precast_weights(k):
    pr, din = k.pr, k.din
    for n_ in ("w_ret_out", "w_diff_out", "w_o", "w_up", "w_down", "w_gates"):
        src = din["w_in"][:, OFF_GTR:OFF_GTR + 2048] if n_ == "w_gates" else din[n_]
        rows = src.shape[0]
        for r0 in range(0, rows, 256):
            r1 = min(rows, r0 + 256)
            pr.dma(k.wbf[n_][r0:r1, :], src[r0:r1, :], [], [k.Bwbf[n_]], q="pool")


def phaseA(k):
    pr, nc, din = k.pr, k.nc, k.din
    precast_weights(k)
    with ExitStack() as pes:
        sb = lambda n, s, d: pes.enter_context(nc.sbuf_tensor("sp_" + n, s, d))
        xin = [sb("xin%d" % i, [128, 1024], F32) for i in range(3)]
        Bxin = [Buf("xin%d" % i) for i in range(3)]
        xnb = [sb("xnb%d" % i, [128, 1024], BF16) for i in range(2)]
        Bxnb = [Buf("xnb%d" % i) for i in range(2)]
        st6 = [sb("st6%d" % i, [128, 2, 6], F32) for i in range(2)]
        Bst6 = [Buf("st6%d" % i) for i in range(2)]
        for tt in range(34):
            lat = tt < 32
            src = din["x"][tt * 128:(tt + 1) * 128, :] if lat else din["ctx"][(tt - 32) * 128:(tt - 31) * 128, :]
            s3, s2 = tt % 3, tt % 2
            pr.dma(xin[s3][:], src, [], [Bxin[s3]], q=("sp" if tt % 2 == 0 else "act"))
            for i in range(2):
                pr.add("dve", lambda e, i=i, s3=s3, s2=s2: e.bn_stats(st6[s2][:, i, :], xin[s3][:, i * 512:(i + 1) * 512]),
                       [Bxin[s3]], [Bst6[s2]])
            stt_ = k.stats[:, tt, :]
            pr.add("dve", lambda e, s2=s2, stt_=stt_: e.bn_aggr(stt_[:, 0:2], st6[s2][:].rearrange("p a b -> p (a b)")),
                   [Bst6[s2]], [k.Bst[tt]])
            rstd_ops(k, stt_[:, 3:4], stt_[:, 1:2], stt_[:, 2:3], [k.Bst[tt]], [k.Bst[tt]])
            pr.ts(xnb[s2][:], xin[s3][:], stt_[:, 0:1], stt_[:, 3:4], ALU.subtract, ALU.mult,
                  [Bxin[s3], k.Bst[tt]], [Bxnb[s2]])
            pb = k.ps[s2].bitcast(BF16)
            for kc in range(8):
                pr.tr(pb[:, kc * 128:(kc + 1) * 128], xnb[s2][:, kc * 128:(kc + 1) * 128], k.identb[:],
                      [Bxnb[s2], k.Bc], [k.Bps[s2]])
            g = tt // 4 if lat else 8
            si = 0 if lat else 1
            for kc in range(8):
                pr.act(k.xmT[:, kc, 1 + tt * 128:1 + (tt + 1) * 128], pb[:, kc * 128:(kc + 1) * 128], AF.Identity,
                       [k.Bps[s2], k.Bmod], [k.Bx[g]], scale=k.AB1[:, kc, si:si + 1], bias=k.AB1[:, kc, 2 + si:3 + si],
                       eng=("act" if kc % 2 == 0 else "act"))
        d = k.dbg_t("xmT", [128, 8 * (1 + NT)], BF16)
        if d is not None:
            pr.dma(d, k.xmT[:].rearrange("p a b -> p (a b)"), k.Bx + [k.Bxpad], [])
        d = k.dbg_t("stats", [128, 34 * 4])
        if d is not None:
            pr.dma(d, k.stats[:].rearrange("p a b -> p (a b)"), k.Bst, [])
        pr.flush()


def phaseB2(k):
    pr, nc, din = k.pr, k.nc, k.din
    sm = k.small
    with ExitStack() as pes:
        sb = lambda n, s, d: pes.enter_context(nc.sbuf_tensor("sp_" + n, s, d))
        QT = [sb("dQT%d" % i, [128, S], BF16) for i in range(2)]
        KT = [sb("dKT%d" % i, [128, NT], BF16) for i in range(2)]
        Vg = [sb("dV%d" % i, [128, 34, 128], BF16) for i in range(2)]
        BQ = [[Buf("dq%d_%d" % (i, g)) for g in range(8)] for i in range(2)]
        BK = [[Buf("dk%d_%d" % (i, g)) for g in range(9)] for i in range(2)]
        BV = [[Buf("dv%d_%d" % (i, g)) for g in range(9)] for i in range(2)]
        tab = sb("dtab", [128, 4, 512], F32)
        Btab = Buf("dtab")
        qbf = [sb("dqbf%d" % i, [128, 512], BF16) for i in range(2)]
        Bqbf = [Buf("dqbf%d" % i) for i in range(2)]
        t1 = [sb("dt1%d" % i, [128, 512], F32) for i in range(2)]
        t2 = [sb("dt2%d" % i, [128, 512], F32) for i in range(2)]
        Bt1 = [Buf("dt1%d" % i) for i in range(2)]
        Bt2 = [Buf("dt2%d" % i) for i in range(2)]
        Pt = [sb("dP%d" % i, [128, 512], BF16) for i in range(4)]
        BP = [Buf("dP%d" % i) for i in range(4)]
        o32 = [sb("do%d" % i, [128, 256], F32) for i in range(2)]
        Bo = [Buf("do%d" % i) for i in range(2)]
        Osb = sb("dOsb", [128, 512], F32)
        BOsb = Buf("dOsb")
        rr = sb("drr", [128, 512], F32)
        Brr = Buf("drr")
        rs = sb("drs", [128, 256], F32)
        Brs = Buf("drs")
        onesf = sb("donesf", [128, 128], F32)
        Bones = Buf("dones")
        pr.memset(onesf[:], 1.0, [Bones])
        onesb = sb("donesb", [128, 128], BF16)
        pr.memset(onesb[:], 1.0, [Bones])
        stg = [sb("dstg%d" % i, [128, 512], BF16) for i in range(2)]
        Bstg = [Buf("dstg%d" % i) for i in range(2)]
        w_in = din["w_in"]
        NH_ = 1 if "nh1" in k.debug else 8
        NQC_ = 2 if "nqc2" in k.debug else 16
        lamneg = sm[:, 2:3]
        ctr = [0]

        def wload(h):
            return wtile_load(k, [(0, w_in[:, OFF_QD + h * 128:OFF_QD + (h + 1) * 128]),
                                  (128, w_in[:, OFF_KD + h * 128:OFF_KD + (h + 1) * 128]),
                                  (256, w_in[:, OFF_VD + h * 128:OFF_VD + (h + 1) * 128])])

        def proj_units(h, s_, W, BW):
            units = []
            pb, Bpb = k.ps[5], k.Bps[5]
            pr2, Bpr2 = k.ps[7], k.Bps[7]
            for g in range(8):
                for which in range(2):
                    i = (g * 2 + which) % 2

                    def st0(g=g, which=which):
                        if which == 0:
                            for t_ in range(4):
                                pr.dma(tab[:, t_, :], din["ropeD"][t_, :, g * 512:(g + 1) * 512], [], [Btab], q="sp")
                        for kc in range(8):
                            pr.mm(pb, W[:, kc, which * 128:(which + 1) * 128], k.xmT[:, kc, 1 + g * 512:1 + (g + 1) * 512],
                                  kc == 0, kc == 7, [BW, k.Bx[g]], [Bpb])

                    def st1(i=i):
                        pr.cp(qbf[i][:], pb, [Bpb], [Bqbf[i]], eng="act")

                    def st2(i=i):
                        pr.mm(pr2, k.permR[:], qbf[i][:], True, True, [k.Bc, Bqbf[i]], [Bpr2])

                    def st3(i=i, which=which):
                        pr.tt(t1[i][:], pb, tab[:, 2 * which, :], ALU.mult, [Bpb, Btab], [Bt1[i]])
                        pr.tt(t2[i][:], pr2, tab[:, 2 * which + 1, :], ALU.mult, [Bpr2, Btab], [Bt2[i]])

                    def st4(i=i, g=g, which=which):
                        dst = (QT[s_] if which == 0 else KT[s_])[:, g * 512:(g + 1) * 512]
                        pr.tt(dst, t1[i][:], t2[i][:], ALU.add, [Bt1[i], Bt2[i]], [(BQ[s_] if which == 0 else BK[s_])[g]], eng="pool")
                    units.append([st0, st1, st2, st3, st4])

            def c0():
                for kc in range(8):
                    pr.mm(pb[:, 0:256], W[:, kc, 128:256], k.xmT[:, kc, 1 + S:1 + NT], kc == 0, kc == 7, [BW, k.Bx[8]], [Bpb])

            def c1():
                pr.cp(KT[s_][:, S:NT], pb[:, 0:256], [Bpb], [BK[s_][8]], eng="act")
            units.append([c0, c1])
            for g in range(9):
                def v0(g=g):
                    nch = 4 if g < 8 else 2
                    for j in range(nch):
                        tc = g * 4 + j
                        for kc in range(8):
                            pr.mm(pb[:, j * 128:(j + 1) * 128], k.xmT[:, kc, 1 + tc * 128:1 + (tc + 1) * 128], W[:, kc, 256:384],
                                  kc == 0, kc == 7, [BW, k.Bx[g]], [Bpb])

                def v1(g=g):
                    nch = 4 if g < 8 else 2
                    pr.cp(Vg[s_][:, g * 4:g * 4 + nch, :], pb[:, 0:nch * 128].rearrange("p (a b) -> p a b", b=128),
                          [Bpb], [BV[s_][g]], eng="dve")
                units.append([v0, v1])
            return units

        Wn = wload(0)
        for u in proj_units(0, 0, Wn[0], Wn[1]):
            for st in u:
                st()
        for h in range(NH_):
            s_ = h % 2
            pend = []
            if h + 1 < NH_:
                Wn = wload(h + 1)
                pend = proj_units(h + 1, (h + 1) % 2, Wn[0], Wn[1])
            if "noattn" in k.debug:
                for u in pend:
                    for st in u:
                        st()
                continue
            sched = {}
            its = [(qc, kc) for qc in range(NQC_) for kc in range(34)]
            every = max(15, (len(its) - 40) // max(1, len(pend))) if pend else 0

            def emitS(i):
                qc, kc = its[i]
                sset = 2 * (i % 2)
                gk = min(kc // 4, 8)
                for c in range(2):
                    pr.mm(k.ps[sset + c][:, 0:256], KT[s_][c * 64:(c + 1) * 64, kc * 128:(kc + 1) * 128],
                          QT[s_][c * 64:(c + 1) * 64, qc * 256:(qc + 1) * 256], True, True,
                          [BK[s_][gk], BQ[s_][qc // 2]], [k.Bps[sset + c]])

            def epi0(qc):
                pr.cp(Osb[:], k.ps[4], [k.Bps[4]], [BOsb], eng="act")
                pr.add("dve", lambda e_: e_.reciprocal(rr[:], k.ps[6]), [k.Bps[6]], [Brr])

            def epiA(qc):
                pr.ts(rr[:, 256:512], rr[:, 256:512], lamneg, None, ALU.mult, None, [Brr, k.Bsm], [Brr])
                pr.tt(o32[0][:], Osb[:, 0:256], rr[:, 0:256], ALU.mult, [BOsb, Brr], [Bo[0]])
                pr.tt(o32[1][:], Osb[:, 256:512], rr[:, 256:512], ALU.mult, [BOsb, Brr], [Bo[1]])
                pr.tt(o32[0][:], o32[0][:], o32[1][:], ALU.add, [Bo[0], Bo[1]], [Bo[0]], eng="pool")
                pr.tt(o32[1][:], o32[0][:], o32[0][:], ALU.mult, [Bo[0]], [Bo[1]], eng="pool")

            def epiB(qc, h=h):
                pr.mm(k.ps[0][:, 256:512], onesf[:], o32[1][:], True, True, [Bones, Bo[1]], [k.Bps[0]])
                pr.act(rs[:], k.ps[0][:, 256:512], AF.Ln, [k.Bps[0], k.Bsm], [Brs], scale=1.0 / 128.0, bias=sm[:, 0:1])
                pr.act(rs[:], rs[:], AF.Exp, [Brs], [Brs], scale=-0.5)
                si = (qc // 2) % 2
                pr.stt(stg[si][:, (qc % 2) * 256:(qc % 2 + 1) * 256], o32[0][:], sm[:, 3:4], rs[:], ALU.mult, ALU.mult,
                       [Bo[0], Brs, k.Bsm], [Bstg[si]])
                if qc % 2 == 1:
                    pr.dma(k.difpreT[h * 128:(h + 1) * 128, (qc - 1) * 256:(qc + 1) * 256], stg[si][:], [Bstg[si]], [], q="sp")

            emitS(0)
            dsum = [None]
            for i, (qc, kc) in enumerate(its):
                sset = 2 * (i % 2)
                gk = min(kc // 4, 8)
                if i + 1 < len(its):
                    emitS(i + 1)
                if dsum[0] is not None:
                    dsum[0]()
                    dsum[0] = None
                p_ = i % 4
                pr.act(Pt[p_][:].rearrange("p (c q) -> p c q", c=2), k.psall[:, sset:sset + 2, 0:256], AF.Exp,
                       [k.Bps[sset], k.Bps[sset + 1]], [BP[p_]])

                def sums_(p_=p_, kc=kc, gk=gk):
                    pr.mm(k.ps[4], Vg[s_][:, kc, :], Pt[p_][:], kc == 0, kc == 33, [BP[p_], BV[s_][gk]], [k.Bps[4]])
                    pr.mm(k.ps[6], onesb[:], Pt[p_][:], kc == 0, kc == 33, [BP[p_], Bones], [k.Bps[6]])
                if kc == 33:
                    sums_()
                else:
                    dsum[0] = sums_
                if kc == 33:
                    epi0(qc)
                if qc > 0 and kc == 1:
                    epiA(qc - 1)
                if qc > 0 and kc == 6:
                    epiB(qc - 1)
                if pend and i >= 12 and (i - 12) % every == 0:
                    for j_, st in enumerate(pend.pop(0)):
                        sched.setdefault(i + 3 * j_, []).append(st)
                for st in sched.pop(i, []):
                    st()
            epiA(NQC_ - 1)
            epiB(NQC_ - 1)
            for i_ in sorted(sched):
                for st in sched[i_]:
                    st()
            for u in pend:
                for st in u:
                    st()
        d = k.dbg_t("difpreT", [1024, S], BF16)
        if d is not None:
            pr.flush()
            for i_ in range(8):
                pr.dma(d[i_ * 128:(i_ + 1) * 128, :], k.difpreT[i_ * 128:(i_ + 1) * 128, :], [], [])
        pr.flush()


def phaseB1(k):
    pr, nc, din = k.pr, k.nc, k.din
    sm = k.small
    with ExitStack() as pes:
        sb = lambda n, s, d: pes.enter_context(nc.sbuf_tensor("sp_" + n, s, d))
        NS = 2
        QT = [sb("rQT%d" % i, [128, 2, 512], BF16) for i in range(NS)]
        QsT = [sb("rQsT%d" % i, [128, 2, 512], BF16) for i in range(NS)]
        KT = [sb("rKT%d" % i, [128, 2, 512], BF16) for i in range(NS)]
        Ktok = [sb("rKtok%d" % i, [128, 4, 256], BF16) for i in range(NS)]
        V = [sb("rV%d" % i, [128, 4, 512], BF16) for i in range(NS)]
        G = [sb("rG%d" % i, [128, 4, 512], BF16) for i in range(NS)]
        tabR = [sb("rtab0", [128, 4, 512], F32)] * NS
        BS = [{n: Buf("r%s%d" % (n, i)) for n in "QT QsT KT Ktok V G".split()} for i in range(NS)]
        _bt = Buf("rtab")
        for _b in BS:
            _b["tab"] = _bt
        ra = [sb("rr%d" % i, [128, 512], F32) for i in range(4)]
        dqrep = sb("rdq", [128, 2, 512], F32)
        S32 = sb("rS32", [128, 2, 512], F32)
        Sbf = sb("rSbf", [128, 2, 512], BF16)
        ybt = [sb("rybt%d" % i, [128, 512], F32) for i in range(2)]
        ytot = sb("rytot", [128, 512], F32)
        og = sb("rog", [128, 512], BF16)
        innb = sb("rinnb", [128, 128], BF16)
        stg = [sb("rstg0", [128, 4, 512], BF16)] * 2
        KcT = sb("rKcT", [128, 2, 256], BF16)
        Kcw = sb("rKcw", [128, 2, 2, 256], BF16)
        Vc = sb("rVc", [128, 2, 512], BF16)
        gst = sb("rgst", [128, 16], F32)
        B = {n: Buf("r" + n) for n in "dq S32 Sbf ytot og innb KcT Kcw Vc gst".split()}
        BSb = [Buf("rSbf0"), Buf("rSbf1")]
        BS32 = [Buf("rS32_0"), Buf("rS32_1")]
        Bra = [Buf("rra%d" % i) for i in range(4)]
        Bybt = [Buf("rybt%d" % i) for i in range(2)]
        Bstg = [Buf("rstg0")] * 2
        Byb = [Buf("rybscr%d" % i) for i in range(32)]
        w_in = din["w_in"]
        ps, Bps = k.ps, k.Bps
        NH_ = 1 if "nh1" in k.debug else 4
        ybctr = [0]

        def proj_mm(W, BW, g, coff, b0):
            for dc in range(2):
                for kc in range(8):
                    pr.mm(ps[b0 + dc], W[:, kc, coff + dc * 128:coff + (dc + 1) * 128], k.xmT[:, kc, 1 + g * 512:1 + (g + 1) * 512],
                          kc == 0, kc == 7, [BW, k.Bx[g]], [Bps[b0 + dc]])

        def rope_ops(s_, b0, tq, dst, Bdst, scale_rep=None, dst2=None, Bdst2=None):
            tb = tabR[s_]
            Bt = BS[s_]["tab"]
            cs, sn = tb[:, tq, :], tb[:, tq + 1, :]
            pr.tt(ra[0][:], ps[b0], cs, ALU.mult, [Bps[b0], Bt], [Bra[0]])
            pr.tt(ra[3][:], ps[b0], sn, ALU.mult, [Bps[b0], Bt], [Bra[3]])
            pr.tt(ra[1][:], ps[b0 + 1], sn, ALU.mult, [Bps[b0 + 1], Bt], [Bra[1]])
            pr.tt(ra[2][:], ps[b0 + 1], cs, ALU.mult, [Bps[b0 + 1], Bt], [Bra[2]])
            if dst2 is None:
                pr.tt(dst[:, 0, :], ra[0][:], ra[1][:], ALU.subtract, [Bra[0], Bra[1]], [Bdst], eng="pool")
                pr.tt(dst[:, 1, :], ra[2][:], ra[3][:], ALU.add, [Bra[2], Bra[3]], [Bdst], eng="pool")
            else:
                pr.tt(ra[0][:], ra[0][:], ra[1][:], ALU.subtract, [Bra[0], Bra[1]], [Bra[0]], eng="pool")
                pr.tt(ra[2][:], ra[2][:], ra[3][:], ALU.add, [Bra[2], Bra[3]], [Bra[2]], eng="pool")
                if dst is not None:
                    pr.cp(dst[:, 0, :], ra[0][:], [Bra[0]], [Bdst], eng="act")
                    pr.cp(dst[:, 1, :], ra[2][:], [Bra[2]], [Bdst], eng="act")
                pr.tt(dst2[:, 0, :], ra[0][:], scale_rep, ALU.mult, [Bra[0], B["dq"]], [Bdst2])
                pr.tt(dst2[:, 1, :], ra[2][:], scale_rep, ALU.mult, [Bra[2], B["dq"]], [Bdst2])

        def ktok_make(s_, dkcol):
            ptb = ps[4].bitcast(BF16)
            for ci in range(4):
                for dc in range(2):
                    pr.tr(ptb[:, (ci * 2 + dc) * 128:(ci * 2 + dc + 1) * 128], KT[s_][:, dc, ci * 128:(ci + 1) * 128], k.identb[:],
                          [BS[s_]["KT"], k.Bc], [Bps[4]])
            pr.act(Ktok[s_][:].rearrange("p a b -> p (a b)"), ptb, AF.Copy, [Bps[4], k.Bdec], [BS[s_]["Ktok"]], scale=dkcol)

        def v_make(W, BW, g, dst, Bdst, cis, silu=False):
            for ci in cis:
                pb = ci % 2
                tc = g * 4 + ci
                for kc in range(8):
                    pr.mm(ps[pb], k.xmT[:, kc, 1 + tc * 128:1 + (tc + 1) * 128], W[:, kc, :], kc == 0, kc == 7,
                          [BW, k.Bx[g]], [Bps[pb]])
                if silu:
                    pr.act(dst[:, ci, :], ps[pb], AF.Silu, [Bps[pb]], [Bdst])
                else:
                    pr.cp(dst[:, ci, :], ps[pb], [Bps[pb]], [Bdst], eng="act")

        def state_update(s_, ci, dscol, alt=7, mid=None):
            banks = (7, alt)

            def mm_(dc):
                pr.mm(ps[banks[dc]], Ktok[s_][:, ci, dc * 128:(dc + 1) * 128], V[s_][:, ci, :], True, True,
                      [BS[s_]["Ktok"], BS[s_]["V"]], [Bps[banks[dc]]])

            def upd_(dc):
                pr.stt(S32[:, dc, :], S32[:, dc, :], dscol, ps[banks[dc]], ALU.mult, ALU.add,
                       [BS32[dc], Bps[banks[dc]], k.Bdec], [BS32[dc]])
                pr.cp(Sbf[:, dc, :], S32[:, dc, :], [BS32[dc]], [BSb[dc]], eng="act")
            if alt != 7:
                mm_(0)
                mm_(1)
                upd_(0)
                upd_(1)
            else:
                mm_(0)
                if mid is not None:
                    mid()
                upd_(0)
                mm_(1)
                upd_(1)

        def init_state(dirn):
            for dc in range(2):
                for j in range(2):
                    pr.mm(ps[7], Kcw[:, dirn, j, dc * 128:(dc + 1) * 128], Vc[:, j, :], j == 0, j == 1, [B["Kcw"], B["Vc"]], [Bps[7]])
                pr.cp(S32[:, dc, :], ps[7], [Bps[7]], [BS32[dc]], eng="dve")
                pr.cp(Sbf[:, dc, :], S32[:, dc, :], [BS32[dc]], [BSb[dc]], eng="act")

        for h in range(NH_):
            WQK, BWQK = wtile_load(k, [(0, w_in[:, OFF_QR + h * 256:OFF_QR + (h + 1) * 256]),
                                        (256, w_in[:, OFF_KR + h * 256:OFF_KR + (h + 1) * 256])])
            WV, BWV = wtile_load(k, [(0, w_in[:, OFF_VR + h * 512:OFF_VR + (h + 1) * 512])])
            WG, BWG = wtile_load(k, [(0, w_in[:, OFF_GR + h * 512:OFF_GR + (h + 1) * 512])])
            dc_ = k.deccol
            for d_ in range(2):
                for r_ in range(4):
                    pr.cp(dqrep[:, d_, r_ * 128:(r_ + 1) * 128], k.dec[:, h, 1 + d_, :], [k.Bdec], [B["dq"]], eng="dve")
            for dc in range(2):
                for kc in range(8):
                    pr.mm(ps[dc][:, 0:256], WQK[:, kc, 256 + dc * 128:256 + (dc + 1) * 128], k.xmT[:, kc, 1 + S:1 + NT],
                          kc == 0, kc == 7, [BWQK, k.Bx[8]], [Bps[dc]])
                pr.act(KcT[:, dc, :], ps[dc][:, 0:256], AF.Copy, [Bps[dc]], [B["KcT"]], scale=1.0 / 16.0)
            ptb = ps[4].bitcast(BF16)
            for j in range(2):
                for dc in range(2):
                    pr.tr(ptb[:, (j * 2 + dc) * 128:(j * 2 + dc + 1) * 128], KcT[:, dc, j * 128:(j + 1) * 128], k.identb[:],
                          [B["KcT"], k.Bc], [Bps[4]])
            for dirn in range(2):
                for j in range(2):
                    pr.ts(Kcw[:, dirn, j, :], ptb[:, j * 256:(j + 1) * 256], dc_[:, h, 4 + 2 * dirn + j:5 + 2 * dirn + j], None,
                          ALU.mult, None, [Bps[4], k.Bdec], [B["Kcw"]])
            for j in range(2):
                for kc in range(8):
                    pr.mm(ps[2], k.xmT[:, kc, 1 + S + j * 128:1 + S + (j + 1) * 128], WV[:, kc, :], kc == 0, kc == 7,
                          [BWV, k.Bx[8]], [Bps[2]])
                pr.cp(Vc[:, j, :], ps[2], [Bps[2]], [B["Vc"]], eng="act")

            def prep_part(sweep, g, part):
                s_ = g % 2
                bs = BS[s_]
                if part == 0:
                    for t_ in range(4):
                        pr.dma(tabR[s_][:, t_, :], din["ropeR"][t_, :, g * 512:(g + 1) * 512], [], [bs["tab"]], q="sp")
                    proj_mm(WQK, BWQK, g, 0, 0)
                    proj_mm(WQK, BWQK, g, 256, 2)
                    if sweep == 1:
                        rope_ops(s_, 0, 0, None, None, scale_rep=dqrep[:, 1, :], dst2=QsT[s_], Bdst2=bs["QsT"])
                    else:
                        rope_ops(s_, 0, 0, QT[s_], bs["QT"], scale_rep=dqrep[:, 0, :], dst2=QsT[s_], Bdst2=bs["QsT"])
                elif part == 1:
                    rope_ops(s_, 2, 2, KT[s_], bs["KT"])
                    v_make(WV, BWV, g, V[s_], bs["V"], (0, 1))
                elif part == 2:
                    v_make(WV, BWV, g, V[s_], bs["V"], (2, 3))
                else:
                    if sweep == 2:
                        v_make(WG, BWG, g, G[s_], bs["G"], (0, 1, 2, 3), silu=True)
                    ktok_make(s_, dc_[:, h, 1:2] if sweep == 1 else dc_[:, h, 0:1])

            init_state(1)
            order = list(range(7, -1, -1))
            for part in range(4):
                prep_part(1, order[0], part)
            for gi, g in enumerate(order):
                s_ = g % 2
                for step, ci in enumerate(range(3, -1, -1)):
                    for dc in range(2):
                        pr.mm(ps[5], QsT[s_][:, dc, ci * 128:(ci + 1) * 128], Sbf[:, dc, :], dc == 0, dc == 1,
                              [BS[s_]["QsT"], BSb[dc]], [Bps[5]])
                    yi = ybctr[0] % 2
                    ybctr[0] += 1
                    pr.cp(ybt[yi][:], ps[5], [Bps[5]], [Bybt[yi]], eng="act")
                    pr.dma(k.ybscr[h, g * 4 + ci], ybt[yi][:], [Bybt[yi]], [Byb[g * 4 + ci]], q="sp")
                    state_update(s_, ci, dc_[:, h, 3:4], alt=6)
                    if gi + 1 < 8:
                        prep_part(1, order[gi + 1], step)
            def yb_load(n2, base, h=h):
                c_ = n2
                yj = (base + n2) % 2
                pr.dma(ybt[yj][:], k.ybscr[h, c_], [Byb[c_]], [Bybt[yj]], q="sp")
            init_state(0)
            order = list(range(8))
            for part in range(4):
                prep_part(2, order[0], part)
            for gi, g in enumerate(order):
                s_ = g % 2
                sg = g % 2
                for step, ci in enumerate(range(4)):
                    n_ = gi * 4 + step
                    if n_ == 0:
                        yb_base = ybctr[0]
                        yb_load(0, yb_base)
                    if n_ + 1 < 32:
                        yb_load(n_ + 1, yb_base)
                    yi = (yb_base + n_) % 2
                    ybctr[0] += 1
                    for dc in range(2):
                        pr.mm(ps[5][:, 0:128], KT[s_][:, dc, ci * 128:(ci + 1) * 128], QT[s_][:, dc, ci * 128:(ci + 1) * 128],
                              dc == 0, dc == 1, [BS[s_]["KT"], BS[s_]["QT"]], [Bps[5]])
                    pr.tt(innb[:], ps[5][:, 0:128], k.dec[:, h, 0, :], ALU.mult, [Bps[5], k.Bdec], [B["innb"]])
                    for dc in range(2):
                        pr.mm(ps[6], QsT[s_][:, dc, ci * 128:(ci + 1) * 128], Sbf[:, dc, :], dc == 0, False,
                              [BS[s_]["QsT"], BSb[dc]], [Bps[6]])
                    def av_(s_=s_, ci=ci):
                        pr.mm(ps[6], innb[:], V[s_][:, ci, :], False, True, [B["innb"], BS[s_]["V"]], [Bps[6]])
                    state_update(s_, ci, dc_[:, h, 2:3], mid=av_)
                    pr.tt(ytot[:], ps[6], ybt[yi][:], ALU.add, [Bps[6], Bybt[yi]], [B["ytot"]])
                    pr.add("dve", lambda e: e.bn_stats(gst[:, 0:6], ytot[:]), [B["ytot"]], [B["gst"]])
                    pr.add("dve", lambda e: e.bn_aggr(gst[:, 6:8], gst[:, 0:6]), [B["gst"]], [B["gst"]])
                    rstd_ops(k, gst[:, 9:10], gst[:, 7:8], gst[:, 8:9], [B["gst"]], [B["gst"]])
                    pr.ts(ytot[:], ytot[:], gst[:, 6:7], gst[:, 9:10], ALU.subtract, ALU.mult, [B["ytot"], B["gst"]], [B["ytot"]])
                    pr.tt(og[:], ytot[:], G[s_][:, ci, :], ALU.mult, [B["ytot"], BS[s_]["G"]], [B["og"]], eng="pool")
                    if gi + 1 < 8:
                        prep_part(2, order[gi + 1], step)
                    ptb = ps[4].bitcast(BF16)
                    for vc in range(4):
                        pr.tr(ptb[:, vc * 128:(vc + 1) * 128], og[:, vc * 128:(vc + 1) * 128], k.identb[:], [B["og"], k.Bc], [Bps[4]])
                    pr.cp(stg[sg][:, :, ci * 128:(ci + 1) * 128], ptb[:, 0:512].rearrange("p (a b) -> p a b", b=128),
                          [Bps[4]], [Bstg[sg]], eng="act")
                pr.dma(k.retpreT[h * 512:(h + 1) * 512, g * 512:(g + 1) * 512].rearrange("(vc p) n -> p vc n", p=128),
                       stg[sg][:], [Bstg[sg]], [], q="sp")
        d = k.dbg_t("retpreT", [2048, S], BF16)
        if d is not None:
            pr.flush()
            for i_ in range(16):
                pr.dma(d[i_ * 128:(i_ + 1) * 128, :], k.retpreT[i_ * 128:(i_ + 1) * 128, :], [], [])
        pr.flush()


def run_steps(k, steps):
    allb = list(k.Bwbf.values())

    def load(specs):
        return [wtile_load(k, sp, q="hw", rd=allb) for sp in specs]
    nxt = load(steps[0][0])
    for i, (specs, fn) in enumerate(steps):
        cur = nxt
        if i + 1 < len(steps):
            nxt = load(steps[i + 1][0])
        fn(cur)


def ln_rows(k, pr, xin, Bxin, gst, Bg, st6, out_ops):
    rows = xin.shape[0]
    for i in range(2):
        pr.add("dve", lambda e, i=i: e.bn_stats(st6[0:rows, i, :], xin[:, i * 512:(i + 1) * 512]), [Bxin], [Bg])
    pr.add("dve", lambda e: e.bn_aggr(gst[0:rows, 0:2], st6[0:rows].rearrange("p a b -> p (a b)")), [Bg], [Bg])
    pr.act(gst[0:rows, 2:3], gst[0:rows, 1:2], AF.Ln, [Bg, k.Bsm], [Bg], bias=k.small[0:rows, 0:1])
    pr.act(gst[0:rows, 3:4], gst[0:rows, 2:3], AF.Exp, [Bg], [Bg], scale=-0.5)


def phaseC1(k):
    pr, nc, din = k.pr, k.nc, k.din
    ps, Bps = k.ps, k.Bps
    with ExitStack() as pes:
        sb = lambda n, s, d: pes.enter_context(nc.sbuf_tensor("sp_" + n, s, d))
        rows = sb("c1rows", [128, 4, 1024], F32)
        Brows = Buf("c1rows")
        rp = sb("c1rp", [128, 16, 512], BF16)
        dp = sb("c1dp", [128, 8, 512], BF16)
        Brp, Bdp = Buf("c1rp"), Buf("c1dp")
        m1 = sb("c1m1", [128, 4, 512], F32)
        Bm1 = Buf("c1m1")
        sg = [sb("c1sg%d" % i, [128, 512], F32) for i in range(2)]
        Bsg = [Buf("c1sg%d" % i) for i in range(2)]
        mg = sb("c1mg", [128, 8, 512], BF16)
        Bmg = Buf("c1mg")
        xt = [sb("c1xt0", [128, 1024], F32)] * 2
        Bxt = [Buf("c1xt0")] * 2
        tmp = sb("c1tmp", [128, 512], F32)
        Btmp = Buf("c1tmp")
        x1 = [sb("c1x10", [128, 1024], F32)] * 2
        Bx1 = [Buf("c1x10")] * 2
        x1nb = sb("c1x1nb", [128, 1024], BF16)
        Bx1nb = Buf("c1x1nb")
        gst = sb("c1gst", [128, 8], F32)
        st6 = sb("c1st6", [128, 2, 6], F32)
        Bg = Buf("c1g")
        g1row = sb("c1g1row", [128, 1024], F32)
        Bg1 = Buf("c1g1row")
        pr.dma(g1row[:], k.growscr[:, 0:1024], [k.Bgs], [Bg1])
        pr.dma(rows[:, 0, :], din["lnin_row"][0:1, :].partition_broadcast(128), [], [Brows])
        pr.dma(rows[:, 1, :], din["lnin_row"][1:2, :].partition_broadcast(128), [], [Brows])
        pr.dma(rows[:, 2, :], din["ln1_row"][0:1, :].partition_broadcast(128), [], [Brows])
        pr.dma(rows[:, 3, :], din["ln1_row"][1:2, :].partition_broadcast(128), [], [Brows])
        pr.ts(rows[:, 0:2, :], rows[:, 0:2, :], ALPHA, None, ALU.mult, None, [Brows], [Brows])
        w_in = din["w_in"]
        sctr = [0]
        steps = []
        for g in range(8):
            def loadpre(tiles, g):
                pr.dma(rp[:], k.retpreT[:, g * 512:(g + 1) * 512].rearrange("(kc p) n -> p kc n", p=128), [], [Brp], q="sp")
                pr.dma(dp[:], k.difpreT[:, g * 512:(g + 1) * 512].rearrange("(kc p) n -> p kc n", p=128), [], [Bdp], q="act")

            def gate_sig(Wg, BWg, oc4, bcol, g):
                i = oc4 % 2
                pb = 2 * i + 1
                for kc in range(8):
                    pr.mm(ps[pb], Wg[:, kc, oc4 * 128:(oc4 + 1) * 128], k.xmT[:, kc, 1 + g * 512:1 + (g + 1) * 512],
                          kc == 0, kc == 7, [BWg, k.Bx[g]], [Bps[pb]])
                pr.act(sg[i][:], ps[pb], AF.Sigmoid, [Bps[pb], k.Bc], [Bsg[i]], bias=bcol)
                return i

            for ch in range(2):
                def stepA(tiles, g=g, ch=ch):
                    (W0, B0), (W1, B1), (Wg, BWg) = tiles
                    if ch == 0:
                        loadpre(tiles, g)
                    for oc4 in range(4):
                        oc = ch * 4 + oc4
                        for kc in range(16):
                            W, BW_ = (W0, B0) if kc < 8 else (W1, B1)
                            pr.mm(ps[2 * (oc4 % 2)], W[:, kc % 8, oc4 * 128:(oc4 + 1) * 128], rp[:, kc, :], kc == 0, kc == 15,
                                  [BW_, Brp], [Bps[2 * (oc4 % 2)]])
                        i = gate_sig(Wg, BWg, oc4, k.bgate[:, oc:oc + 1], g)
                        pr.tt(m1[:, oc4, :], ps[2 * i], sg[i][:], ALU.mult, [Bps[2 * i], Bsg[i]], [Bm1])
                steps.append(([[(0, k.wbf["w_ret_out"][0:1024, ch * 512:(ch + 1) * 512])],
                               [(0, k.wbf["w_ret_out"][1024:2048, ch * 512:(ch + 1) * 512])],
                               [(0, k.wbf["w_gates"][:, ch * 512:(ch + 1) * 512])]], stepA))

                def stepB(tiles, g=g, ch=ch):
                    (Wd, BWd), (Wg, BWg) = tiles
                    for oc4 in range(4):
                        oc = ch * 4 + oc4
                        for kc in range(8):
                            pr.mm(ps[2 * (oc4 % 2)], Wd[:, kc, oc4 * 128:(oc4 + 1) * 128], dp[:, kc, :], kc == 0, kc == 7,
                                  [BWd, Bdp], [Bps[2 * (oc4 % 2)]])
                        i = gate_sig(Wg, BWg, oc4, k.bgate[:, 8 + oc:9 + oc], g)
                        pr.tt(sg[i][:], ps[2 * i], sg[i][:], ALU.mult, [Bps[2 * i], Bsg[i]], [Bsg[i]])
                        pr.tt(mg[:, oc, :], sg[i][:], m1[:, oc4, :], ALU.add, [Bsg[i], Bm1], [Bmg], eng="pool")
                steps.append(([[(0, k.wbf["w_diff_out"][:, ch * 512:(ch + 1) * 512])],
                               [(0, k.wbf["w_gates"][:, 1024 + ch * 512:1024 + (ch + 1) * 512])]], stepB))

            def stepO(tiles, g=g):
                for ts_ in range(4):
                    tt_ = g * 4 + ts_
                    i2 = tt_ % 2
                    pr.dma(xt[i2][:], din["x"][tt_ * 128:(tt_ + 1) * 128, :], [], [Bxt[i2]], q="sp")
                    for chh in range(2):
                        Wo, BWo = tiles[chh]
                        for kc in range(8):
                            pr.mm(ps[4 + chh], mg[:, kc, ts_ * 128:(ts_ + 1) * 128], Wo[:, kc, :], kc == 0, kc == 7,
                                  [Bmg, BWo], [Bps[4 + chh]])
                    st = k.stats[:, tt_, :]
                    X = xt[i2]
                    pr.ts(X[:], X[:], st[:, 0:1], st[:, 3:4], ALU.subtract, ALU.mult, [Bxt[i2], k.Bst[tt_]], [Bxt[i2]])
                    pr.tt(X[:], X[:], rows[:, 0, :], ALU.mult, [Bxt[i2], Brows], [Bxt[i2]])
                    pr.tt(X[:], X[:], rows[:, 1, :], ALU.add, [Bxt[i2], Brows], [Bxt[i2]], eng="pool")
                    for chh in range(2):
                        hs = slice(chh * 512, (chh + 1) * 512)
                        pr.tt(tmp[:], ps[4 + chh], g1row[:, hs], ALU.mult, [Bps[4 + chh], Bg1], [Btmp])
                        pr.tt(X[:, hs], X[:, hs], tmp[:], ALU.add, [Bxt[i2], Btmp], [Bxt[i2]])
                    ln_rows(k, pr, X[:], Bxt[i2], gst, Bg, st6, None)
                    pr.ts(x1nb[:], X[:], gst[:, 0:1], gst[:, 3:4], ALU.subtract, ALU.mult, [Bxt[i2], Bg], [Bx1nb])
                    pr.ts(x1[i2][:], X[:], gst[:, 0:1], gst[:, 3:4], ALU.subtract, ALU.mult, [Bxt[i2], Bg], [Bx1[i2]])
                    pr.tt(x1[i2][:], x1[i2][:], rows[:, 2, :], ALU.mult, [Bx1[i2], Brows], [Bx1[i2]], eng="pool")
                    pr.tt(x1[i2][:], x1[i2][:], rows[:, 3, :], ALU.add, [Bx1[i2], Brows], [Bx1[i2]], eng="pool")
                    pr.dma(k.x1scr[tt_ * 128:(tt_ + 1) * 128, :], x1[i2][:], [Bx1[i2]], [], q="act")
                    ptb = ps[6 + ts_ % 2].bitcast(BF16)
                    Bptb = Bps[6 + ts_ % 2]
                    for kc in range(8):
                        pr.tr(ptb[:, kc * 128:(kc + 1) * 128], x1nb[:, kc * 128:(kc + 1) * 128], k.identb[:], [Bx1nb, k.Bc], [Bptb])
                    for kc in range(8):
                        pr.act(k.xmT[:, kc, 1 + tt_ * 128:1 + (tt_ + 1) * 128], ptb[:, kc * 128:(kc + 1) * 128], AF.Identity,
                               [Bptb, k.Bmod], [k.Bx[g]], scale=k.AB2[:, kc, 0:1], bias=k.AB2[:, kc, 1:2])
            steps.append(([[(0, k.wbf["w_o"][:, 0:512])], [(0, k.wbf["w_o"][:, 512:1024])]], stepO))
        run_steps(k, steps)
        d = k.dbg_t("x1", [S, D])
        if d is not None:
            pr.flush()
            for i_ in range(8):
                pr.dma(d[i_ * 512:(i_ + 1) * 512, :], k.x1scr[i_ * 512:(i_ + 1) * 512, :], [], [])
        pr.flush()


def phaseC2(k):
    pr, nc, din = k.pr, k.nc, k.din
    ps, Bps = k.ps, k.Bps
    with ExitStack() as pes:
        sb = lambda n, s, d: pes.enter_context(nc.sbuf_tensor("sp_" + n, s, d))
        rows = sb("c2rows", [128, 2, 1024], F32)
        Brows = Buf("c2rows")
        xo = sb("c2xo", [128, 4, 1024], F32)
        Bxo = [Buf("c2xo%d" % i) for i in range(4)]
        actT = sb("c2act", [128, 22, 512], BF16)
        Bact = Buf("c2act")
        ut = [sb("c2ut%d" % i, [128, 512], F32) for i in range(4)]
        But = [Buf("c2ut%d" % i) for i in range(4)]
        tmp = sb("c2tmp", [128, 512], F32)
        Btmp = Buf("c2tmp")
        gst = sb("c2gst", [128, 8], F32)
        st6 = sb("c2st6", [128, 2, 6], F32)
        Bg = Buf("c2g")
        g2row = sb("c2g2row", [128, 1024], F32)
        Bg2 = Buf("c2g2row")
        pr.dma(g2row[:], k.growscr[:, 1024:2048], [k.Bgs], [Bg2])
        pr.dma(rows[:, 0, :], din["ln2_row"][0:1, :].partition_broadcast(128), [], [Brows])
        pr.dma(rows[:, 1, :], din["ln2_row"][1:2, :].partition_broadcast(128), [], [Brows])
        pr.memset(k.xmT[:, :, 1 + S:2 + S], 0.0, [k.Bx[8]])
        hT = k.xmT
        Bh = k.Bx + [k.Bxpad]
        groups = [(i * 510, 510) for i in range(8)] + [(4080, 16)]
        steps = []
        fctr = [0]
        for (t0, n) in groups:
            nsub = (n + 127) // 128
            for fg in range(11):
                def stepU(tiles, t0=t0, n=n, fg=fg, nsub=nsub):
                    Wu, BWu = tiles[0]
                    if fg == 0:
                        for ts_ in range(nsub):
                            r_ = min(128, n - ts_ * 128)
                            pr.dma(xo[0:r_, ts_, :], k.x1scr[t0 + ts_ * 128:t0 + ts_ * 128 + r_, :], [], [Bxo[ts_]], q="sp")
                            pr.ts(xo[0:r_, ts_, :], xo[0:r_, ts_, :], ALPHA, None, ALU.mult, None, [Bxo[ts_]], [Bxo[ts_]], eng="pool")
                    for f2 in range(2):
                        fc = fg * 2 + f2
                        i = fctr[0] % 4
                        fctr[0] += 1
                        pu, pg = ps[2 * i], ps[2 * i + 1]
                        Bpu, Bpg = Bps[2 * i], Bps[2 * i + 1]
                        for kc in range(8):
                            pr.mm(pu[:, 0:n + 2], Wu[:, kc, f2 * 128:(f2 + 1) * 128], hT[:, kc, t0:t0 + n + 2], kc == 0, kc == 7,
                                  [BWu] + Bh, [Bpu])
                        for kc in range(8):
                            pr.mm(pg[:, 0:n], Wu[:, kc, 256 + f2 * 128:256 + (f2 + 1) * 128], hT[:, kc, t0 + 1:t0 + 1 + n], kc == 0, kc == 7,
                                  [BWu] + Bh, [Bpg])
                        U = ut[i]
                        pr.act(U[:, 0:n], pu[:, 1:n + 1], AF.Identity, [Bpu, k.Bc], [But[i]], scale=k.convw[:, fc, 1:2], bias=k.convb[:, fc:fc + 1])
                        pr.stt(U[:, 0:n], pu[:, 0:n], k.convw[:, fc, 0:1], U[:, 0:n], ALU.mult, ALU.add, [Bpu, k.Bc, But[i]], [But[i]])
                        pr.stt(U[:, 0:n], pu[:, 2:n + 2], k.convw[:, fc, 2:3], U[:, 0:n], ALU.mult, ALU.add, [Bpu, k.Bc, But[i]], [But[i]])
                        pr.act(U[:, 0:n], U[:, 0:n], AF.Gelu, [But[i]], [But[i]])
                        pr.tt(actT[:, fc, 0:n], U[:, 0:n], pg[:, 0:n], ALU.mult, [But[i], Bpg], [Bact])
                steps.append(([[(0, k.wbf["w_up"][:, fg * 256:(fg + 1) * 256]), (256, k.wbf["w_up"][:, DFF + fg * 256:DFF + (fg + 1) * 256])]], stepU))
            for chh in range(2):
                def stepD(tiles, t0=t0, n=n, chh=chh, nsub=nsub):
                    hs = slice(chh * 512, (chh + 1) * 512)
                    for ts_ in range(nsub):
                        r_ = min(128, n - ts_ * 128)
                        pb = 4 + ts_
                        for fc in range(22):
                            Wd, BWd = tiles[fc // 8]
                            pr.mm(ps[pb][0:r_, :], actT[:, fc, ts_ * 128:ts_ * 128 + r_], Wd[:, fc % 8, :], fc == 0, fc == 21,
                                  [Bact, BWd], [Bps[pb]])
                        pr.tt(tmp[0:r_, :], ps[pb][0:r_, :], g2row[0:r_, hs], ALU.mult, [Bps[pb], Bg2], [Btmp])
                        pr.tt(xo[0:r_, ts_, hs], xo[0:r_, ts_, hs], tmp[0:r_, :], ALU.add, [Bxo[ts_], Btmp], [Bxo[ts_]])
                        if chh == 1:
                            X = xo[0:r_, ts_, :]
                            ln_rows(k, pr, X, Bxo[ts_], gst, Bg, st6, None)
                            pr.ts(X, X, gst[0:r_, 0:1], gst[0:r_, 3:4], ALU.subtract, ALU.mult, [Bxo[ts_], Bg], [Bxo[ts_]])
                            pr.tt(X, X, rows[0:r_, 0, :], ALU.mult, [Bxo[ts_], Brows], [Bxo[ts_]], eng="pool")
                            pr.tt(X, X, rows[0:r_, 1, :], ALU.add, [Bxo[ts_], Brows], [Bxo[ts_]], eng="pool")
                            pr.dma(k.out[t0 + ts_ * 128:t0 + ts_ * 128 + r_, :], X, [Bxo[ts_]], [], q="act")
                steps.append(([[(0, k.wbf["w_down"][0:1024, chh * 512:(chh + 1) * 512])],
                               [(0, k.wbf["w_down"][1024:2048, chh * 512:(chh + 1) * 512])],
                               [(0, k.wbf["w_down"][2048:2816, chh * 512:(chh + 1) * 512])]], stepD))
        run_steps(k, steps)
        pr.flush()


_CACHE = {}


def run(inputs, debug=(), cores=8):
    key = tuple(sorted(debug))
    if key not in _CACHE:
        _CACHE[key] = build_program(debug)
    nc, cst = _CACHE[key]
    in_maps = [host_inputs(inputs, b, cst) for b in range(cores)]
    res = run_bass_kernel_spmd(nc, in_maps, core_ids=list(range(cores)))
    return res.results


def kernel(**inputs):
    results = run(inputs)
    return np.stack([np.asarray(r["out"], dtype=np.float32) for r in results], axis=0)
```

```python
import numpy as np
import ml_dtypes
from contextlib import ExitStack
import concourse.bass as bass
import concourse.mybir as mybir
from concourse.bass_utils import run_bass_kernel_spmd

F32 = mybir.dt.float32
BF16 = mybir.dt.bfloat16
AF = mybir.ActivationFunctionType
ALU = mybir.AluOpType

D = 1024
S = 4096
P = 256
NT = S + P
NIN = 11264
DFF = 2816
EPS = 1e-5
ALPHA = 2.0 ** 0.25
LAM_INIT = 0.2
OFF_QR, OFF_KR, OFF_VR, OFF_GR, OFF_QD, OFF_KD, OFF_VD, OFF_GTR, OFF_GTD = (
    0, 1024, 2048, 4096, 6144, 7168, 8192, 9216, 10240)

SAME_ENG_RAW = True


class Buf:
    __slots__ = ("name", "lw", "rs", "excl")

    def __init__(self, name, excl=False):
        self.name = name
        self.lw = None
        self.rs = []
        self.excl = excl


class Op:
    __slots__ = ("eng", "fn", "reads", "writes", "dma", "key", "pos", "signal", "sigval",
                 "waits", "kref", "extra")

    def __init__(self, eng, fn, reads, writes, dma):
        self.eng = eng
        self.fn = fn
        self.reads = reads
        self.writes = writes
        self.dma = dma
        self.key = None
        self.pos = 0
        self.signal = False
        self.sigval = 0
        self.waits = []
        self.kref = None
        self.extra = None


class Prog:
    ENG = ("pe", "act", "dve", "pool", "sp")

    def __init__(self, nc, es):
        self.nc = nc
        self.eo = dict(pe=nc.tensor, act=nc.scalar, dve=nc.vector, pool=nc.gpsimd, sp=nc.sync)
        self.esem = {e: es.enter_context(nc.semaphore("s_" + e)) for e in self.ENG}
        self.nslots = dict(sp=12, act=6, pool=8)
        self.dsem = {}
        self.slot_cnt = {}
        self.slot_last = {}
        for q, n in self.nslots.items():
            for i in range(n):
                k = ("d", q, i)
                self.dsem[k] = es.enter_context(nc.semaphore("d_%s%d" % (q, i)))
                self.slot_cnt[k] = 0
                self.slot_last[k] = None
        self.slot_rr = {q: 0 for q in self.nslots}
        self.pending = []
        self.known = {e: {} for e in self.ENG}
        self.epos = {e: 0 for e in self.ENG}
        self.esig = {e: 0 for e in self.ENG}
        self.last_real = {e: None for e in self.ENG}
        self.nops = 0

    def add(self, eng, fn, reads=(), writes=(), dma=False):
        op = Op(eng, fn, tuple(reads), tuple(writes), dma)
        self.pending.append(op)
        return op

    def barrier(self):
        marks = []
        for e in self.ENG:
            op = self.add(e, None)
            op.extra = "barrier"
            marks.append(op)
        return marks

    def flush(self, final=False):
        if not final:
            self.barrier()
        ops = self.pending
        self.pending = []
        nc = self.nc
        for X in ops:
            q = X.eng
            deps = {}
            if X.extra == "barrier":
                for e in self.ENG:
                    if self.last_real[e] is not None:
                        deps[self.last_real[e]] = False
                for k, d in self.slot_last.items():
                    if d is not None:
                        deps[d] = False
            else:
                for b in X.reads:
                    if b.lw is not None:
                        deps[b.lw] = True
                    if b.excl:
                        for r in b.rs:
                            deps.setdefault(r, False)
                for b in X.writes:
                    if b.lw is not None:
                        deps.setdefault(b.lw, False)
                    for r in b.rs:
                        deps.setdefault(r, False)
            if X.dma:
                i = self.slot_rr[q]
                self.slot_rr[q] = (i + 1) % self.nslots[q]
                key = ("d", q, i)
                prev = self.slot_last[key]
                if prev is not None:
                    deps.setdefault(prev, False)
                self.slot_cnt[key] += 1
                X.key = key
                X.pos = self.slot_cnt[key]
                self.slot_last[key] = X
            else:
                self.epos[q] += 1
                X.key = q
                X.pos = self.epos[q]
                if X.fn is not None:
                    self.last_real[q] = X
            known = self.known[q]
            need = {}
            for Dp, raw in deps.items():
                if Dp is X:
                    continue
                if (not Dp.dma) and Dp.key == q and X.extra != "barrier":
                    if q == "pe" or not SAME_ENG_RAW:
                        continue
                if known.get(Dp.key, 0) >= Dp.pos:
                    continue
                o = need.get(Dp.key)
                if o is None or o.pos < Dp.pos:
                    need[Dp.key] = Dp
            if need:
                newk = dict(known)
                for k, Dp in need.items():
                    Dp.signal = True
                    X.waits.append(Dp)
                    kr = Dp.kref
                    if kr:
                        for kk, vv in kr.items():
                            if newk.get(kk, 0) < vv:
                                newk[kk] = vv
                    if newk.get(k, 0) < Dp.pos:
                        newk[k] = Dp.pos
                self.known[q] = newk
            X.kref = self.known[q]
            for b in X.reads:
                if b.excl:
                    b.lw = X
                    b.rs = []
                else:
                    b.rs.append(X)
            for b in X.writes:
                b.lw = X
                b.rs = []
        per = {e: [] for e in self.ENG}
        for X in ops:
            if (not X.dma) and X.signal:
                if X.fn is None:
                    raise RuntimeError("signal on empty op")
                self.esig[X.eng] += 1
                X.sigval = self.esig[X.eng]
            per[X.eng].append(X)
        self.nops += len(ops)
        esem, dsem = self.esem, self.dsem

        def emit(eng, lst):
            for X in lst:
                for Dp in X.waits:
                    if Dp.dma:
                        eng.wait_ge(dsem[Dp.key], 16 * Dp.pos)
                    else:
                        eng.wait_ge(esem[Dp.key], Dp.sigval)
                if X.fn is not None:
                    ins = X.fn(eng)
                    if X.dma:
                        ins.then_inc(dsem[X.key], 16)
                    elif X.signal:
                        ins.then_inc(esem[X.eng], 1)

        with nc.Block() as block:
            if per["pe"]:
                @block.tensor
                def _(e):
                    emit(e, per["pe"])
            if per["act"]:
                @block.scalar
                def _(e):
                    emit(e, per["act"])
            if per["dve"]:
                @block.vector
                def _(e):
                    emit(e, per["dve"])
            if per["pool"]:
                @block.gpsimd
                def _(e):
                    emit(e, per["pool"])
            if per["sp"]:
                @block.sync
                def _(e):
                    emit(e, per["sp"])

    def mm(self, out, lhsT, rhs, start, stop, r, w, skip=False):
        if skip:
            return self.add("pe", lambda e: e.matmul(out, lhsT, rhs, start=start, stop=stop, skip_group_check=True), r, w)
        return self.add("pe", lambda e: e.matmul(out, lhsT, rhs, start=start, stop=stop), r, w)

    def tr(self, out, in_, ident, r, w):
        return self.add("pe", lambda e: e.transpose(out, in_, ident), r, w)

    def act(self, out, in_, func, r, w, bias=None, scale=None, accum_out=None, eng="act"):
        kw = {}
        if bias is not None:
            kw["bias"] = bias
        if scale is not None:
            kw["scale"] = scale
        if accum_out is not None:
            kw["accum_out"] = accum_out
        return self.add(eng, lambda e: e.activation(out, in_, func, **kw), r, w)

    def ts(self, out, in0, s1, s2, op0, op1, r, w, eng="dve", accum_out=None):
        if op1 is None:
            return self.add(eng, lambda e: e.tensor_scalar(out, in0, s1, None, op0), r, w)
        if accum_out is not None:
            return self.add(eng, lambda e: e.tensor_scalar(out, in0, s1, s2, op0, op1, accum_out), r, w)
        return self.add(eng, lambda e: e.tensor_scalar(out, in0, s1, s2, op0, op1), r, w)

    def tt(self, out, in0, in1, op, r, w, eng="dve"):
        return self.add(eng, lambda e: e.tensor_tensor(out, in0, in1, op), r, w)

    def stt(self, out, in0, scalar, in1, op0, op1, r, w, eng="dve", accum_out=None):
        if accum_out is not None:
            return self.add(eng, lambda e: e.scalar_tensor_tensor(out, in0, scalar, in1, op0, op1, accum_out), r, w)
        return self.add(eng, lambda e: e.scalar_tensor_tensor(out, in0, scalar, in1, op0, op1), r, w)

    def cp(self, out, in_, r, w, eng="dve"):
        if eng == "act":
            return self.add("act", lambda e: e.copy(out, in_), r, w)
        return self.add(eng, lambda e: e.tensor_copy(out, in_), r, w)

    def memset(self, out, val, w, eng="dve"):
        return self.add(eng, lambda e: e.memset(out, val), (), w)

    def dma(self, out, in_, r, w, q="sp"):
        return self.add(q, lambda e: e.dma_start(out=out, in_=in_), r, w, dma=True)


class K:
    pass


def _consts():
    c = {}
    c["identb"] = np.eye(128, dtype=np.float32).astype(ml_dtypes.bfloat16)
    c["identf"] = np.eye(128, dtype=np.float32)
    pr = np.zeros((128, 128), np.float32)
    for m in range(128):
        b, i = m // 64, m % 64
        if i < 32:
            pr[b * 64 + i + 32, m] = -1.0
        else:
            pr[b * 64 + i - 32, m] = 1.0
    c["permR"] = pr.astype(ml_dtypes.bfloat16)
    k = np.arange(128, dtype=np.float32)[:, None]
    q = np.arange(128, dtype=np.float32)[None, :]
    dec = np.zeros((128, 4, 128), np.float32)
    dec[:, 0] = np.maximum(q - k, 0)
    dec[:, 1] = np.maximum(k - q, 0)
    dec[:, 2] = (q >= k)
    dec[:, 3] = (k >= q)
    c["dectab"] = dec
    j = np.arange(128, dtype=np.float32)
    prow = np.zeros((128, 2, 128), np.float32)
    prow[:, 0, :] = j + 1.0
    prow[:, 1, :] = 128.0 - j
    c["posrow"] = prow
    p = np.arange(128, dtype=np.float32)
    pc = np.stack([127.0 - p, p, 255.0 - p, 127.0 - p, p, 128.0 + p], axis=1)
    c["poscol"] = np.ascontiguousarray(pc.astype(np.float32))
    t = np.arange(S)
    row = (t // 64).astype(np.float64)
    col = (t % 64).astype(np.float64)

    def tabs(head_dim):
        half = head_dim // 2
        inv = 10000.0 ** (-(np.arange(0, half, 2, dtype=np.float64) / half))
        inv = inv.astype(np.float32).astype(np.float64)
        ang = np.concatenate([row[:, None] * inv, col[:, None] * inv], axis=-1)
        ang = ang.astype(np.float32).astype(np.float64)
        return np.cos(ang).astype(np.float32).T, np.sin(ang).astype(np.float32).T

    cr, sr = tabs(256)
    c["ropeR"] = np.ascontiguousarray(np.stack([cr, sr, cr / 16.0, sr / 16.0], 0).astype(np.float32))
    cd, sd = tabs(64)
    cd4 = np.tile(cd, (4, 1))
    sd4 = np.tile(sd, (4, 1))
    c["ropeD"] = np.ascontiguousarray(np.stack([cd4 / 8.0, sd4 / 8.0, cd4, sd4], 0).astype(np.float32))
    return c


CONST_DT = dict(identb=BF16, identf=F32, permR=BF16, dectab=F32, posrow=F32, poscol=F32, ropeR=F32, ropeD=F32)

IN_SHAPES = dict(
    x=[S, D], ctx=[P, D], ccol=[128, 8, 2], lnin_col=[128, 8, 2], ln1_col=[128, 8, 2],
    lnin_row=[2, D], ln1_row=[2, D], ln2_row=[2, D],
    w_mod=[D, 6 * D], bmod_col=[128, 48], bmod_row=[1, 6 * D], w_in=[D, NIN], bgate_col=[128, 16],
    decay_logit=[1, 8], diff_lambda=[1, 256], subln_col=[128, 1],
    w_ret_out=[2048, D], w_diff_out=[D, D], w_o=[D, D], w_up=[D, 2 * DFF],
    convw_col=[128, 22, 3], convb_col=[128, 22], w_down=[DFF, D],
)


def host_inputs(inp, b, consts):
    f = lambda a: np.ascontiguousarray(np.asarray(a, dtype=np.float32))
    col8 = lambda v: f(v).reshape(8, 128).T
    m = {}
    m["x"] = f(inp["x"][b])
    m["ctx"] = f(inp["ctx"][b])
    m["ccol"] = f(np.stack([col8(inp["c"][b]), col8(inp["c_ctx"])], -1))
    m["lnin_col"] = f(np.stack([col8(inp["ln_in_g"]), col8(inp["ln_in_b"])], -1))
    m["ln1_col"] = f(np.stack([col8(inp["ln1_g"][0]), col8(inp["ln1_b"][0])], -1))
    m["lnin_row"] = f(np.stack([inp["ln_in_g"], inp["ln_in_b"]], 0))
    m["ln1_row"] = f(np.stack([inp["ln1_g"][0], inp["ln1_b"][0]], 0))
    m["ln2_row"] = f(np.stack([inp["ln2_g"][0], inp["ln2_b"][0]], 0))
    m["w_mod"] = f(inp["w_mod"][0])
    m["bmod_col"] = f(f(inp["b_mod"][0]).reshape(48, 128).T)
    m["bmod_row"] = f(inp["b_mod"][0]).reshape(1, -1)
    m["w_in"] = f(inp["w_in"][0])
    m["bgate_col"] = f(f(inp["b_gate"][0]).reshape(16, 128).T)
    m["decay_logit"] = f(inp["ret_decay_logit"][0]).reshape(1, 8)
    m["diff_lambda"] = f(inp["diff_lambda"][0]).reshape(1, 256)
    m["subln_col"] = f(inp["diff_subln_g"][0]).reshape(128, 1)
    m["w_ret_out"] = f(inp["w_ret_out"][0])
    m["w_diff_out"] = f(inp["w_diff_out"][0])
    m["w_o"] = f(inp["w_o"][0])
    m["w_up"] = f(inp["w_up"][0])
    m["convw_col"] = f(f(inp["conv_w"][0]).reshape(3, 22, 128).transpose(2, 1, 0))
    m["convb_col"] = f(f(inp["conv_b"][0]).reshape(22, 128).T)
    m["w_down"] = f(inp["w_down"][0])
    m.update(consts)
    return m


def build_program(debug=()):
    nc = bass.Bass("TRN2", target_bir_lowering=False)
    k = K()
    k.nc = nc
    k.debug = set(debug)
    k.din = {}
    for n, shp in IN_SHAPES.items():
        k.din[n] = nc.dram_tensor(n, shp, F32, kind="ExternalInput").ap()
    cst = _consts()
    for n, arr in cst.items():
        k.din[n] = nc.dram_tensor(n, list(arr.shape), CONST_DT[n], kind="ExternalInput").ap()
    k.out = nc.dram_tensor("out", [S, D], F32, kind="ExternalOutput").ap()
    k.dbg = {}

    def dbg_t(name, shape, dt=F32):
        if name in k.debug:
            k.dbg[name] = nc.dram_tensor("dbg_" + name, shape, dt, kind="ExternalOutput").ap()
            return k.dbg[name]
        return None
    k.dbg_t = dbg_t
    k.retpreT = nc.dram_tensor("retpreT", [2048, S], BF16, kind="Internal").ap()
    k.difpreT = nc.dram_tensor("difpreT", [1024, S], BF16, kind="Internal").ap()
    k.ybscr = nc.dram_tensor("ybscr", [4, 32, 128, 512], F32, kind="Internal").ap()
    k.x1scr = nc.dram_tensor("x1scr", [S, D], F32, kind="Internal").ap()
    k.growscr = nc.dram_tensor("growscr", [128, 2048], F32, kind="Internal").ap()
    k.wbf = {}
    k.Bwbf = {}
    for n_, shp in (("w_ret_out", [2048, D]), ("w_diff_out", [D, D]), ("w_o", [D, D]), ("w_up", [D, 2 * DFF]),
                    ("w_down", [DFF, D]), ("w_gates", [D, 2048])):
        k.wbf[n_] = nc.dram_tensor("wbf_" + n_, shp, BF16, kind="Internal").ap()
        k.Bwbf[n_] = Buf("wbf_" + n_)

    with ExitStack() as es:
        pr = Prog(nc, es)
        k.pr = pr
        sb = lambda n, s, d: es.enter_context(nc.sbuf_tensor("sb_" + n, s, d))
        k.xmT = sb("xmT", [128, 8, 1 + NT], BF16)
        k.Bx = [Buf("xm%d" % g) for g in range(9)]
        k.Bxpad = Buf("xmpad")
        k.NW = 6
        k.wsl = [sb("w%d" % i, [128, 8, 512], BF16) for i in range(k.NW)]
        k.Bw = [Buf("w%d" % i) for i in range(k.NW)]
        k.wctr = 0
        k.psall = es.enter_context(nc.psum_tensor("psall", [128, 8, 512], F32))
        k.ps = [k.psall[:, i, :] for i in range(8)]
        k.Bps = [Buf("ps%d" % i, excl=True) for i in range(8)]
        k.identb = sb("identb", [128, 128], BF16)
        k.identf = sb("identf", [128, 128], F32)
        k.permR = sb("permR", [128, 128], BF16)
        k.poscol = sb("poscol", [128, 6], F32)
        k.Bc = Buf("consts")
        k.small = sb("small", [128, 64], F32)
        k.Bsm = Buf("small")
        k.modT = sb("modT", [128, 48, 2], F32)
        k.Bmod = Buf("modT")
        k.AB1 = sb("AB1", [128, 8, 4], F32)
        k.AB2 = sb("AB2", [128, 8, 2], F32)
        k.Bgrow = Buf("grow")
        k.stats = sb("stats", [128, 34, 4], F32)
        k.Bst = [Buf("st%d" % i) for i in range(34)]
        k.dec = sb("dec", [128, 4, 3, 128], F32)
        k.deccol = sb("deccol", [128, 4, 8], F32)
        k.Bdec = Buf("dec")
        k.bgate = sb("bgate", [128, 16], F32)
        k.convw = sb("convw", [128, 22, 3], F32)
        k.convb = sb("convb", [128, 22], F32)
        k.subln = sb("subln", [128, 1], F32)

        phase0(k)
        phaseA(k)
        if "stopA" not in k.debug:
            if "skipB2" not in k.debug:
                phaseB2(k)
            if "stopB2" not in k.debug:
                phaseB1(k)
                if "stopB1" not in k.debug:
                    phaseC1(k)
                    if "stopC1" not in k.debug:
                        phaseC2(k)
        pr.flush(final=False)
    return nc, cst


def wtile_load(k, pieces, q="pool", rd=()):
    pr = k.pr
    s = k.wctr % k.NW
    k.wctr += 1
    for off, src in pieces:
        nk = src.shape[0] // 128
        n = src.shape[1]
        qq = q if q == "pool" else ("sp" if k.wctr % 2 == 0 else "act")
        pr.dma(k.wsl[s][:, 0:nk, off:off + n], src.rearrange("(kc p) c -> p kc c", p=128), list(rd), [k.Bw[s]], q=qq)
    return k.wsl[s], k.Bw[s]


def rstd_ops(k, out_col, var_col, tmp_col, r, w, eng="act"):
    pr = k.pr
    pr.act(tmp_col, var_col, AF.Ln, list(r) + [k.Bsm], w, bias=k.small[:, 0:1])
    pr.act(out_col, tmp_col, AF.Exp, w, w, scale=-0.5)


def phase0(k):
    pr, nc, din = k.pr, k.nc, k.din
    sm = k.small
    with ExitStack() as pes:
        sb = lambda n, s, d: pes.enter_context(nc.sbuf_tensor("sp_" + n, s, d))
        k.grow = sb("grow", [128, 2, 1024], F32)
        k.dectab = sb("dectab", [128, 4, 128], F32)
        k.posrow = sb("posrow", [128, 2, 128], F32)
        pr.dma(k.identb[:], din["identb"], [], [k.Bc])
        pr.dma(k.identf[:], din["identf"], [], [k.Bc])
        pr.dma(k.permR[:], din["permR"], [], [k.Bc])
        pr.dma(k.dectab[:], din["dectab"], [], [k.Bc])
        pr.dma(k.posrow[:], din["posrow"], [], [k.Bc])
        pr.dma(k.poscol[:], din["poscol"], [], [k.Bc])
        pr.dma(k.bgate[:], din["bgate_col"], [], [k.Bc])
        pr.dma(k.convw[:], din["convw_col"], [], [k.Bc])
        pr.dma(k.convb[:], din["convb_col"], [], [k.Bc])
        pr.dma(k.subln[:], din["subln_col"], [], [k.Bc])
        pr.memset(sm[:], 0.0, [k.Bsm])
        pr.memset(sm[:, 0:1], EPS, [k.Bsm])
        pr.memset(k.xmT[:, :, 0:1], 0.0, [k.Bxpad])
        ccol = sb("ccol", [128, 8, 2], F32)
        cond = sb("cond", [128, 8, 2], F32)
        lnin = sb("lnin", [128, 8, 2], F32)
        ln1c = sb("ln1c", [128, 8, 2], F32)
        bmodc = sb("bmodc", [128, 48], F32)
        brow = sb("brow", [128, 2, 1024], F32)
        condrep = sb("condrep", [128, 8, 128], F32)
        ones = sb("ones", [128, 128], F32)
        lamb = sb("lamb", [128, 256], F32)
        tmp = sb("tmp0", [128, 8, 2], F32)
        B0 = Buf("p0")
        Bcr = Buf("condrep")
        pr.dma(ccol[:], din["ccol"], [], [B0])
        pr.dma(lnin[:], din["lnin_col"], [], [B0])
        pr.dma(ln1c[:], din["ln1_col"], [], [B0])
        pr.dma(bmodc[:], din["bmod_col"], [], [B0])
        pr.dma(brow[:, 0, :], din["bmod_row"][:, 2048:3072].partition_broadcast(128), [], [B0])
        pr.dma(brow[:, 1, :], din["bmod_row"][:, 5120:6144].partition_broadcast(128), [], [B0])
        pr.dma(sm[:, 8:16], din["decay_logit"].partition_broadcast(128), [], [k.Bsm])
        pr.dma(lamb[:], din["diff_lambda"].partition_broadcast(128), [], [B0])
        pr.memset(ones[:], 1.0, [B0])
        pr.act(cond[:], ccol[:], AF.Silu, [B0], [B0])
        for kc in range(8):
            pr.ts(condrep[:, kc, :], ones[:], cond[:, kc, 0:1], None, ALU.mult, None, [B0], [Bcr])
        pr.tt(lamb[:, 0:64], lamb[:, 0:64], lamb[:, 64:128], ALU.mult, [B0], [B0])
        pr.tt(lamb[:, 128:192], lamb[:, 128:192], lamb[:, 192:256], ALU.mult, [B0], [B0])
        pr.add("dve", lambda e: e.reduce_sum(sm[:, 24:25], lamb[:, 0:64], mybir.AxisListType.X), [B0], [k.Bsm])
        pr.add("dve", lambda e: e.reduce_sum(sm[:, 25:26], lamb[:, 128:192], mybir.AxisListType.X), [B0], [k.Bsm])
        pr.act(sm[:, 24:26], sm[:, 24:26], AF.Exp, [k.Bsm], [k.Bsm])
        pr.tt(sm[:, 1:2], sm[:, 24:25], sm[:, 25:26], ALU.subtract, [k.Bsm], [k.Bsm])
        pr.ts(sm[:, 1:2], sm[:, 1:2], LAM_INIT, None, ALU.add, None, [k.Bsm], [k.Bsm])
        pr.ts(sm[:, 2:3], sm[:, 1:2], -1.0, None, ALU.mult, None, [k.Bsm], [k.Bsm])
        pr.ts(sm[:, 3:4], k.subln[:, 0:1], 1.0 - LAM_INIT, None, ALU.mult, None, [k.Bc], [k.Bsm])
        pr.act(sm[:, 16:24], sm[:, 8:16], AF.Exp, [k.Bsm], [k.Bsm], scale=-1.0)
        pr.ts(sm[:, 16:24], sm[:, 16:24], 1.0, None, ALU.add, None, [k.Bsm], [k.Bsm])
        pr.act(sm[:, 16:24], sm[:, 16:24], AF.Ln, [k.Bsm], [k.Bsm])
        pr.ts(sm[:, 16:24], sm[:, 16:24], -1.0, None, ALU.mult, None, [k.Bsm], [k.Bsm])
        dt = k.dectab
        t1 = sb("t1", [128, 128], F32)
        t2 = sb("t2", [128, 128], F32)
        Bt = Buf("t12")
        for h in range(4):
            lgf = sm[:, 16 + h:17 + h]
            lgb = sm[:, 20 + h:21 + h]
            pr.act(t1[:], dt[:, 0, :], AF.Exp, [k.Bc, k.Bsm], [Bt], scale=lgf)
            pr.tt(t1[:], t1[:], dt[:, 2, :], ALU.mult, [Bt, k.Bc], [Bt])
            pr.act(t2[:], dt[:, 1, :], AF.Exp, [k.Bc, k.Bsm], [Bt], scale=lgb)
            pr.tt(t2[:], t2[:], dt[:, 3, :], ALU.mult, [Bt, k.Bc], [Bt])
            pr.tt(k.dec[:, h, 0, :], t1[:], t2[:], ALU.add, [Bt], [k.Bdec])
            pr.act(k.dec[:, h, 1, :], k.posrow[:, 0, :], AF.Exp, [k.Bc, k.Bsm], [k.Bdec], scale=lgf)
            pr.act(k.dec[:, h, 2, :], k.posrow[:, 1, :], AF.Exp, [k.Bc, k.Bsm], [k.Bdec], scale=lgb)
            dc = k.deccol
            pc = k.poscol
            pr.act(dc[:, h, 0:1], pc[:, 0:1], AF.Exp, [k.Bc, k.Bsm], [k.Bdec], scale=lgf)
            pr.act(dc[:, h, 1:2], pc[:, 1:2], AF.Exp, [k.Bc, k.Bsm], [k.Bdec], scale=lgb)
            pr.act(dc[:, h, 2:3], lgf, AF.Exp, [k.Bsm], [k.Bdec], scale=128.0)
            pr.act(dc[:, h, 3:4], lgb, AF.Exp, [k.Bsm], [k.Bdec], scale=128.0)
            pr.act(dc[:, h, 4:6], pc[:, 2:4], AF.Exp, [k.Bc, k.Bsm], [k.Bdec], scale=lgf)
            pr.act(dc[:, h, 6:8], pc[:, 4:6], AF.Exp, [k.Bc, k.Bsm], [k.Bdec], scale=lgb)
        wm = [sb("wm%d" % i, [128, 8, 512], F32) for i in range(3)]
        Bwm = [Buf("wm%d" % i) for i in range(3)]
        psc = k.ps[0]
        for j in range(12):
            s = j % 3
            pr.dma(wm[s][:], din["w_mod"][:, j * 512:(j + 1) * 512].rearrange("(kc p) c -> p kc c", p=128),
                   [], [Bwm[s]], q=("sp" if j % 2 == 0 else "act"))
            for o in range(4):
                oc = j * 4 + o
                for kc in range(8):
                    pr.mm(psc[:, oc * 2:oc * 2 + 2], wm[s][:, kc, o * 128:(o + 1) * 128], cond[:, kc, :],
                          kc == 0, kc == 7, [Bwm[s], B0], [k.Bps[0]])
            if j in (4, 5, 10, 11):
                gi, half = (0 if j < 6 else 1), j % 2
                pb = k.ps[1 + (j % 2)]
                for kc in range(8):
                    pr.mm(pb[:], condrep[:, kc, :], wm[s][:, kc, :], kc == 0, kc == 7, [Bwm[s], Bcr], [k.Bps[1 + (j % 2)]])
                pr.tt(k.grow[:, gi, half * 512:(half + 1) * 512], pb[:], brow[:, gi, half * 512:(half + 1) * 512],
                      ALU.add, [k.Bps[1 + (j % 2)], B0], [k.Bgrow])
        pview = psc[:, 0:96].rearrange("p (o s) -> p o s", s=2)
        for s2 in range(2):
            pr.tt(k.modT[:, :, s2], pview[:, :, s2], bmodc[:], ALU.add, [k.Bps[0], B0], [k.Bmod])
        for s2 in range(2):
            pr.ts(tmp[:, :, s2], k.modT[:, 8:16, s2], 1.0, None, ALU.add, None, [k.Bmod], [B0])
            pr.tt(k.AB1[:, :, s2], tmp[:, :, s2], lnin[:, :, 0], ALU.mult, [B0], [k.Bmod])
            pr.tt(k.AB1[:, :, 2 + s2], tmp[:, :, s2], lnin[:, :, 1], ALU.mult, [B0], [k.Bmod])
            pr.tt(k.AB1[:, :, 2 + s2], k.AB1[:, :, 2 + s2], k.modT[:, 0:8, s2], ALU.add, [k.Bmod], [k.Bmod])
        pr.ts(tmp[:, :, 0], k.modT[:, 32:40, 0], 1.0, None, ALU.add, None, [k.Bmod], [B0])
        pr.tt(k.AB2[:, :, 0], tmp[:, :, 0], ln1c[:, :, 0], ALU.mult, [B0], [k.Bmod])
        pr.tt(k.AB2[:, :, 1], tmp[:, :, 0], ln1c[:, :, 1], ALU.mult, [B0], [k.Bmod])
        pr.tt(k.AB2[:, :, 1], k.AB2[:, :, 1], k.modT[:, 24:32, 0], ALU.add, [k.Bmod], [k.Bmod])
        d = k.dbg_t("modT", [128, 96])
        if d is not None:
            pr.dma(d, k.modT[:].rearrange("p o s -> p (o s)"), [k.Bmod], [])
        Bgs = Buf("growscr")
        k.Bgs = Bgs
        pr.dma(k.growscr, k.grow[:].rearrange("p o s -> p (o s)"), [k.Bgrow], [Bgs])
        d = k.dbg_t("grow", [128, 2048])
        if d is not None:
            pr.dma(d, k.grow[:].rearrange("p o s -> p (o s)"), [k.Bgrow], [])
        d = k.dbg_t("small", [128, 64])
        if d is not None:
            pr.dma(d, k.small[:], [k.Bsm], [])
        d = k.dbg_t("dec", [128, 4 * 3 * 128])
        if d is not None:
            pr.dma(d, k.dec[:].rearrange("p a b c -> p (a b c)"), [k.Bdec], [])
        d = k.dbg_t("deccol", [128, 32])
        if d is not None:
            pr.dma(d, k.deccol[:].rearrange("p a b -> p (a b)"), [k.Bdec], [])
        pr.flush()


def precast_weights(k):
    pr, din = k.pr, k.din
    for n_ in ("w_ret_out", "w_diff_out", "w_o", "w_up", "w_down", "w_gates"):
        src = din["w_in"][:, OFF_GTR:OFF_GTR + 2048] if n_ == "w_gates" else din[n_]
        rows = src.shape[0]
        for r0 in range(0, rows, 256):
            r1 = min(rows, r0 + 256)
            pr.dma(k.wbf[n_][r0:r1, :], src[r0:r1, :], [], [k.Bwbf[n_]], q="pool")


def phaseA(k):
    pr, nc, din = k.pr, k.nc, k.din
    precast_weights(k)
    with ExitStack() as pes:
        sb = lambda n, s, d: pes.enter_context(nc.sbuf_tensor("sp_" + n, s, d))
        xin = [sb("xin%d" % i, [128, 1024], F32) for i in range(3)]
        Bxin = [Buf("xin%d" % i) for i in range(3)]
        xnb = [sb("xnb%d" % i, [128, 1024], BF16) for i in range(2)]
        Bxnb = [Buf("xnb%d" % i) for i in range(2)]
        st6 = [sb("st6%d" % i, [128, 2, 6], F32) for i in range(2)]
        Bst6 = [Buf("st6%d" % i) for i in range(2)]
        for tt in range(34):
            lat = tt < 32
            src = din["x"][tt * 128:(tt + 1) * 128, :] if lat else din["ctx"][(tt - 32) * 128:(tt - 31) * 128, :]
            s3, s2 = tt % 3, tt % 2
            pr.dma(xin[s3][:], src, [], [Bxin[s3]], q=("sp" if tt % 2 == 0 else "act"))
            for i in range(2):
                pr.add("dve", lambda e, i=i, s3=s3, s2=s2: e.bn_stats(st6[s2][:, i, :], xin[s3][:, i * 512:(i + 1) * 512]),
                       [Bxin[s3]], [Bst6[s2]])
            stt_ = k.stats[:, tt, :]
            pr.add("dve", lambda e, s2=s2, stt_=stt_: e.bn_aggr(stt_[:, 0:2], st6[s2][:].rearrange("p a b -> p (a b)")),
                   [Bst6[s2]], [k.Bst[tt]])
            rstd_ops(k, stt_[:, 3:4], stt_[:, 1:2], stt_[:, 2:3], [k.Bst[tt]], [k.Bst[tt]])
            pr.ts(xnb[s2][:], xin[s3][:], stt_[:, 0:1], stt_[:, 3:4], ALU.subtract, ALU.mult,
                  [Bxin[s3], k.Bst[tt]], [Bxnb[s2]])
            pb = k.ps[s2].bitcast(BF16)
            for kc in range(8):
                pr.tr(pb[:, kc * 128:(kc + 1) * 128], xnb[s2][:, kc * 128:(kc + 1) * 128], k.identb[:],
                      [Bxnb[s2], k.Bc], [k.Bps[s2]])
            g = tt // 4 if lat else 8
            si = 0 if lat else 1
            for kc in range(8):
                pr.act(k.xmT[:, kc, 1 + tt * 128:1 + (tt + 1) * 128], pb[:, kc * 128:(kc + 1) * 128], AF.Identity,
                       [k.Bps[s2], k.Bmod], [k.Bx[g]], scale=k.AB1[:, kc, si:si + 1], bias=k.AB1[:, kc, 2 + si:3 + si],
                       eng=("act" if kc % 2 == 0 else "act"))
        d = k.dbg_t("xmT", [128, 8 * (1 + NT)], BF16)
        if d is not None:
            pr.dma(d, k.xmT[:].rearrange("p a b -> p (a b)"), k.Bx + [k.Bxpad], [])
        d = k.dbg_t("stats", [128, 34 * 4])
        if d is not None:
            pr.dma(d, k.stats[:].rearrange("p a b -> p (a b)"), k.Bst, [])
        pr.flush()


def phaseB2(k):
    pr, nc, din = k.pr, k.nc, k.din
    sm = k.small
    with ExitStack() as pes:
        sb = lambda n, s, d: pes.enter_context(nc.sbuf_tensor("sp_" + n, s, d))
        QT = [sb("dQT%d" % i, [128, S], BF16) for i in range(2)]
        KT = [sb("dKT%d" % i, [128, NT], BF16) for i in range(2)]
        Vg = [sb("dV%d" % i, [128, 34, 128], BF16) for i in range(2)]
        BQ = [[Buf("dq%d_%d" % (i, g)) for g in range(8)] for i in range(2)]
        BK = [[Buf("dk%d_%d" % (i, g)) for g in range(9)] for i in range(2)]
        BV = [[Buf("dv%d_%d" % (i, g)) for g in range(9)] for i in range(2)]
        tab = sb("dtab", [128, 4, 512], F32)
        Btab = Buf("dtab")
        qbf = [sb("dqbf%d" % i, [128, 512], BF16) for i in range(2)]
        Bqbf = [Buf("dqbf%d" % i) for i in range(2)]
        t1 = [sb("dt1%d" % i, [128, 512], F32) for i in range(2)]
        t2 = [sb("dt2%d" % i, [128, 512], F32) for i in range(2)]
        Bt1 = [Buf("dt1%d" % i) for i in range(2)]
        Bt2 = [Buf("dt2%d" % i) for i in range(2)]
        Pt = [sb("dP%d" % i, [128, 512], BF16) for i in range(4)]
        BP = [Buf("dP%d" % i) for i in range(4)]
        o32 = [sb("do%d" % i, [128, 256], F32) for i in range(2)]
        Bo = [Buf("do%d" % i) for i in range(2)]
        Osb = sb("dOsb", [128, 512], F32)
        BOsb = Buf("dOsb")
        rr = sb("drr", [128, 512], F32)
        Brr = Buf("drr")
        rs = sb("drs", [128, 256], F32)
        Brs = Buf("drs")
        onesf = sb("donesf", [128, 128], F32)
        Bones = Buf("dones")
        pr.memset(onesf[:], 1.0, [Bones])
        onesb = sb("donesb", [128, 128], BF16)
        pr.memset(onesb[:], 1.0, [Bones])
        stg = [sb("dstg%d" % i, [128, 512], BF16) for i in range(2)]
        Bstg = [Buf("dstg%d" % i) for i in range(2)]
        w_in = din["w_in"]
        NH_ = 1 if "nh1" in k.debug else 8
        NQC_ = 2 if "nqc2" in k.debug else 16
        lamneg = sm[:, 2:3]
        ctr = [0]

        def wload(h):
            return wtile_load(k, [(0, w_in[:, OFF_QD + h * 128:OFF_QD + (h + 1) * 128]),
                                  (128, w_in[:, OFF_KD + h * 128:OFF_KD + (h + 1) * 128]),
                                  (256, w_in[:, OFF_VD + h * 128:OFF_VD + (h + 1) * 128])])

        def proj_units(h, s_, W, BW):
            units = []
            pb, Bpb = k.ps[5], k.Bps[5]
            pr2, Bpr2 = k.ps[7], k.Bps[7]
            for g in range(8):
                for which in range(2):
                    i = (g * 2 + which) % 2

                    def st0(g=g, which=which):
                        if which == 0:
                            for t_ in range(4):
                                pr.dma(tab[:, t_, :], din["ropeD"][t_, :, g * 512:(g + 1) * 512], [], [Btab], q="sp")
                        for kc in range(8):
                            pr.mm(pb, W[:, kc, which * 128:(which + 1) * 128], k.xmT[:, kc, 1 + g * 512:1 + (g + 1) * 512],
                                  kc == 0, kc == 7, [BW, k.Bx[g]], [Bpb])

                    def st1(i=i):
                        pr.cp(qbf[i][:], pb, [Bpb], [Bqbf[i]], eng="act")

                    def st2(i=i):
                        pr.mm(pr2, k.permR[:], qbf[i][:], True, True, [k.Bc, Bqbf[i]], [Bpr2])

                    def st3(i=i, which=which):
                        pr.tt(t1[i][:], pb, tab[:, 2 * which, :], ALU.mult, [Bpb, Btab], [Bt1[i]])
                        pr.tt(t2[i][:], pr2, tab[:, 2 * which + 1, :], ALU.mult, [Bpr2, Btab], [Bt2[i]])

                    def st4(i=i, g=g, which=which):
                        dst = (QT[s_] if which == 0 else KT[s_])[:, g * 512:(g + 1) * 512]
                        pr.tt(dst, t1[i][:], t2[i][:], ALU.add, [Bt1[i], Bt2[i]], [(BQ[s_] if which == 0 else BK[s_])[g]], eng="pool")
                    units.append([st0, st1, st2, st3, st4])

            def c0():
                for kc in range(8):
                    pr.mm(pb[:, 0:256], W[:, kc, 128:256], k.xmT[:, kc, 1 + S:1 + NT], kc == 0, kc == 7, [BW, k.Bx[8]], [Bpb])

            def c1():
                pr.cp(KT[s_][:, S:NT], pb[:, 0:256], [Bpb], [BK[s_][8]], eng="act")
            units.append([c0, c1])
            for g in range(9):
                def v0(g=g):
                    nch = 4 if g < 8 else 2
                    for j in range(nch):
                        tc = g * 4 + j
                        for kc in range(8):
                            pr.mm(pb[:, j * 128:(j + 1) * 128], k.xmT[:, kc, 1 + tc * 128:1 + (tc + 1) * 128], W[:, kc, 256:384],
                                  kc == 0, kc == 7, [BW, k.Bx[g]], [Bpb])

                def v1(g=g):
                    nch = 4 if g < 8 else 2
                    pr.cp(Vg[s_][:, g * 4:g * 4 + nch, :], pb[:, 0:nch * 128].rearrange("p (a b) -> p a b", b=128),
                          [Bpb], [BV[s_][g]], eng="dve")
                units.append([v0, v1])
            return units

        Wn = wload(0)
        for u in proj_units(0, 0, Wn[0], Wn[1]):
            for st in u:
                st()
        for h in range(NH_):
            s_ = h % 2
            pend = []
            if h + 1 < NH_:
                Wn = wload(h + 1)
                pend = proj_units(h + 1, (h + 1) % 2, Wn[0], Wn[1])
            if "noattn" in k.debug:
                for u in pend:
                    for st in u:
                        st()
                continue
            sched = {}
            its = [(qc, kc) for qc in range(NQC_) for kc in range(34)]
            every = max(15, (len(its) - 40) // max(1, len(pend))) if pend else 0

            def emitS(i):
                qc, kc = its[i]
                sset = 2 * (i % 2)
                gk = min(kc // 4, 8)
                for c in range(2):
                    pr.mm(k.ps[sset + c][:, 0:256], KT[s_][c * 64:(c + 1) * 64, kc * 128:(kc + 1) * 128],
                          QT[s_][c * 64:(c + 1) * 64, qc * 256:(qc + 1) * 256], True, True,
                          [BK[s_][gk], BQ[s_][qc // 2]], [k.Bps[sset + c]])

            def epi0(qc):
                pr.cp(Osb[:], k.ps[4], [k.Bps[4]], [BOsb], eng="act")
                pr.add("dve", lambda e_: e_.reciprocal(rr[:], k.ps[6]), [k.Bps[6]], [Brr])

            def epiA(qc):
                pr.ts(rr[:, 256:512], rr[:, 256:512], lamneg, None, ALU.mult, None, [Brr, k.Bsm], [Brr])
                pr.tt(o32[0][:], Osb[:, 0:256], rr[:, 0:256], ALU.mult, [BOsb, Brr], [Bo[0]])
                pr.tt(o32[1][:], Osb[:, 256:512], rr[:, 256:512], ALU.mult, [BOsb, Brr], [Bo[1]])
                pr.tt(o32[0][:], o32[0][:], o32[1][:], ALU.add, [Bo[0], Bo[1]], [Bo[0]], eng="pool")
                pr.tt(o32[1][:], o32[0][:], o32[0][:], ALU.mult, [Bo[0]], [Bo[1]], eng="pool")

            def epiB(qc, h=h):
                pr.mm(k.ps[0][:, 256:512], onesf[:], o32[1][:], True, True, [Bones, Bo[1]], [k.Bps[0]])
                pr.act(rs[:], k.ps[0][:, 256:512], AF.Ln, [k.Bps[0], k.Bsm], [Brs], scale=1.0 / 128.0, bias=sm[:, 0:1])
                pr.act(rs[:], rs[:], AF.Exp, [Brs], [Brs], scale=-0.5)
                si = (qc // 2) % 2
                pr.stt(stg[si][:, (qc % 2) * 256:(qc % 2 + 1) * 256], o32[0][:], sm[:, 3:4], rs[:], ALU.mult, ALU.mult,
                       [Bo[0], Brs, k.Bsm], [Bstg[si]])
                if qc % 2 == 1:
                    pr.dma(k.difpreT[h * 128:(h + 1) * 128, (qc - 1) * 256:(qc + 1) * 256], stg[si][:], [Bstg[si]], [], q="sp")

            emitS(0)
            dsum = [None]
            pepi = [None]
            for i, (qc, kc) in enumerate(its):
                sset = 2 * (i % 2)
                gk = min(kc // 4, 8)
                if i + 1 < len(its):
                    emitS(i + 1)
                if dsum[0] is not None:
                    dsum[0]()
                    dsum[0] = None
                if pepi[0] is not None:
                    epi0(pepi[0])
                    pepi[0] = None
                p_ = i % 4
                pr.act(Pt[p_][:].rearrange("p (c q) -> p c q", c=2), k.psall[:, sset:sset + 2, 0:256], AF.Exp,
                       [k.Bps[sset], k.Bps[sset + 1]], [BP[p_]])

                def sums_(p_=p_, kc=kc, gk=gk):
                    pr.mm(k.ps[4], Vg[s_][:, kc, :], Pt[p_][:], kc == 0, kc == 33, [BP[p_], BV[s_][gk]], [k.Bps[4]])
                    pr.mm(k.ps[6], onesb[:], Pt[p_][:], kc == 0, kc == 33, [BP[p_], Bones], [k.Bps[6]])
                dsum[0] = sums_
                if kc == 33:
                    pepi[0] = qc
                if qc > 0 and kc == 1:
                    epiA(qc - 1)
                if qc > 0 and kc == 6:
                    epiB(qc - 1)
                if pend and i >= 12 and (i - 12) % every == 0:
                    for j_, st in enumerate(pend.pop(0)):
                        sched.setdefault(i + 3 * j_, []).append(st)
                for st in sched.pop(i, []):
                    st()
            dsum[0]()
            epi0(pepi[0])
            epiA(NQC_ - 1)
            epiB(NQC_ - 1)
            for i_ in sorted(sched):
                for st in sched[i_]:
                    st()
            for u in pend:
                for st in u:
                    st()
        d = k.dbg_t("difpreT", [1024, S], BF16)
        if d is not None:
            pr.flush()
            for i_ in range(8):
                pr.dma(d[i_ * 128:(i_ + 1) * 128, :], k.difpreT[i_ * 128:(i_ + 1) * 128, :], [], [])
        pr.flush()


def phaseB1(k):
    pr, nc, din = k.pr, k.nc, k.din
    sm = k.small
    with ExitStack() as pes:
        sb = lambda n, s, d: pes.enter_context(nc.sbuf_tensor("sp_" + n, s, d))
        NS = 2
        QT = [sb("rQT%d" % i, [128, 2, 512], BF16) for i in range(NS)]
        QsT = [sb("rQsT%d" % i, [128, 2, 512], BF16) for i in range(NS)]
        KT = [sb("rKT%d" % i, [128, 2, 512], BF16) for i in range(NS)]
        Ktok = [sb("rKtok%d" % i, [128, 4, 256], BF16) for i in range(NS)]
        V = [sb("rV%d" % i, [128, 4, 512], BF16) for i in range(NS)]
        G = [sb("rG%d" % i, [128, 4, 512], BF16) for i in range(NS)]
        tabR = [sb("rtab0", [128, 4, 512], F32)] * NS
        BS = [{n: Buf("r%s%d" % (n, i)) for n in "QT QsT KT Ktok V G".split()} for i in range(NS)]
        _bt = Buf("rtab")
        for _b in BS:
            _b["tab"] = _bt
        ra = [sb("rr%d" % i, [128, 512], F32) for i in range(4)]
        dqrep = sb("rdq", [128, 2, 512], F32)
        S32 = sb("rS32", [128, 2, 512], F32)
        Sbf = sb("rSbf", [128, 2, 512], BF16)
        ybt = [sb("rybt%d" % i, [128, 512], F32) for i in range(2)]
        ytot = sb("rytot", [128, 512], F32)
        og = sb("rog", [128, 512], BF16)
        innb = sb("rinnb", [128, 128], BF16)
        stg = [sb("rstg0", [128, 4, 512], BF16)] * 2
        KcT = sb("rKcT", [128, 2, 256], BF16)
        Kcw = sb("rKcw", [128, 2, 2, 256], BF16)
        Vc = sb("rVc", [128, 2, 512], BF16)
        gst = sb("rgst", [128, 16], F32)
        B = {n: Buf("r" + n) for n in "dq S32 Sbf ytot og innb KcT Kcw Vc gst".split()}
        BSb = [Buf("rSbf0"), Buf("rSbf1")]
        BS32 = [Buf("rS32_0"), Buf("rS32_1")]
        Bra = [Buf("rra%d" % i) for i in range(4)]
        Bybt = [Buf("rybt%d" % i) for i in range(2)]
        Bstg = [Buf("rstg0")] * 2
        Byb = [Buf("rybscr%d" % i) for i in range(32)]
        w_in = din["w_in"]
        ps, Bps = k.ps, k.Bps
        NH_ = 1 if "nh1" in k.debug else 4
        ybctr = [0]

        def proj_mm(W, BW, g, coff, b0):
            for dc in range(2):
                for kc in range(8):
                    pr.mm(ps[b0 + dc], W[:, kc, coff + dc * 128:coff + (dc + 1) * 128], k.xmT[:, kc, 1 + g * 512:1 + (g + 1) * 512],
                          kc == 0, kc == 7, [BW, k.Bx[g]], [Bps[b0 + dc]])

        def rope_ops(s_, b0, tq, dst, Bdst, scale_rep=None, dst2=None, Bdst2=None):
            tb = tabR[s_]
            Bt = BS[s_]["tab"]
            cs, sn = tb[:, tq, :], tb[:, tq + 1, :]
            pr.tt(ra[0][:], ps[b0], cs, ALU.mult, [Bps[b0], Bt], [Bra[0]])
            pr.tt(ra[3][:], ps[b0], sn, ALU.mult, [Bps[b0], Bt], [Bra[3]])
            pr.tt(ra[1][:], ps[b0 + 1], sn, ALU.mult, [Bps[b0 + 1], Bt], [Bra[1]])
            pr.tt(ra[2][:], ps[b0 + 1], cs, ALU.mult, [Bps[b0 + 1], Bt], [Bra[2]])
            if dst2 is None:
                pr.tt(dst[:, 0, :], ra[0][:], ra[1][:], ALU.subtract, [Bra[0], Bra[1]], [Bdst], eng="pool")
                pr.tt(dst[:, 1, :], ra[2][:], ra[3][:], ALU.add, [Bra[2], Bra[3]], [Bdst], eng="pool")
            else:
                pr.tt(ra[0][:], ra[0][:], ra[1][:], ALU.subtract, [Bra[0], Bra[1]], [Bra[0]], eng="pool")
                pr.tt(ra[2][:], ra[2][:], ra[3][:], ALU.add, [Bra[2], Bra[3]], [Bra[2]], eng="pool")
                if dst is not None:
                    pr.cp(dst[:, 0, :], ra[0][:], [Bra[0]], [Bdst], eng="act")
                    pr.cp(dst[:, 1, :], ra[2][:], [Bra[2]], [Bdst], eng="act")
                pr.tt(dst2[:, 0, :], ra[0][:], scale_rep, ALU.mult, [Bra[0], B["dq"]], [Bdst2])
                pr.tt(dst2[:, 1, :], ra[2][:], scale_rep, ALU.mult, [Bra[2], B["dq"]], [Bdst2])

        def ktok_make(s_, dkcol):
            ptb = ps[4].bitcast(BF16)
            for ci in range(4):
                for dc in range(2):
                    pr.tr(ptb[:, (ci * 2 + dc) * 128:(ci * 2 + dc + 1) * 128], KT[s_][:, dc, ci * 128:(ci + 1) * 128], k.identb[:],
                          [BS[s_]["KT"], k.Bc], [Bps[4]])
            pr.act(Ktok[s_][:].rearrange("p a b -> p (a b)"), ptb, AF.Copy, [Bps[4], k.Bdec], [BS[s_]["Ktok"]], scale=dkcol)

        def v_make(W, BW, g, dst, Bdst, cis, silu=False):
            for ci in cis:
                pb = ci % 2
                tc = g * 4 + ci
                for kc in range(8):
                    pr.mm(ps[pb], k.xmT[:, kc, 1 + tc * 128:1 + (tc + 1) * 128], W[:, kc, :], kc == 0, kc == 7,
                          [BW, k.Bx[g]], [Bps[pb]])
                if silu:
                    pr.act(dst[:, ci, :], ps[pb], AF.Silu, [Bps[pb]], [Bdst])
                else:
                    pr.cp(dst[:, ci, :], ps[pb], [Bps[pb]], [Bdst], eng="act")

        def state_update(s_, ci, dscol, alt=7, mid=None):
            banks = (7, alt)

            def mm_(dc):
                pr.mm(ps[banks[dc]], Ktok[s_][:, ci, dc * 128:(dc + 1) * 128], V[s_][:, ci, :], True, True,
                      [BS[s_]["Ktok"], BS[s_]["V"]], [Bps[banks[dc]]])

            def upd_(dc):
                pr.stt(S32[:, dc, :], S32[:, dc, :], dscol, ps[banks[dc]], ALU.mult, ALU.add,
                       [BS32[dc], Bps[banks[dc]], k.Bdec], [BS32[dc]])
                pr.cp(Sbf[:, dc, :], S32[:, dc, :], [BS32[dc]], [BSb[dc]], eng="act")
            if alt != 7:
                mm_(0)
                mm_(1)
                upd_(0)
                upd_(1)
            else:
                mm_(0)
                if mid is not None:
                    mid()
                upd_(0)
                mm_(1)
                upd_(1)

        def init_state(dirn):
            for dc in range(2):
                for j in range(2):
                    pr.mm(ps[7], Kcw[:, dirn, j, dc * 128:(dc + 1) * 128], Vc[:, j, :], j == 0, j == 1, [B["Kcw"], B["Vc"]], [Bps[7]])
                pr.cp(S32[:, dc, :], ps[7], [Bps[7]], [BS32[dc]], eng="dve")
                pr.cp(Sbf[:, dc, :], S32[:, dc, :], [BS32[dc]], [BSb[dc]], eng="act")

        for h in range(NH_):
            WQK, BWQK = wtile_load(k, [(0, w_in[:, OFF_QR + h * 256:OFF_QR + (h + 1) * 256]),
                                        (256, w_in[:, OFF_KR + h * 256:OFF_KR + (h + 1) * 256])])
            WV, BWV = wtile_load(k, [(0, w_in[:, OFF_VR + h * 512:OFF_VR + (h + 1) * 512])])
            WG, BWG = wtile_load(k, [(0, w_in[:, OFF_GR + h * 512:OFF_GR + (h + 1) * 512])])
            dc_ = k.deccol
            for d_ in range(2):
                for r_ in range(4):
                    pr.cp(dqrep[:, d_, r_ * 128:(r_ + 1) * 128], k.dec[:, h, 1 + d_, :], [k.Bdec], [B["dq"]], eng="dve")
            for dc in range(2):
                for kc in range(8):
                    pr.mm(ps[dc][:, 0:256], WQK[:, kc, 256 + dc * 128:256 + (dc + 1) * 128], k.xmT[:, kc, 1 + S:1 + NT],
                          kc == 0, kc == 7, [BWQK, k.Bx[8]], [Bps[dc]])
                pr.act(KcT[:, dc, :], ps[dc][:, 0:256], AF.Copy, [Bps[dc]], [B["KcT"]], scale=1.0 / 16.0)
            ptb = ps[4].bitcast(BF16)
            for j in range(2):
                for dc in range(2):
                    pr.tr(ptb[:, (j * 2 + dc) * 128:(j * 2 + dc + 1) * 128], KcT[:, dc, j * 128:(j + 1) * 128], k.identb[:],
                          [B["KcT"], k.Bc], [Bps[4]])
            for dirn in range(2):
                for j in range(2):
                    pr.ts(Kcw[:, dirn, j, :], ptb[:, j * 256:(j + 1) * 256], dc_[:, h, 4 + 2 * dirn + j:5 + 2 * dirn + j], None,
                          ALU.mult, None, [Bps[4], k.Bdec], [B["Kcw"]])
            for j in range(2):
                for kc in range(8):
                    pr.mm(ps[2], k.xmT[:, kc, 1 + S + j * 128:1 + S + (j + 1) * 128], WV[:, kc, :], kc == 0, kc == 7,
                          [BWV, k.Bx[8]], [Bps[2]])
                pr.cp(Vc[:, j, :], ps[2], [Bps[2]], [B["Vc"]], eng="act")

            def prep_part(sweep, g, part):
                s_ = g % 2
                bs = BS[s_]
                if part == 0:
                    for t_ in range(4):
                        pr.dma(tabR[s_][:, t_, :], din["ropeR"][t_, :, g * 512:(g + 1) * 512], [], [bs["tab"]], q="sp")
                    proj_mm(WQK, BWQK, g, 0, 0)
                    proj_mm(WQK, BWQK, g, 256, 2)
                    if sweep == 1:
                        rope_ops(s_, 0, 0, None, None, scale_rep=dqrep[:, 1, :], dst2=QsT[s_], Bdst2=bs["QsT"])
                    else:
                        rope_ops(s_, 0, 0, QT[s_], bs["QT"], scale_rep=dqrep[:, 0, :], dst2=QsT[s_], Bdst2=bs["QsT"])
                elif part == 1:
                    rope_ops(s_, 2, 2, KT[s_], bs["KT"])
                    v_make(WV, BWV, g, V[s_], bs["V"], (0, 1))
                elif part == 2:
                    v_make(WV, BWV, g, V[s_], bs["V"], (2, 3))
                else:
                    if sweep == 2:
                        v_make(WG, BWG, g, G[s_], bs["G"], (0, 1, 2, 3), silu=True)
                    ktok_make(s_, dc_[:, h, 1:2] if sweep == 1 else dc_[:, h, 0:1])

            init_state(1)
            order = list(range(7, -1, -1))
            for part in range(4):
                prep_part(1, order[0], part)
            for gi, g in enumerate(order):
                s_ = g % 2
                for step, ci in enumerate(range(3, -1, -1)):
                    for dc in range(2):
                        pr.mm(ps[5], QsT[s_][:, dc, ci * 128:(ci + 1) * 128], Sbf[:, dc, :], dc == 0, dc == 1,
                              [BS[s_]["QsT"], BSb[dc]], [Bps[5]])
                    yi = ybctr[0] % 2
                    ybctr[0] += 1
                    pr.cp(ybt[yi][:], ps[5], [Bps[5]], [Bybt[yi]], eng="act")
                    pr.dma(k.ybscr[h, g * 4 + ci], ybt[yi][:], [Bybt[yi]], [Byb[g * 4 + ci]], q="sp")
                    state_update(s_, ci, dc_[:, h, 3:4], alt=6)
                    if gi + 1 < 8:
                        prep_part(1, order[gi + 1], step)
            def yb_load(n2, base, h=h):
                c_ = n2
                yj = (base + n2) % 2
                pr.dma(ybt[yj][:], k.ybscr[h, c_], [Byb[c_]], [Bybt[yj]], q="sp")
            init_state(0)
            order = list(range(8))
            for part in range(4):
                prep_part(2, order[0], part)
            for gi, g in enumerate(order):
                s_ = g % 2
                sg = g % 2
                for step, ci in enumerate(range(4)):
                    n_ = gi * 4 + step
                    if n_ == 0:
                        yb_base = ybctr[0]
                        yb_load(0, yb_base)
                    if n_ + 1 < 32:
                        yb_load(n_ + 1, yb_base)
                    yi = (yb_base + n_) % 2
                    ybctr[0] += 1
                    for dc in range(2):
                        pr.mm(ps[5][:, 0:128], KT[s_][:, dc, ci * 128:(ci + 1) * 128], QT[s_][:, dc, ci * 128:(ci + 1) * 128],
                              dc == 0, dc == 1, [BS[s_]["KT"], BS[s_]["QT"]], [Bps[5]])
                    pr.tt(innb[:], ps[5][:, 0:128], k.dec[:, h, 0, :], ALU.mult, [Bps[5], k.Bdec], [B["innb"]])
                    for dc in range(2):
                        pr.mm(ps[6], QsT[s_][:, dc, ci * 128:(ci + 1) * 128], Sbf[:, dc, :], dc == 0, False,
                              [BS[s_]["QsT"], BSb[dc]], [Bps[6]])
                    def av_(s_=s_, ci=ci):
                        pr.mm(ps[6], innb[:], V[s_][:, ci, :], False, True, [B["innb"], BS[s_]["V"]], [Bps[6]])
                    state_update(s_, ci, dc_[:, h, 2:3], mid=av_)
                    pr.tt(ytot[:], ps[6], ybt[yi][:], ALU.add, [Bps[6], Bybt[yi]], [B["ytot"]])
                    pr.add("dve", lambda e: e.bn_stats(gst[:, 0:6], ytot[:]), [B["ytot"]], [B["gst"]])
                    pr.add("dve", lambda e: e.bn_aggr(gst[:, 6:8], gst[:, 0:6]), [B["gst"]], [B["gst"]])
                    rstd_ops(k, gst[:, 9:10], gst[:, 7:8], gst[:, 8:9], [B["gst"]], [B["gst"]])
                    pr.ts(ytot[:], ytot[:], gst[:, 6:7], gst[:, 9:10], ALU.subtract, ALU.mult, [B["ytot"], B["gst"]], [B["ytot"]])
                    pr.tt(og[:], ytot[:], G[s_][:, ci, :], ALU.mult, [B["ytot"], BS[s_]["G"]], [B["og"]], eng="pool")
                    if gi + 1 < 8:
                        prep_part(2, order[gi + 1], step)
                    ptb = ps[4].bitcast(BF16)
                    for vc in range(4):
                        pr.tr(ptb[:, vc * 128:(vc + 1) * 128], og[:, vc * 128:(vc + 1) * 128], k.identb[:], [B["og"], k.Bc], [Bps[4]])
                    pr.cp(stg[sg][:, :, ci * 128:(ci + 1) * 128], ptb[:, 0:512].rearrange("p (a b) -> p a b", b=128),
                          [Bps[4]], [Bstg[sg]], eng="act")
                pr.dma(k.retpreT[h * 512:(h + 1) * 512, g * 512:(g + 1) * 512].rearrange("(vc p) n -> p vc n", p=128),
                       stg[sg][:], [Bstg[sg]], [], q="sp")
        d = k.dbg_t("retpreT", [2048, S], BF16)
        if d is not None:
            pr.flush()
            for i_ in range(16):
                pr.dma(d[i_ * 128:(i_ + 1) * 128, :], k.retpreT[i_ * 128:(i_ + 1) * 128, :], [], [])
        pr.flush()


def run_steps(k, steps):
    allb = list(k.Bwbf.values())

    def load(specs):
        return [wtile_load(k, sp, q="hw", rd=allb) for sp in specs]
    nxt = load(steps[0][0])
    for i, (specs, fn) in enumerate(steps):
        cur = nxt
        if i + 1 < len(steps):
            nxt = load(steps[i + 1][0])
        fn(cur)


def ln_rows(k, pr, xin, Bxin, gst, Bg, st6, out_ops):
    rows = xin.shape[0]
    for i in range(2):
        pr.add("dve", lambda e, i=i: e.bn_stats(st6[0:rows, i, :], xin[:, i * 512:(i + 1) * 512]), [Bxin], [Bg])
    pr.add("dve", lambda e: e.bn_aggr(gst[0:rows, 0:2], st6[0:rows].rearrange("p a b -> p (a b)")), [Bg], [Bg])
    pr.act(gst[0:rows, 2:3], gst[0:rows, 1:2], AF.Ln, [Bg, k.Bsm], [Bg], bias=k.small[0:rows, 0:1])
    pr.act(gst[0:rows, 3:4], gst[0:rows, 2:3], AF.Exp, [Bg], [Bg], scale=-0.5)


def phaseC1(k):
    pr, nc, din = k.pr, k.nc, k.din
    ps, Bps = k.ps, k.Bps
    with ExitStack() as pes:
        sb = lambda n, s, d: pes.enter_context(nc.sbuf_tensor("sp_" + n, s, d))
        rows = sb("c1rows", [128, 4, 1024], F32)
        Brows = Buf("c1rows")
        rp = sb("c1rp", [128, 16, 512], BF16)
        dp = sb("c1dp", [128, 8, 512], BF16)
        Brp, Bdp = Buf("c1rp"), Buf("c1dp")
        m1 = sb("c1m1", [128, 4, 512], F32)
        Bm1 = Buf("c1m1")
        sg = [sb("c1sg%d" % i, [128, 512], F32) for i in range(2)]
        Bsg = [Buf("c1sg%d" % i) for i in range(2)]
        mg = sb("c1mg", [128, 8, 512], BF16)
        Bmg = Buf("c1mg")
        xt = [sb("c1xt0", [128, 1024], F32)] * 2
        Bxt = [Buf("c1xt0")] * 2
        tmp = sb("c1tmp", [128, 512], F32)
        Btmp = Buf("c1tmp")
        x1 = [sb("c1x10", [128, 1024], F32)] * 2
        Bx1 = [Buf("c1x10")] * 2
        x1nb = sb("c1x1nb", [128, 1024], BF16)
        Bx1nb = Buf("c1x1nb")
        gst = sb("c1gst", [128, 8], F32)
        st6 = sb("c1st6", [128, 2, 6], F32)
        Bg = Buf("c1g")
        g1row = sb("c1g1row", [128, 1024], F32)
        Bg1 = Buf("c1g1row")
        pr.dma(g1row[:], k.growscr[:, 0:1024], [k.Bgs], [Bg1])
        pr.dma(rows[:, 0, :], din["lnin_row"][0:1, :].partition_broadcast(128), [], [Brows])
        pr.dma(rows[:, 1, :], din["lnin_row"][1:2, :].partition_broadcast(128), [], [Brows])
        pr.dma(rows[:, 2, :], din["ln1_row"][0:1, :].partition_broadcast(128), [], [Brows])
        pr.dma(rows[:, 3, :], din["ln1_row"][1:2, :].partition_broadcast(128), [], [Brows])
        pr.ts(rows[:, 0:2, :], rows[:, 0:2, :], ALPHA, None, ALU.mult, None, [Brows], [Brows])
        w_in = din["w_in"]
        sctr = [0]
        steps = []
        for g in range(8):
            def loadpre(tiles, g):
                pr.dma(rp[:], k.retpreT[:, g * 512:(g + 1) * 512].rearrange("(kc p) n -> p kc n", p=128), [], [Brp], q="sp")
                pr.dma(dp[:], k.difpreT[:, g * 512:(g + 1) * 512].rearrange("(kc p) n -> p kc n", p=128), [], [Bdp], q="act")

            def gate_sig(Wg, BWg, oc4, bcol, g):
                i = oc4 % 2
                pb = 2 * i + 1
                for kc in range(8):
                    pr.mm(ps[pb], Wg[:, kc, oc4 * 128:(oc4 + 1) * 128], k.xmT[:, kc, 1 + g * 512:1 + (g + 1) * 512],
                          kc == 0, kc == 7, [BWg, k.Bx[g]], [Bps[pb]])
                pr.act(sg[i][:], ps[pb], AF.Sigmoid, [Bps[pb], k.Bc], [Bsg[i]], bias=bcol)
                return i

            for ch in range(2):
                def stepA(tiles, g=g, ch=ch):
                    (W0, B0), (W1, B1), (Wg, BWg) = tiles
                    if ch == 0:
                        loadpre(tiles, g)
                    for oc4 in range(4):
                        oc = ch * 4 + oc4
                        for kc in range(16):
                            W, BW_ = (W0, B0) if kc < 8 else (W1, B1)
                            pr.mm(ps[2 * (oc4 % 2)], W[:, kc % 8, oc4 * 128:(oc4 + 1) * 128], rp[:, kc, :], kc == 0, kc == 15,
                                  [BW_, Brp], [Bps[2 * (oc4 % 2)]])
                        i = gate_sig(Wg, BWg, oc4, k.bgate[:, oc:oc + 1], g)
                        pr.tt(m1[:, oc4, :], ps[2 * i], sg[i][:], ALU.mult, [Bps[2 * i], Bsg[i]], [Bm1])
                steps.append(([[(0, k.wbf["w_ret_out"][0:1024, ch * 512:(ch + 1) * 512])],
                               [(0, k.wbf["w_ret_out"][1024:2048, ch * 512:(ch + 1) * 512])],
                               [(0, k.wbf["w_gates"][:, ch * 512:(ch + 1) * 512])]], stepA))

                def stepB(tiles, g=g, ch=ch):
                    (Wd, BWd), (Wg, BWg) = tiles
                    for oc4 in range(4):
                        oc = ch * 4 + oc4
                        for kc in range(8):
                            pr.mm(ps[2 * (oc4 % 2)], Wd[:, kc, oc4 * 128:(oc4 + 1) * 128], dp[:, kc, :], kc == 0, kc == 7,
                                  [BWd, Bdp], [Bps[2 * (oc4 % 2)]])
                        i = gate_sig(Wg, BWg, oc4, k.bgate[:, 8 + oc:9 + oc], g)
                        pr.tt(sg[i][:], ps[2 * i], sg[i][:], ALU.mult, [Bps[2 * i], Bsg[i]], [Bsg[i]])
                        pr.tt(mg[:, oc, :], sg[i][:], m1[:, oc4, :], ALU.add, [Bsg[i], Bm1], [Bmg], eng="pool")
                steps.append(([[(0, k.wbf["w_diff_out"][:, ch * 512:(ch + 1) * 512])],
                               [(0, k.wbf["w_gates"][:, 1024 + ch * 512:1024 + (ch + 1) * 512])]], stepB))

            def stepO(tiles, g=g):
                for ts_ in range(4):
                    tt_ = g * 4 + ts_
                    i2 = tt_ % 2
                    pr.dma(xt[i2][:], din["x"][tt_ * 128:(tt_ + 1) * 128, :], [], [Bxt[i2]], q="sp")
                    for chh in range(2):
                        Wo, BWo = tiles[chh]
                        for kc in range(8):
                            pr.mm(ps[4 + chh], mg[:, kc, ts_ * 128:(ts_ + 1) * 128], Wo[:, kc, :], kc == 0, kc == 7,
                                  [Bmg, BWo], [Bps[4 + chh]])
                    st = k.stats[:, tt_, :]
                    X = xt[i2]
                    pr.ts(X[:], X[:], st[:, 0:1], st[:, 3:4], ALU.subtract, ALU.mult, [Bxt[i2], k.Bst[tt_]], [Bxt[i2]])
                    pr.tt(X[:], X[:], rows[:, 0, :], ALU.mult, [Bxt[i2], Brows], [Bxt[i2]])
                    pr.tt(X[:], X[:], rows[:, 1, :], ALU.add, [Bxt[i2], Brows], [Bxt[i2]], eng="pool")
                    for chh in range(2):
                        hs = slice(chh * 512, (chh + 1) * 512)
                        pr.tt(tmp[:], ps[4 + chh], g1row[:, hs], ALU.mult, [Bps[4 + chh], Bg1], [Btmp])
                        pr.tt(X[:, hs], X[:, hs], tmp[:], ALU.add, [Bxt[i2], Btmp], [Bxt[i2]])
                    ln_rows(k, pr, X[:], Bxt[i2], gst, Bg, st6, None)
                    pr.ts(x1nb[:], X[:], gst[:, 0:1], gst[:, 3:4], ALU.subtract, ALU.mult, [Bxt[i2], Bg], [Bx1nb])
                    pr.ts(x1[i2][:], X[:], gst[:, 0:1], gst[:, 3:4], ALU.subtract, ALU.mult, [Bxt[i2], Bg], [Bx1[i2]])
                    pr.tt(x1[i2][:], x1[i2][:], rows[:, 2, :], ALU.mult, [Bx1[i2], Brows], [Bx1[i2]], eng="pool")
                    pr.tt(x1[i2][:], x1[i2][:], rows[:, 3, :], ALU.add, [Bx1[i2], Brows], [Bx1[i2]], eng="pool")
                    pr.dma(k.x1scr[tt_ * 128:(tt_ + 1) * 128, :], x1[i2][:], [Bx1[i2]], [], q="act")
                    ptb = ps[6 + ts_ % 2].bitcast(BF16)
                    Bptb = Bps[6 + ts_ % 2]
                    for kc in range(8):
                        pr.tr(ptb[:, kc * 128:(kc + 1) * 128], x1nb[:, kc * 128:(kc + 1) * 128], k.identb[:], [Bx1nb, k.Bc], [Bptb])
                    for kc in range(8):
                        pr.act(k.xmT[:, kc, 1 + tt_ * 128:1 + (tt_ + 1) * 128], ptb[:, kc * 128:(kc + 1) * 128], AF.Identity,
                               [Bptb, k.Bmod], [k.Bx[g]], scale=k.AB2[:, kc, 0:1], bias=k.AB2[:, kc, 1:2])
            steps.append(([[(0, k.wbf["w_o"][:, 0:512])], [(0, k.wbf["w_o"][:, 512:1024])]], stepO))
        run_steps(k, steps)
        d = k.dbg_t("x1", [S, D])
        if d is not None:
            pr.flush()
            for i_ in range(8):
                pr.dma(d[i_ * 512:(i_ + 1) * 512, :], k.x1scr[i_ * 512:(i_ + 1) * 512, :], [], [])
        pr.flush()


def phaseC2(k):
    pr, nc, din = k.pr, k.nc, k.din
    ps, Bps = k.ps, k.Bps
    with ExitStack() as pes:
        sb = lambda n, s, d: pes.enter_context(nc.sbuf_tensor("sp_" + n, s, d))
        rows = sb("c2rows", [128, 2, 1024], F32)
        Brows = Buf("c2rows")
        xo = sb("c2xo", [128, 4, 1024], F32)
        Bxo = [Buf("c2xo%d" % i) for i in range(4)]
        actT = sb("c2act", [128, 22, 512], BF16)
        Bact = Buf("c2act")
        ut = [sb("c2ut%d" % i, [128, 512], F32) for i in range(4)]
        But = [Buf("c2ut%d" % i) for i in range(4)]
        tmp = sb("c2tmp", [128, 512], F32)
        Btmp = Buf("c2tmp")
        gst = sb("c2gst", [128, 8], F32)
        st6 = sb("c2st6", [128, 2, 6], F32)
        Bg = Buf("c2g")
        g2row = sb("c2g2row", [128, 1024], F32)
        Bg2 = Buf("c2g2row")
        pr.dma(g2row[:], k.growscr[:, 1024:2048], [k.Bgs], [Bg2])
        pr.dma(rows[:, 0, :], din["ln2_row"][0:1, :].partition_broadcast(128), [], [Brows])
        pr.dma(rows[:, 1, :], din["ln2_row"][1:2, :].partition_broadcast(128), [], [Brows])
        pr.memset(k.xmT[:, :, 1 + S:2 + S], 0.0, [k.Bx[8]])
        hT = k.xmT
        Bh = k.Bx + [k.Bxpad]
        groups = [(i * 510, 510) for i in range(8)] + [(4080, 16)]
        steps = []
        fctr = [0]
        for (t0, n) in groups:
            nsub = (n + 127) // 128
            for fg in range(11):
                def stepU(tiles, t0=t0, n=n, fg=fg, nsub=nsub):
                    Wu, BWu = tiles[0]
                    if fg == 0:
                        for ts_ in range(nsub):
                            r_ = min(128, n - ts_ * 128)
                            pr.dma(xo[0:r_, ts_, :], k.x1scr[t0 + ts_ * 128:t0 + ts_ * 128 + r_, :], [], [Bxo[ts_]], q="sp")
                            pr.ts(xo[0:r_, ts_, :], xo[0:r_, ts_, :], ALPHA, None, ALU.mult, None, [Bxo[ts_]], [Bxo[ts_]], eng="pool")
                    for f2 in range(2):
                        fc = fg * 2 + f2
                        i = fctr[0] % 4
                        fctr[0] += 1
                        pu, pg = ps[2 * i], ps[2 * i + 1]
                        Bpu, Bpg = Bps[2 * i], Bps[2 * i + 1]
                        for kc in range(8):
                            pr.mm(pu[:, 0:n + 2], Wu[:, kc, f2 * 128:(f2 + 1) * 128], hT[:, kc, t0:t0 + n + 2], kc == 0, kc == 7,
                                  [BWu] + Bh, [Bpu])
                        for kc in range(8):
                            pr.mm(pg[:, 0:n], Wu[:, kc, 256 + f2 * 128:256 + (f2 + 1) * 128], hT[:, kc, t0 + 1:t0 + 1 + n], kc == 0, kc == 7,
                                  [BWu] + Bh, [Bpg])
                        U = ut[i]
                        pr.act(U[:, 0:n], pu[:, 1:n + 1], AF.Identity, [Bpu, k.Bc], [But[i]], scale=k.convw[:, fc, 1:2], bias=k.convb[:, fc:fc + 1])
                        pr.stt(U[:, 0:n], pu[:, 0:n], k.convw[:, fc, 0:1], U[:, 0:n], ALU.mult, ALU.add, [Bpu, k.Bc, But[i]], [But[i]])
                        pr.stt(U[:, 0:n], pu[:, 2:n + 2], k.convw[:, fc, 2:3], U[:, 0:n], ALU.mult, ALU.add, [Bpu, k.Bc, But[i]], [But[i]])
                        pr.act(U[:, 0:n], U[:, 0:n], AF.Gelu, [But[i]], [But[i]])
                        pr.tt(actT[:, fc, 0:n], U[:, 0:n], pg[:, 0:n], ALU.mult, [But[i], Bpg], [Bact])
                steps.append(([[(0, k.wbf["w_up"][:, fg * 256:(fg + 1) * 256]), (256, k.wbf["w_up"][:, DFF + fg * 256:DFF + (fg + 1) * 256])]], stepU))
            for chh in range(2):
                def stepD(tiles, t0=t0, n=n, chh=chh, nsub=nsub):
                    hs = slice(chh * 512, (chh + 1) * 512)
                    for ts_ in range(nsub):
                        r_ = min(128, n - ts_ * 128)
                        pb = 4 + ts_
                        for fc in range(22):
                            Wd, BWd = tiles[fc // 8]
                            pr.mm(ps[pb][0:r_, :], actT[:, fc, ts_ * 128:ts_ * 128 + r_], Wd[:, fc % 8, :], fc == 0, fc == 21,
                                  [Bact, BWd], [Bps[pb]])
                        pr.tt(tmp[0:r_, :], ps[pb][0:r_, :], g2row[0:r_, hs], ALU.mult, [Bps[pb], Bg2], [Btmp])
                        pr.tt(xo[0:r_, ts_, hs], xo[0:r_, ts_, hs], tmp[0:r_, :], ALU.add, [Bxo[ts_], Btmp], [Bxo[ts_]])
                        if chh == 1:
                            X = xo[0:r_, ts_, :]
                            ln_rows(k, pr, X, Bxo[ts_], gst, Bg, st6, None)
                            pr.ts(X, X, gst[0:r_, 0:1], gst[0:r_, 3:4], ALU.subtract, ALU.mult, [Bxo[ts_], Bg], [Bxo[ts_]])
                            pr.tt(X, X, rows[0:r_, 0, :], ALU.mult, [Bxo[ts_], Brows], [Bxo[ts_]], eng="pool")
                            pr.tt(X, X, rows[0:r_, 1, :], ALU.add, [Bxo[ts_], Brows], [Bxo[ts_]], eng="pool")
                            pr.dma(k.out[t0 + ts_ * 128:t0 + ts_ * 128 + r_, :], X, [Bxo[ts_]], [], q="act")
                steps.append(([[(0, k.wbf["w_down"][0:1024, chh * 512:(chh + 1) * 512])],
                               [(0, k.wbf["w_down"][1024:2048, chh * 512:(chh + 1) * 512])],
                               [(0, k.wbf["w_down"][2048:2816, chh * 512:(chh + 1) * 512])]], stepD))
        run_steps(k, steps)
        pr.flush()


_CACHE = {}


def run(inputs, debug=(), cores=8):
    key = tuple(sorted(debug))
    if key not in _CACHE:
        _CACHE[key] = build_program(debug)
    nc, cst = _CACHE[key]
    in_maps = [host_inputs(inputs, b, cst) for b in range(cores)]
    res = run_bass_kernel_spmd(nc, in_maps, core_ids=list(range(cores)))
    return res.results


def kernel(**inputs):
    results = run(inputs)
    return np.stack([np.asarray(r["out"], dtype=np.float32) for r in results], axis=0)
```
